# Optimizing a Trainium2 kernel written in Bass

```python
import math
import jax, jax.numpy as jnp
from jax import lax
import numpy as np


D_MODEL = 1024
BATCH = 8
SEQ = 4096
DEPTH = 1
DEC_BATCH = 32
DEC_SEQ = 4
PAST_LEN = 16384
PAGE_SIZE = 128

A_WINDOWS = (128, 512, 2048)
A_DILATIONS = (1, 4, 16)
A_GROUPS = 3
A_HEADS = 8
A_HEAD_DIM = 64
A_WIDTH = A_GROUPS * A_HEADS * A_HEAD_DIM
A_OUT = A_HEADS * A_HEAD_DIM
A_BLOCK = 128

GLA_HEADS = 4
GLA_DK = 128
GLA_DV = 256
GLA_QK = GLA_HEADS * GLA_DK
GLA_V = GLA_HEADS * GLA_DV
GLA_RANK = 16
GLA_NORMALIZER = 16.0
GLA_CHUNK = 64

PEER_HEADS = 8
PEER_NKEYS = 128
PEER_EXPERTS = PEER_NKEYS * PEER_NKEYS
PEER_DQ = 256
PEER_TOPK = 16
PEER_BLOCK = 128

REL_BUCKETS = 32
REL_MAX_DIST = 2048
REL_HEADS = A_GROUPS * A_HEADS

IN_SPLITS = (A_WIDTH, A_WIDTH, A_WIDTH, GLA_QK, GLA_QK, GLA_V, GLA_RANK, GLA_V, D_MODEL, D_MODEL)
IN_WIDTH = 3 * A_WIDTH + 2 * GLA_QK + 2 * GLA_V + GLA_RANK + 2 * D_MODEL

DEEPNORM_ALPHA = (2.0 * DEPTH) ** 0.25
DEEPNORM_BETA = (8.0 * DEPTH) ** -0.25
LN_EPS = 1e-5
NEG_INF = -1e30

kernel_name = 'hybrid_dilated_gla_peer_step'


def layer_norm(x, g, b):
    xf = x.astype(jnp.float32)
    mu = xf.mean(-1, keepdims=True)
    var = jnp.square(xf - mu).mean(-1, keepdims=True)
    return ((xf - mu) * lax.rsqrt(var + LN_EPS) * g.astype(jnp.float32) + b.astype(jnp.float32)).astype(x.dtype)


def rel_bucket(dist):
    exact = REL_BUCKETS // 2
    d = jnp.maximum(dist, 1).astype(jnp.float32)
    large = exact + (jnp.log(d / exact) / math.log(REL_MAX_DIST / exact) * (REL_BUCKETS - exact)).astype(jnp.int32)
    large = jnp.minimum(large, REL_BUCKETS - 1)
    return jnp.where(dist < exact, dist, large)


def dilated_window_prompt(q, k, v, bias_table, window, dil):
    B, T, H, dh = q.shape
    band = window // dil
    c = A_BLOCK
    span = c * dil
    T_pad = -(-T // span) * span
    L = T_pad // dil
    nb = L // c

    def to_blocks(a):
        a = jnp.pad(a, ((0, 0), (0, T_pad - T), (0, 0), (0, 0))).reshape(B, L, dil, H, dh)
        return a.transpose(0, 2, 1, 3, 4).reshape(B, dil, nb, c, H, dh)

    def with_prev(a):
        prev = jnp.concatenate([jnp.zeros_like(a[:, :, :1]), a[:, :, :-1]], axis=2)
        return jnp.concatenate([prev, a], axis=3)

    qb = to_blocks(q)
    kk = with_prev(to_blocks(k))
    vv = with_prev(to_blocks(v))
    s = jnp.einsum('brnqhd,brnkhd->brnhqk', qb, kk).astype(jnp.float32) * (dh ** -0.5)
    qi = jnp.arange(c)[:, None]
    kj = jnp.arange(2 * c)[None, :]
    rel = c + qi - kj
    bias = bias_table[rel_bucket(jnp.clip(rel, 0, band) * dil)].transpose(2, 0, 1).astype(jnp.float32)
    in_band = (rel >= 0) & (rel <= band)
    first = (jnp.arange(nb) == 0)[:, None, None] & (kj < c)[None]
    valid = in_band[None] & ~first
    s = jnp.where(valid[:, None], s + bias, NEG_INF)
    lse = jax.nn.logsumexp(s, axis=-1)
    p = jnp.exp(s - lse[..., None])
    o = jnp.einsum('brnhqk,brnkhd->brnqhd', p.astype(v.dtype), vv)

    def from_blocks(a):
        a = a.reshape((B, dil, L) + a.shape[4:])
        a = jnp.moveaxis(a, 1, 2).reshape((B, T_pad) + a.shape[3:])
        return a[:, :T]

    return from_blocks(o), from_blocks(lse.transpose(0, 1, 2, 4, 3))


def dilated_window_sample(q, k, v, buf, bias_table, window, dil):
    B, S, H, dh = q.shape
    Wb = buf.shape[1]
    band = window // dil
    ext_k = jnp.concatenate([buf[:, :, 0], k], axis=1)
    ext_v = jnp.concatenate([buf[:, :, 1], v], axis=1)
    j = jnp.arange(band + 1)
    idx = Wb + jnp.arange(S)[:, None] - j[None, :] * dil
    valid = idx >= 0
    idx = jnp.maximum(idx, 0)
    kg = ext_k[:, idx]
    vg = ext_v[:, idx]
    s = jnp.einsum('bshd,bsjhd->bhsj', q, kg).astype(jnp.float32) * (dh ** -0.5)
    bias = bias_table[rel_bucket(j * dil)].T.astype(jnp.float32)
    s = jnp.where(valid[None, None], s + bias[None, :, None, :], NEG_INF)
    lse = jax.nn.logsumexp(s, axis=-1)
    p = jnp.exp(s - lse[..., None])
    o = jnp.einsum('bhsj,bsjhd->bshd', p.astype(v.dtype), vg)
    new_buf = jnp.stack([ext_k[:, -Wb:], ext_v[:, -Wb:]], axis=2)
    return o, lse.transpose(0, 2, 1), new_buf


def gla_recurrence(q, k, v, log_a, state):
    B, T, H, dk = q.shape
    dv = v.shape[-1]
    c = GLA_CHUNK if T % GLA_CHUNK == 0 else T
    n = T // c

    def chunks(a):
        return a.reshape(B, n, c, H, a.shape[-1]).transpose(1, 0, 3, 2, 4).astype(jnp.float32)

    qc, kc, vc, ac = chunks(q * (dk ** -0.5)), chunks(k), chunks(v), chunks(log_a)
    causal = jnp.tril(jnp.ones((c, c), dtype=bool))

    def step(S, xs):
        qi, ki, vi, ai = xs
        b = jnp.cumsum(ai, axis=-2)
        b_last = b[..., -1:, :]
        qe = qi * jnp.exp(b)
        ke = ki * jnp.exp(-b)
        att = jnp.where(causal, jnp.einsum('bhid,bhjd->bhij', qe, ke), 0.0)
        o = jnp.einsum('bhij,bhje->bhie', att, vi) + jnp.einsum('bhid,bhde->bhie', qe, S)
        S = jnp.exp(b_last)[..., 0, :, None] * S + jnp.einsum('bhjd,bhje->bhde', ki * jnp.exp(b_last - b), vi)
        return S, o

    S, o = lax.scan(step, state.astype(jnp.float32), (qc, kc, vc, ac))
    o = o.transpose(1, 0, 3, 2, 4).reshape(B, T, H, dv)
    return o, S


def peer_ffn(x, w_query, sub_keys, u_table, v_table):
    B, T, D = x.shape
    N = B * T
    n_blocks = -(-N // PEER_BLOCK)
    xf = jnp.pad(x.reshape(N, D), ((0, n_blocks * PEER_BLOCK - N), (0, 0))).reshape(n_blocks, PEER_BLOCK, D)
    half = PEER_DQ // 2
    k1 = sub_keys[:, 0].astype(jnp.float32)
    k2 = sub_keys[:, 1].astype(jnp.float32)

    def block(xb):
        q = (xb @ w_query).reshape(-1, PEER_HEADS, PEER_DQ).astype(jnp.float32)
        q = (q - q.mean(-1, keepdims=True)) * lax.rsqrt(q.var(-1, keepdims=True) + LN_EPS)
        s1 = jnp.einsum('nhd,hkd->nhk', q[..., :half], k1)
        s2 = jnp.einsum('nhd,hkd->nhk', q[..., half:], k2)
        v1, i1 = lax.top_k(s1, PEER_TOPK)
        v2, i2 = lax.top_k(s2, PEER_TOPK)
        cand = (v1[..., :, None] + v2[..., None, :]).reshape(-1, PEER_HEADS, PEER_TOPK * PEER_TOPK)
        sv, si = lax.top_k(cand, PEER_TOPK)
        e1 = jnp.take_along_axis(i1, si // PEER_TOPK, axis=-1)
        e2 = jnp.take_along_axis(i2, si % PEER_TOPK, axis=-1)
        expert = e1 * PEER_NKEYS + e2
        gate = jax.nn.softmax(sv, axis=-1)
        act = jax.nn.gelu(jnp.einsum('nd,nhkd->nhk', xb, u_table[expert]).astype(jnp.float32), approximate=False)
        return jnp.einsum('nhk,nhkd->nd', (gate * act).astype(x.dtype), v_table[expert])

    out = lax.map(block, xf)
    return out.reshape(-1, D)[:N].reshape(B, T, D)


def trunk_layer(x, kv_bufs, gla_state, rel_bias, w_in, w_gla_gate2, b_gla_gate, g_gla_norm,
                w_branch_a, w_branch_b, w_out, ln1_g, ln1_b, w_peer_query, peer_sub_keys,
                peer_u, peer_v, ln2_g, ln2_b):
    B, T, _ = x.shape
    proj = x @ w_in
    offs = np.cumsum(IN_SPLITS)[:-1].tolist()
    qa, ka, va, qb, kb, vb, ab, rb, ga, gb = jnp.split(proj, offs, axis=-1)

    qa = qa.reshape(B, T, A_GROUPS, A_HEADS, A_HEAD_DIM)
    ka = ka.reshape(B, T, A_GROUPS, A_HEADS, A_HEAD_DIM)
    va = va.reshape(B, T, A_GROUPS, A_HEADS, A_HEAD_DIM)
    outs, lses, new_bufs = [], [], []
    for g in range(A_GROUPS):
        bias_g = rel_bias[:, g * A_HEADS:(g + 1) * A_HEADS]
        if kv_bufs is None:
            o, lse = dilated_window_prompt(qa[:, :, g], ka[:, :, g], va[:, :, g], bias_g, A_WINDOWS[g], A_DILATIONS[g])
            wb = min(A_WINDOWS[g], T)
            nbuf = jnp.stack([ka[:, T - wb:, g], va[:, T - wb:, g]], axis=2)
        else:
            o, lse, nbuf = dilated_window_sample(qa[:, :, g], ka[:, :, g], va[:, :, g], kv_bufs[g], bias_g, A_WINDOWS[g], A_DILATIONS[g])
        outs.append(o)
        lses.append(lse)
        new_bufs.append(nbuf)
    w_mix = jax.nn.softmax(jnp.stack(lses, 0), axis=0)
    o_a = jnp.sum(w_mix[..., None] * jnp.stack(outs, 0).astype(jnp.float32), axis=0)
    o_a = o_a.astype(x.dtype).reshape(B, T, A_OUT)

    log_a = jax.nn.log_sigmoid((ab @ w_gla_gate2 + b_gla_gate).astype(jnp.float32)) / GLA_NORMALIZER
    state0 = jnp.zeros((B, GLA_HEADS, GLA_DK, GLA_DV), jnp.float32) if gla_state is None else gla_state
    o_b, new_state = gla_recurrence(qb.reshape(B, T, GLA_HEADS, GLA_DK), kb.reshape(B, T, GLA_HEADS, GLA_DK),
                                    vb.reshape(B, T, GLA_HEADS, GLA_DV), log_a.reshape(B, T, GLA_HEADS, GLA_DK), state0)
    o_b = o_b * lax.rsqrt(jnp.mean(jnp.square(o_b), -1, keepdims=True) + LN_EPS)
    o_b = o_b.reshape(B, T, GLA_V) * g_gla_norm.astype(jnp.float32) * jax.nn.silu(rb.astype(jnp.float32))
    o_b = o_b.astype(x.dtype)
    new_state = new_state.astype(x.dtype if gla_state is None else gla_state.dtype)

    mixed = jax.nn.sigmoid(ga) * (o_a @ w_branch_a) + jax.nn.sigmoid(gb) * (o_b @ w_branch_b)
    x = layer_norm(DEEPNORM_ALPHA * x + mixed @ w_out, ln1_g, ln1_b)
    x = layer_norm(DEEPNORM_ALPHA * x + peer_ffn(x, w_peer_query, peer_sub_keys, peer_u, peer_v), ln2_g, ln2_b)
    return x, new_bufs, new_state


def setup_inputs(seed: int = 0) -> dict:
    key = jax.random.key(seed)
    ks = jax.random.split(key, 24)
    f32 = jnp.float32
    nrm = lambda k, s: jax.random.normal(k, s, f32)
    wb = [min(w, PAST_LEN) for w in A_WINDOWS]
    return {
        'x_prompt': nrm(ks[0], (BATCH, SEQ, D_MODEL)),
        'x_sample': nrm(ks[1], (DEC_BATCH, DEC_SEQ, D_MODEL)),
        'cache_kv_a1': nrm(ks[2], (DEPTH, DEC_BATCH, wb[0], 2, A_HEADS, A_HEAD_DIM)),
        'cache_kv_a2': nrm(ks[3], (DEPTH, DEC_BATCH, wb[1], 2, A_HEADS, A_HEAD_DIM)),
        'cache_kv_a3': nrm(ks[4], (DEPTH, DEC_BATCH, wb[2], 2, A_HEADS, A_HEAD_DIM)),
        'state_gla': 0.3 * nrm(ks[5], (DEPTH, DEC_BATCH, GLA_HEADS, GLA_DK, GLA_DV)),
        'rel_bias': 0.1 * nrm(ks[6], (REL_BUCKETS, REL_HEADS)),
        'w_in': nrm(ks[7], (DEPTH, D_MODEL, IN_WIDTH)) * D_MODEL ** -0.5,
        'w_gla_gate2': nrm(ks[8], (DEPTH, GLA_RANK, GLA_QK)) * GLA_RANK ** -0.5,
        'b_gla_gate': 0.1 * nrm(ks[9], (DEPTH, GLA_QK)),
        'g_gla_norm': 1.0 + 0.02 * nrm(ks[10], (DEPTH, GLA_V)),
        'w_branch_a': nrm(ks[11], (DEPTH, A_OUT, D_MODEL)) * A_OUT ** -0.5,
        'w_branch_b': nrm(ks[12], (DEPTH, GLA_V, D_MODEL)) * GLA_V ** -0.5,
        'w_out': nrm(ks[13], (DEPTH, D_MODEL, D_MODEL)) * (D_MODEL ** -0.5 * DEEPNORM_BETA),
        'ln1_g': 1.0 + 0.02 * nrm(ks[14], (DEPTH, D_MODEL)),
        'ln1_b': 0.02 * nrm(ks[15], (DEPTH, D_MODEL)),
        'w_peer_query': nrm(ks[16], (DEPTH, D_MODEL, PEER_HEADS * PEER_DQ)) * D_MODEL ** -0.5,
        'peer_sub_keys': nrm(ks[17], (DEPTH, PEER_HEADS, 2, PEER_NKEYS, PEER_DQ // 2)) * (PEER_DQ // 2) ** -0.5,
        'peer_u': nrm(ks[18], (DEPTH, PEER_EXPERTS, D_MODEL)) * D_MODEL ** -0.5,
        'peer_v': nrm(ks[19], (DEPTH, PEER_EXPERTS, D_MODEL)) * (PEER_HEADS ** -0.5 * DEEPNORM_BETA),
        'ln2_g': 1.0 + 0.02 * nrm(ks[20], (DEPTH, D_MODEL)),
        'ln2_b': 0.02 * nrm(ks[21], (DEPTH, D_MODEL)),
    }


def reference(x_prompt, x_sample, cache_kv_a1, cache_kv_a2, cache_kv_a3, state_gla, rel_bias,
              w_in, w_gla_gate2, b_gla_gate, g_gla_norm, w_branch_a, w_branch_b, w_out,
              ln1_g, ln1_b, w_peer_query, peer_sub_keys, peer_u, peer_v, ln2_g, ln2_b):
    y_p, y_s = x_prompt, x_sample
    caches = (cache_kv_a1, cache_kv_a2, cache_kv_a3)
    new_p = [[] for _ in range(A_GROUPS)]
    new_s = [[] for _ in range(A_GROUPS)]
    st_p, st_s = [], []
    for l in range(DEPTH):
        lw = (w_in[l], w_gla_gate2[l], b_gla_gate[l], g_gla_norm[l], w_branch_a[l], w_branch_b[l], w_out[l],
              ln1_g[l], ln1_b[l], w_peer_query[l], peer_sub_keys[l], peer_u[l], peer_v[l], ln2_g[l], ln2_b[l])
        y_p, bufs_p, s_p = trunk_layer(y_p, None, None, rel_bias, *lw)
        y_s, bufs_s, s_s = trunk_layer(y_s, [c[l] for c in caches], state_gla[l], rel_bias, *lw)
        for g in range(A_GROUPS):
            new_p[g].append(bufs_p[g])
            new_s[g].append(bufs_s[g])
        st_p.append(s_p)
        st_s.append(s_s)
    return (y_p, y_s,
            jnp.stack(new_p[0]), jnp.stack(new_p[1]), jnp.stack(new_p[2]), jnp.stack(st_p),
            jnp.stack(new_s[0]), jnp.stack(new_s[1]), jnp.stack(new_s[2]), jnp.stack(st_s))
```

```python
import math
from contextlib import ExitStack
import numpy as np
import concourse.bass as bass
import concourse.mybir as mybir
from concourse.bass_utils import run_bass_kernel_spmd

F32 = mybir.dt.float32
BF16 = mybir.dt.bfloat16
I32 = mybir.dt.int32
AF = mybir.ActivationFunctionType
ALU = mybir.AluOpType
AX = mybir.AxisListType

NCORES = 8
T = 4096
NS = 16
D = 1024
NEG = -30000.0


class Buf:
    __slots__ = ("name", "t", "base_w", "part_w", "readers", "dsem", "dcount")

    def __init__(self, name, t):
        self.name = name
        self.t = t
        self.base_w = {}
        self.part_w = {}
        self.readers = {}
        self.dsem = None
        self.dcount = 0

    def __getitem__(self, idx):
        return self.t[idx]


class Eng:
    def __init__(self, name, sem):
        self.name = name
        self.sem = sem
        self.count = 0
        self.waited = {}
        self.ops = []


class Prog:
    def __init__(self, nc, stack):
        self.nc = nc
        self.stack = stack
        self.engs = {}
        for n in ("sync", "scalar", "gpsimd", "vector", "tensor"):
            s = stack.enter_context(nc.semaphore("e_" + n))
            self.engs[n] = Eng(n, s)
        self.out_toks = []
        self.ninstr = 0
        self.nsem = 5
        self.root = stack
        self.dbufs = []

    def push(self):
        if not hasattr(self, "stk"):
            self.stk = []
        self.stk.append(self.stack)
        self.stack = ExitStack()
        self.stack.__enter__()

    def pop(self):
        self.barrier()
        self.stack.__exit__(None, None, None)
        self.stack = self.stk.pop()

    def barrier(self):
        fin = {}
        for e in self.engs.values():
            if e.count:
                fin[e.sem] = e.count
        for b in self.dbufs:
            fin[b.dsem] = b.dcount
        for e in self.engs.values():
            for s, v in fin.items():
                if e.waited.get(s, 0) < v:
                    e.waited[s] = v
                    e.ops.append(("w", s, v))

    def sb(self, name, shape, dt):
        t = self.stack.enter_context(self.nc.sbuf_tensor("s_" + name, list(shape), dt))
        return Buf(name, t)

    def ps(self, name, shape, dt=F32):
        t = self.root.enter_context(self.nc.psum_tensor("p_" + name, list(shape), dt))
        return Buf(name, t)

    def wrap(self, name, t):
        return Buf(name, t)

    def _need(self, eng, reads, writes, partial):
        need = {}
        for r in reads:
            for dd in (r.base_w, r.part_w):
                for s, v in dd.items():
                    if need.get(s, 0) < v:
                        need[s] = v
        for w in writes:
            dds = (w.base_w, w.readers) if partial else (w.base_w, w.part_w, w.readers)
            for dd in dds:
                for s, v in dd.items():
                    if need.get(s, 0) < v:
                        need[s] = v
        for s, v in need.items():
            if eng.waited.get(s, 0) < v:
                eng.waited[s] = v
                eng.ops.append(("w", s, v))

    def _commit(self, tok, reads, writes, partial):
        s, v = tok
        for r in reads:
            if r.readers.get(s, 0) < v:
                r.readers[s] = v
        for w in writes:
            if partial:
                if w.part_w.get(s, 0) < v:
                    w.part_w[s] = v
            else:
                w.base_w = {s: v}
                w.part_w = {}
                w.readers = {}

    def op(self, ename, fn, reads=(), writes=(), partial=False):
        eng = self.engs[ename]
        self._need(eng, reads, writes, partial)
        eng.count += 1
        eng.ops.append(("i", fn))
        tok = (eng.sem, eng.count)
        self._commit(tok, reads, writes, partial)
        self.ninstr += 1
        return tok

    def group(self, ename, fns, reads=(), writes=(), partial=False):
        eng = self.engs[ename]
        self._need(eng, reads, writes, partial)
        for f in fns[:-1]:
            eng.ops.append(("n", f))
        eng.count += 1
        eng.ops.append(("i", fns[-1]))
        tok = (eng.sem, eng.count)
        self._commit(tok, reads, writes, partial)
        self.ninstr += len(fns)
        return tok

    def dma(self, ename, fns, owner, reads=(), writes=(), is_output=False, partial=False):
        eng = self.engs[ename]
        if owner.dsem is None:
            owner.dsem = self.root.enter_context(self.nc.semaphore("d_" + owner.name))
            self.nsem += 1
            self.dbufs.append(owner)
        self._need(eng, reads, writes, partial)
        for f in fns:
            owner.dcount += 16
            eng.ops.append(("d", f, owner.dsem))
        tok = (owner.dsem, owner.dcount)
        self._commit(tok, reads, writes, partial)
        if is_output:
            self.out_toks.append(tok)
        self.ninstr += len(fns)
        return tok

    def finish(self):
        eng = self.engs["sync"]
        fin = {}
        for e in self.engs.values():
            if e.count:
                fin[e.sem] = e.count
        for (s, v) in self.out_toks:
            if fin.get(s, 0) < v:
                fin[s] = v
        for s, v in fin.items():
            if eng.waited.get(s, 0) < v and s is not eng.sem:
                eng.ops.append(("w", s, v))
        engs = self.engs

        def replay(e, ne):
            for o in e.ops:
                k = o[0]
                if k == "w":
                    ne.wait_ge(o[1], o[2])
                elif k == "i":
                    o[1](ne).then_inc(e.sem, 1)
                elif k == "n":
                    o[1](ne)
                else:
                    o[1](ne).then_inc(o[2], 16)

        with self.nc.Block() as block:
            @block.sync
            def _(x):
                replay(engs["sync"], x)

            @block.scalar
            def _(x):
                replay(engs["scalar"], x)

            @block.gpsimd
            def _(x):
                replay(engs["gpsimd"], x)

            @block.vector
            def _(x):
                replay(engs["vector"], x)

            @block.tensor
            def _(x):
                replay(engs["tensor"], x)


A_DIL = (1, 4, 16)
COL_QA, COL_KA, COL_VA = 0, 1536, 3072
COL_QB, COL_KB, COL_VB = 4608, 5120, 5632
COL_AB, COL_RB, COL_GA, COL_GB = 6656, 6672, 7696, 8720


def _rel_bucket(dist):
    exact = 16
    d = np.maximum(dist, 1).astype(np.float32)
    large = exact + (np.log(d / np.float32(exact)) / np.float32(math.log(2048 / exact)) * np.float32(32 - exact)).astype(np.int32)
    large = np.minimum(large, 31)
    return np.where(dist < exact, dist, large)


def make_consts():
    c = {}
    c["ident"] = np.eye(128, dtype=np.float32)
    c["flip"] = np.ascontiguousarray(np.eye(128, dtype=np.float32)[::-1])
    oh = np.zeros((3, 33, 384), np.float32)
    for g, dil in enumerate(A_DIL):
        for m in range(384):
            rel = m - 127
            if 0 <= rel <= 128:
                oh[g, int(_rel_bucket(np.int32(rel * dil))), m] = 1.0
            else:
                oh[g, 32, m] = 1.0
    c["oh_bias"] = oh.transpose(1, 0, 2).copy()
    j = np.arange(128)[:, None]
    i = np.arange(128)[None, :]
    same = (j // 64) == (i // 64)
    c["tri_c"] = np.where(same & (j <= i), -1.0 / 16.0, 0.0).astype(np.float32)
    c["tri_rev"] = np.where(same & (j > i), -1.0 / 16.0, 0.0).astype(np.float32)
    i64 = np.arange(64)[None, :]
    c["mask_t"] = ((j % 64) <= i64).astype(np.float32)
    c["iota_c"] = np.tile(np.arange(256, dtype=np.int32)[None, :], (128, 1))
    c["e_sel"] = np.tile(np.eye(16, dtype=np.float32)[None], (128, 1, 1))
    c["sel16"] = np.tile(np.eye(16, dtype=np.float32)[:, :, None], (1, 1, 128))
    ohs = np.zeros((32, 3, 128), np.float32)
    for g, dil in enumerate(A_DIL):
        for p in range(128):
            ohs[int(_rel_bucket(np.int32((128 - p) * dil))), g, p] = 1.0
    c["ohs"] = ohs
    j16 = np.arange(16)[:, None]
    i16 = np.arange(16)[None, :]
    same16 = (j16 // 4) == (i16 // 4)
    c["tri_c16"] = np.where(same16 & (j16 <= i16), -1.0 / 16.0, 0.0).astype(np.float32)
    c["tri_rev16"] = np.where(same16 & (j16 > i16), -1.0 / 16.0, 0.0).astype(np.float32)
    c["mask16"] = (same16 & (j16 <= i16)).astype(np.float32)
    c["colmask"] = np.tile(((np.arange(16)[None, :] // 4) == np.arange(4)[:, None]).astype(np.float32)[None], (128, 1, 1))
    c["rowmask"] = ((np.arange(16)[:, None] // 4) == np.arange(4)[None, :]).astype(np.float32)
    return c


def build_program(debug=False, STOP=0, SAMPLE=True, NTILES_D=0, TILES_D=None):
    nc = bass.Bass("TRN2", target_bir_lowering=False)

    def din(name, shape, dt=F32):
        return nc.dram_tensor(name, list(shape), dt, kind="ExternalInput")

    def dout(name, shape, dt=F32):
        return nc.dram_tensor(name, list(shape), dt, kind="ExternalOutput")

    x_p = din("x_p", [T, D])
    x_s = din("x_s", [NS, D])
    rel_bias = din("rel_bias", [32, 24])
    w_in = din("w_in", [D, 9744])
    c_ident = din("ident", [128, 128])
    c_flip = din("flip", [128, 128])
    c_oh = din("oh_bias", [33, 3, 384])
    c_tric = din("tri_c", [128, 128])
    c_trirev = din("tri_rev", [128, 128])
    c_maskt = din("mask_t", [128, 64])
    w_g2 = din("w_gla_gate2", [16, 512])
    b_g = din("b_gla_gate", [1, 512])
    g_norm = din("g_gla_norm", [1, 1024])
    st_p = dout("st_p", [4, 128, 256])
    w_ba = din("w_branch_a", [512, 1024])
    w_bb = din("w_branch_b", [1024, 1024])
    w_o = din("w_out", [1024, 1024])
    ln1_g = din("ln1_g", [1, 1024])
    ln1_b = din("ln1_b", [1, 1024])
    ln2_g = din("ln2_g", [1, 1024])
    ln2_b = din("ln2_b", [1, 1024])
    w_pq = din("w_peer_query", [1024, 2048])
    sub_keys = din("peer_sub_keys", [8, 2, 128, 128])
    peer_u = din("peer_u", [16384, 1024])
    peer_v = din("peer_v", [16384, 1024])
    c_iota = din("iota_c", [128, 256], I32)
    c_esel = din("e_sel", [128, 16, 16])
    c_sel16 = din("sel16", [16, 16, 128])
    c_ohs = din("ohs", [32, 3, 128])
    c_tric16 = din("tri_c16", [16, 16])
    c_trirev16 = din("tri_rev16", [16, 16])
    c_mask16 = din("mask16", [16, 16])
    c_colmask = din("colmask", [128, 4, 16])
    c_rowmask = din("rowmask", [16, 4])
    WBS = (128, 512, 2048)
    cache = [din("cache%d" % (g + 1), [4, WBS[g], 1024]) for g in range(3)]
    state_in = din("state", [4, 4, 128, 256])
    kv_s = [dout("kv%d_s" % (g + 1), [4, WBS[g], 1024]) for g in range(3)]
    st_s = dout("st_s", [4, 4, 128, 256])
    ext_d = nc.dram_tensor("ext_d", [4, 132, 1024], F32, kind="Internal")
    y_p = dout("y_p", [T, D])
    y_s = dout("y_s", [NS, D])
    if debug:
        X1_d = dout("X1", [T + NS, D])
    else:
        X1_d = nc.dram_tensor("X1", [T + NS, D], F32, kind="Internal")
    UV_d = nc.dram_tensor("UV_d", [16384, 2048], BF16, kind="Internal")
    U_s = nc.dram_tensor("U_s", [NS, 520], F32, kind="Internal")
    OB_s = nc.dram_tensor("OB_s", [NS, 1024], F32, kind="Internal")
    if debug:
        OB_d = dout("OB", [T, 1024])
    else:
        OB_d = nc.dram_tensor("OB", [T, 1024], F32, kind="Internal")

    kv_p = [dout("kv%d_p" % (g + 1), [128 * A_DIL[g], 1024]) for g in range(3)]
    if debug:
        U_d = [dout("U%d" % g, [T, 520]) for g in range(3)]
    else:
        U_d = [nc.dram_tensor("U%d" % g, [T, 520], F32, kind="Internal") for g in range(3)]
    vec_d = nc.dram_tensor("vec_d", [24, 384], F32, kind="Internal")

    with ExitStack() as st:
        P = Prog(nc, st)
        ident = P.sb("ident", [128, 128], F32)
        identb = P.sb("identb", [128, 128], BF16)
        flip = P.sb("flip", [128, 128], F32)
        wst = [P.sb("wst%d" % i, [128, 8, 256], F32) for i in range(2)]
        P.push()
        xT = P.sb("xT", [128, 8, T + NS], BF16)
        TBL = P.wrap("TBL", None)
        P.push()
        rext = P.sb("rext", [33, 24], F32)
        oh = P.sb("oh", [33, 3, 384], F32)
        PS = [P.ps("b%d" % i, [128, 512], F32) for i in range(8)]
        psi = [0]

        ps_reserved = set()

        def psn():
            while True:
                b = PS[psi[0] % 8]
                psi[0] += 1
                if b.name not in ps_reserved:
                    return b

        ev = [0]

        def evac(out_buf, out_ap, in_buf, in_ap, scale=None, partial=False, eng=None):
            if eng is None:
                eng = "vector" if ev[0] % 2 == 0 else "scalar"
                ev[0] += 1
            if eng == "vector":
                if scale is None:
                    P.op("vector", lambda e: e.tensor_copy(out=out_ap, in_=in_ap), reads=[in_buf], writes=[out_buf], partial=partial)
                else:
                    P.op("vector", lambda e: e.tensor_single_scalar(out=out_ap, in_=in_ap, scalar=scale, op=ALU.mult), reads=[in_buf], writes=[out_buf], partial=partial)
            else:
                if scale is None:
                    P.op("scalar", lambda e: e.copy(out=out_ap, in_=in_ap), reads=[in_buf], writes=[out_buf], partial=partial)
                else:
                    P.op("scalar", lambda e: e.mul(out=out_ap, in_=in_ap, mul=scale), reads=[in_buf], writes=[out_buf], partial=partial)

        def mmgroup(out_buf, specs, reads):
            fns = []
            for out_ap, items in specs:
                n = len(items)
                for i, (l, r) in enumerate(items):
                    fns.append(lambda e, o=out_ap, l=l, r=r, a=(i == 0), b=(i == n - 1): e.matmul(o, lhsT=l, rhs=r, start=a, stop=b))
            P.group("tensor", fns, reads=reads, writes=[out_buf])

        P.dma("sync", [lambda e: e.dma_start(out=ident[:], in_=c_ident.ap())], ident, writes=[ident])
        P.dma("sync", [lambda e: e.dma_start(out=flip[:], in_=c_flip.ap())], flip, writes=[flip])
        P.dma("sync", [lambda e: e.dma_start(out=oh[:], in_=c_oh.ap())], oh, writes=[oh])
        P.op("vector", lambda e: e.memset(rext[:], NEG), writes=[rext])
        P.dma("sync", [lambda e: e.dma_start(out=rext[0:32, :], in_=rel_bias.ap())], rext, writes=[rext])
        P.op("vector", lambda e: e.tensor_copy(out=identb[:], in_=ident[:]), reads=[ident], writes=[identb])

        vecs = P.sb("vecs", [8, 3, 384], F32)
        VEC = P.wrap("vec_d", vec_d)
        for g in range(3):
            pb = psn()
            mmgroup(pb, [(pb[0:8, 0:384], [(rext[:, g * 8:(g + 1) * 8], oh[:, g, :])])], reads=[rext, oh])
            evac(vecs, vecs[:, g, :], pb, pb[0:8, 0:384], partial=True, eng="vector")
        P.dma("gpsimd", [lambda e: e.dma_start(out=vec_d.ap().rearrange("(g h) m -> h g m", g=3), in_=vecs[:])], vecs, reads=[vecs], writes=[VEC])

        xin = [P.sb("xin%d" % i, [128, D], F32) for i in range(2)]
        xbf = [P.sb("xbf%d" % i, [128, D], BF16) for i in range(2)]
        NTILE = T // 128
        for n in range(NTILE + 1):
            xi = xin[n % 2]
            xb = xbf[n % 2]
            rows = 128 if n < NTILE else NS
            src = x_p.ap()[n * 128:(n + 1) * 128, :] if n < NTILE else x_s.ap()
            P.dma("sync", [lambda e, xi=xi, src=src, rows=rows: e.dma_start(out=xi[0:rows, :], in_=src)], xi, writes=[xi])
            P.op("gpsimd", lambda e, xi=xi, xb=xb, rows=rows: e.tensor_copy(out=xb[0:rows, :], in_=xi[0:rows, :]), reads=[xi], writes=[xb])
            pb = psn()
            pbv = pb[:].bitcast(BF16)
            fns = []
            for k in range(8):
                fns.append(lambda e, k=k, xb=xb, pbv=pbv, rows=rows: e.transpose(out=pbv[:, k * 128:k * 128 + rows], in_=xb[0:rows, k * 128:(k + 1) * 128], identity=identb[0:rows, 0:rows]))
            P.group("tensor", fns, reads=[xb, identb], writes=[pb])
            evac(xT, xT[:, :, n * 128:n * 128 + rows], pb, pbv.rearrange("p (k t) -> p k t", k=8)[:, :, 0:rows], partial=True)

        if STOP == 1:
            P.finish()
            return nc
        P.pop()
        wi = [0, 0]
        w_in_v = w_in.ap().rearrange("(k p) n -> p k n", p=128)

        wbf_ring = [None]
        UB_all = P.wrap("UB_all", None)
        X1ALL = P.wrap("X1ALL", None)

        def alloc_wbf(tag):
            wbf_ring[0] = [P.sb("wbf%s%d" % (tag, i), [128, 8, 256], BF16) for i in range(4)]

        def get_w(c0, ncol=256):
            s = wst[wi[0] % 2]
            wi[0] += 1
            b = wbf_ring[0][wi[1] % 4]
            wi[1] += 1
            P.dma("sync", [lambda e: e.dma_start(out=s[:, :, 0:ncol], in_=w_in_v[:, :, c0:c0 + ncol])], s, writes=[s])
            P.op("gpsimd", lambda e: e.tensor_copy(out=b[:, :, 0:ncol], in_=s[:, :, 0:ncol]), reads=[s], writes=[b])
            return b


        def phase_b():
            P.push()
            alloc_wbf("B")
            tric = P.sb("tric", [128, 128], F32)
            trirev = P.sb("trirev", [128, 128], F32)
            maskt = P.sb("maskt", [128, 64], F32)
            w2e = P.sb("w2e", [17, 512], F32)
            abT = P.sb("abT", [17, T], F32)
            Gb = P.sb("Gb", [64, 1024], F32)
            P.dma("sync", [lambda e: e.dma_start(out=tric[:], in_=c_tric.ap())], tric, writes=[tric])
            P.dma("sync", [lambda e: e.dma_start(out=trirev[:], in_=c_trirev.ap())], trirev, writes=[trirev])
            P.dma("sync", [lambda e: e.dma_start(out=maskt[:], in_=c_maskt.ap())], maskt, writes=[maskt])
            P.dma("sync", [lambda e: e.dma_start(out=w2e[0:16, :], in_=w_g2.ap()), lambda e: e.dma_start(out=w2e[16:17, :], in_=b_g.ap())], w2e, writes=[w2e])
            P.dma("sync", [lambda e: e.dma_start(out=Gb[:], in_=g_norm.ap().partition_broadcast(64))], Gb, writes=[Gb])
            P.op("vector", lambda e: e.memset(abT[:], 1.0), writes=[abT])
            wab = get_w(COL_AB, 16)
            for tc in range(T // 512):
                pb = psn()
                mmgroup(pb, [(pb[0:16, :], [(wab[:, k, 0:16], xT[:, k, tc * 512:(tc + 1) * 512]) for k in range(8)])], reads=[wab, xT])
                evac(abT, abT[0:16, tc * 512:(tc + 1) * 512], pb, pb[0:16, :], partial=(tc > 0))
            qeT = P.sb("qeT", [128, T], BF16)
            keT = P.sb("keT", [128, T], BF16)
            kd = P.sb("kd", [128, T // 128, 128], BF16)
            vv = P.sb("vv", [128, T // 128, 256], BF16)
            dec = P.sb("dec", [128, T // 64], F32)
            t1 = [P.sb("t1_%d" % i, [128, 4, 128], F32) for i in range(2)]
            la = [P.sb("la%d" % i, [128, 4, 128], F32) for i in range(2)]
            eb = [P.sb("eb%d" % i, [128, 512], F32) for i in range(2)]
            enb = [P.sb("enb%d" % i, [128, 512], F32) for i in range(2)]
            erev = [P.sb("erev%d" % i, [128, 4, 128], F32) for i in range(2)]
            Sst = P.sb("Sst", [128, 256], F32)
            Sbf = P.sb("Sbf", [128, 256], BF16)
            attm = [P.sb("attm%d" % i, [128, 64], BF16) for i in range(2)]
            sr = [P.sb("sr%d" % i, [64, 256], F32) for i in range(2)]
            osq = P.sb("osq", [64, 256], F32)
            ss = [P.sb("ss%d" % i, [64, 2], F32) for i in range(2)]
            ot = [P.sb("ot%d" % i, [64, 256], F32) for i in range(2)]
            oo = [P.sb("oo%d" % i, [64, 256], F32) for i in range(2)]
            OB = UB_all
            STP = P.wrap("st_p", st_p)
            SC = 128.0 ** -0.5
            for h in range(4):
                wq = get_w(COL_QB + h * 128, 128)
                wk = get_w(COL_KB + h * 128, 128)
                wv = get_w(COL_VB + h * 256, 256)
                wr = get_w(COL_RB + h * 256, 256)
                for tc in range(T // 512):
                    i2 = tc % 2
                    tsl = slice(tc * 512, (tc + 1) * 512)
                    pz = psn()
                    mmgroup(pz, [(pz[:, i * 128:(i + 1) * 128], [(abT[:, tc * 512 + i * 128:tc * 512 + (i + 1) * 128], w2e[:, h * 128:(h + 1) * 128])]) for i in range(4)], reads=[abT, w2e])
                    P.op("scalar", lambda e, i2=i2, pz=pz: e.activation(out=t1[i2][:].rearrange("p a d -> p (a d)"), in_=pz[:, :], func=AF.Exp, scale=-1.0), reads=[pz], writes=[t1[i2]])
                    P.op("scalar", lambda e, i2=i2: e.activation(out=la[i2][:].rearrange("p a d -> p (a d)"), in_=t1[i2][:].rearrange("p a d -> p (a d)"), func=AF.Ln, bias=1.0, scale=1.0), reads=[t1[i2]], writes=[la[i2]])
                    pbt = psn()
                    mmgroup(pbt, [(pbt[:, i * 128:(i + 1) * 128], [(la[i2][:, i, :], tric[:])]) for i in range(4)], reads=[la[i2], tric])
                    P.op("scalar", lambda e, i2=i2, pbt=pbt: e.activation(out=eb[i2][:], in_=pbt[:, :], func=AF.Exp), reads=[pbt], writes=[eb[i2]])
                    P.op("scalar", lambda e, i2=i2, pbt=pbt: e.activation(out=enb[i2][:], in_=pbt[:, :], func=AF.Exp, scale=-1.0), reads=[pbt], writes=[enb[i2]])
                    P.op("vector", lambda e, i2=i2, tc=tc: e.tensor_copy(out=dec[:, tc * 8:(tc + 1) * 8], in_=eb[i2][:, 63:512:64]), reads=[eb[i2]], writes=[dec], partial=True)
                    prv = psn()
                    mmgroup(prv, [(prv[:, i * 128:(i + 1) * 128], [(trirev[:], la[i2][:, i, :])]) for i in range(4)], reads=[la[i2], trirev])
                    P.op("scalar", lambda e, i2=i2, prv=prv: e.activation(out=erev[i2][:].rearrange("p a d -> p (a d)"), in_=prv[:, :], func=AF.Exp), reads=[prv], writes=[erev[i2]])
                    pq = psn()
                    mmgroup(pq, [(pq[:, :], [(wq[:, k, 0:128], xT[:, k, tsl]) for k in range(8)])], reads=[wq, xT])
                    P.op("vector", lambda e, i2=i2, pq=pq, tsl=tsl: e.scalar_tensor_tensor(out=qeT[:, tsl], in0=pq[:, :], scalar=SC, in1=eb[i2][:], op0=ALU.mult, op1=ALU.mult), reads=[pq, eb[i2]], writes=[qeT], partial=True)
                    pk = psn()
                    mmgroup(pk, [(pk[:, :], [(wk[:, k, 0:128], xT[:, k, tsl]) for k in range(8)])], reads=[wk, xT])
                    P.op("vector", lambda e, i2=i2, pk=pk, tsl=tsl: e.tensor_tensor(out=keT[:, tsl], in0=pk[:, :], in1=enb[i2][:], op=ALU.mult), reads=[pk, enb[i2]], writes=[keT], partial=True)
                    pkt = psn()
                    mmgroup(pkt, [(pkt[:, i * 128:(i + 1) * 128], [(xT[:, k, tc * 512 + i * 128:tc * 512 + (i + 1) * 128], wk[:, k, 0:128]) for k in range(8)]) for i in range(4)], reads=[wk, xT])
                    P.op("vector", lambda e, i2=i2, pkt=pkt, tc=tc: e.tensor_tensor(out=kd[:, tc * 4:(tc + 1) * 4, :], in0=pkt[:, :].rearrange("p (a d) -> p a d", a=4), in1=erev[i2][:], op=ALU.mult), reads=[pkt, erev[i2]], writes=[kd], partial=True)
                    for i in range(2):
                        pv = psn()
                        mmgroup(pv, [(pv[:, a * 256:(a + 1) * 256], [(xT[:, k, tc * 512 + (i * 2 + a) * 128:tc * 512 + (i * 2 + a + 1) * 128], wv[:, k, 0:256]) for k in range(8)]) for a in range(2)], reads=[wv, xT])
                        evac(vv, vv[:, tc * 4 + i * 2:tc * 4 + i * 2 + 2, :], pv, pv[:, :].rearrange("p (a d) -> p a d", a=2), partial=True)
                P.op("vector", lambda e: e.memset(Sst[:], 0.0), writes=[Sst])
                P.op("vector", lambda e: e.memset(Sbf[:], 0.0), writes=[Sbf])
                for c in range(T // 64):
                    n, half = c // 2, c % 2
                    pr = slice(half * 64, half * 64 + 64)
                    csl = slice(c * 64, c * 64 + 64)
                    i2 = c % 2
                    prb = psn()
                    mmgroup(prb, [(prb[0:64, 0:256], [(xT[:, k, csl], wr[:, k, 0:256]) for k in range(8)])], reads=[wr, xT])
                    P.op("scalar", lambda e, i2=i2, prb=prb: e.activation(out=sr[i2][:], in_=prb[0:64, 0:256], func=AF.Silu), reads=[prb], writes=[sr[i2]])
                    pa = psn()
                    mmgroup(pa, [(pa[:, 0:64], [(keT[:, n * 128:(n + 1) * 128], qeT[:, csl])])], reads=[keT, qeT])
                    P.op("vector", lambda e, i2=i2, pa=pa, pr=pr: e.tensor_tensor(out=attm[i2][pr, :], in0=pa[pr, 0:64], in1=maskt[pr, :], op=ALU.mult), reads=[pa, maskt], writes=[attm[i2]])
                    po = psn()
                    mmgroup(po, [(po[0:64, 0:256], [(attm[i2][pr, :], vv[pr, n, :]), (qeT[:, csl], Sbf[:])])], reads=[attm[i2], vv, qeT, Sbf])
                    pS = psn()
                    mmgroup(pS, [(pS[:, 0:256], [(kd[pr, n, :], vv[pr, n, :])])], reads=[kd, vv])
                    P.op("vector", lambda e, c=c, pS=pS: e.scalar_tensor_tensor(out=Sst[:], in0=Sst[:], scalar=dec[:, c:c + 1], in1=pS[:, 0:256], op0=ALU.mult, op1=ALU.add), reads=[Sst, dec, pS], writes=[Sst])
                    P.op("scalar", lambda e: e.copy(out=Sbf[:], in_=Sst[:]), reads=[Sst], writes=[Sbf])
                    P.op("gpsimd", lambda e, i2=i2: e.memset(ss[i2][:], 0.0), writes=[ss[i2]])
                    P.op("scalar", lambda e, i2=i2, po=po: e.activation(out=osq[:], in_=po[0:64, 0:256], func=AF.Square, accum_out=ss[i2][:, 0:1]), reads=[po], writes=[osq, ss[i2]])
                    P.op("vector", lambda e, i2=i2: e.tensor_scalar(out=ss[i2][:, 1:2], in0=ss[i2][:, 0:1], scalar1=1.0 / 256.0, scalar2=1e-5, op0=ALU.mult, op1=ALU.add), reads=[ss[i2]], writes=[ss[i2]])
                    P.op("scalar", lambda e, i2=i2: e.sqrt(out=ss[i2][:, 1:2], in_=ss[i2][:, 1:2]), reads=[ss[i2]], writes=[ss[i2]])
                    P.op("vector", lambda e, i2=i2: e.reciprocal(out=ss[i2][:, 1:2], in_=ss[i2][:, 1:2]), reads=[ss[i2]], writes=[ss[i2]])
                    P.op("vector", lambda e, i2=i2, po=po, h=h: e.scalar_tensor_tensor(out=ot[i2][:], in0=po[0:64, 0:256], scalar=ss[i2][:, 1:2], in1=Gb[:, h * 256:(h + 1) * 256], op0=ALU.mult, op1=ALU.mult), reads=[po, ss[i2], Gb], writes=[ot[i2]])
                    P.op("gpsimd", lambda e, i2=i2: e.tensor_tensor(out=oo[i2][:], in0=ot[i2][:], in1=sr[i2][:], op=ALU.mult), reads=[ot[i2], sr[i2]], writes=[oo[i2]])
                    P.dma("sync", [lambda e, i2=i2, c=c, h=h: e.dma_start(out=OB_d.ap()[c * 64:(c + 1) * 64, h * 256:(h + 1) * 256], in_=oo[i2][:])], oo[i2], reads=[oo[i2]], writes=[OB], is_output=debug, partial=True)
                P.dma("sync", [lambda e, h=h: e.dma_start(out=st_p.ap()[h], in_=Sst[:])], Sst, reads=[Sst], writes=[STP], is_output=True, partial=True)
            P.pop()


        ALPHA = 2.0 ** 0.25

        def load_wres(dst, src_view, K, ncols):
            for c0 in range(0, ncols, 256):
                st_ = wst[wi[0] % 2]
                wi[0] += 1
                P.dma("sync", [lambda e, st_=st_, c0=c0: e.dma_start(out=st_[:, 0:K, :], in_=src_view[:, :, c0:c0 + 256])], st_, writes=[st_])
                ce = ("gpsimd", "vector", "scalar")[(c0 // 256) % 3]
                if ce == "scalar":
                    P.op("scalar", lambda e, st_=st_, c0=c0: e.copy(out=dst[:, :, c0:c0 + 256], in_=st_[:, 0:K, :]), reads=[st_], writes=[dst], partial=True)
                else:
                    P.op(ce, lambda e, st_=st_, c0=c0: e.tensor_copy(out=dst[:, :, c0:c0 + 256], in_=st_[:, 0:K, :]), reads=[st_], writes=[dst], partial=True)

        def layernorm(rows, y, junk, st4, Gt, Bt, out):
            r = slice(0, rows)
            P.op("vector", lambda e: e.tensor_reduce(out=st4[r, 0:1], in_=y[r, :], axis=AX.X, op=ALU.add), reads=[y], writes=[st4])
            P.op("vector", lambda e: e.tensor_single_scalar(out=st4[r, 1:2], in_=st4[r, 0:1], scalar=-1.0 / 1024.0, op=ALU.mult), reads=[st4], writes=[st4])
            P.op("vector", lambda e: e.tensor_scalar(out=y[r, :], in0=y[r, :], scalar1=st4[r, 1:2], scalar2=None, op0=ALU.add), reads=[y, st4], writes=[y])
            P.op("gpsimd", lambda e: e.memset(st4[r, 2:3], 0.0), reads=[], writes=[st4], partial=True)
            P.op("vector", lambda e: e.scalar_tensor_tensor(out=junk[r, :], in0=y[r, :], scalar=1.0, in1=y[r, :], op0=ALU.mult, op1=ALU.mult, accum_out=st4[r, 2:3]), reads=[y, st4], writes=[junk, st4])
            P.op("vector", lambda e: e.tensor_scalar(out=st4[r, 3:4], in0=st4[r, 2:3], scalar1=1.0 / 1024.0, scalar2=1e-5, op0=ALU.mult, op1=ALU.add), reads=[st4], writes=[st4])
            P.op("scalar", lambda e: e.sqrt(out=st4[r, 3:4], in_=st4[r, 3:4]), reads=[st4], writes=[st4])
            P.op("vector", lambda e: e.reciprocal(out=st4[r, 3:4], in_=st4[r, 3:4]), reads=[st4], writes=[st4])
            P.op("vector", lambda e: e.scalar_tensor_tensor(out=out[r, :], in0=y[r, :], scalar=st4[r, 3:4], in1=Gt[r, :], op0=ALU.mult, op1=ALU.mult), reads=[y, st4, Gt], writes=[out])
            P.op("gpsimd", lambda e: e.tensor_tensor(out=out[r, :], in0=out[r, :], in1=Bt[r, :], op=ALU.add), reads=[out, Bt], writes=[out])

        def transpose_to(dstT, src_bf, rows, nk):
            pb = psn()
            pbv = pb[:].bitcast(BF16)
            fns = []
            for k in range(nk):
                fns.append(lambda e, k=k: e.transpose(out=pbv[:, k * 128:k * 128 + rows], in_=src_bf[0:rows, k * 128:(k + 1) * 128], identity=identb[0:rows, 0:rows]))
            P.group("tensor", fns, reads=[src_bf, identb], writes=[pb])
            evac(dstT, dstT[:, 0:nk, 0:rows], pb, pbv[:, 0:nk * 128].rearrange("p (k t) -> p k t", k=nk)[:, :, 0:rows])

        def phase_c():
            P.push()
            Wa = P.sb("Wa", [128, 4, 1024], BF16)
            Wb = P.sb("Wb", [128, 8, 1024], BF16)
            Wo = P.sb("Wo", [128, 8, 1024], BF16)
            Wga = P.sb("Wga", [128, 8, 1024], BF16)
            Wgb = P.sb("Wgb", [128, 8, 1024], BF16)
            load_wres(Wa, w_ba.ap().rearrange("(k p) n -> p k n", p=128), 4, 1024)
            load_wres(Wb, w_bb.ap().rearrange("(k p) n -> p k n", p=128), 8, 1024)
            load_wres(Wo, w_o.ap().rearrange("(k p) n -> p k n", p=128), 8, 1024)
            load_wres(Wga, w_in_v[:, :, COL_GA:COL_GA + 1024], 8, 1024)
            load_wres(Wgb, w_in_v[:, :, COL_GB:COL_GB + 1024], 8, 1024)
            G1 = P.sb("G1", [128, 1024], F32)
            B1 = P.sb("B1", [128, 1024], F32)
            P.dma("sync", [lambda e: e.dma_start(out=G1[:], in_=ln1_g.ap().partition_broadcast(128))], G1, writes=[G1])
            P.dma("sync", [lambda e: e.dma_start(out=B1[:], in_=ln1_b.ap().partition_broadcast(128))], B1, writes=[B1])
            Ut = [P.sb("Ut%d" % i, [128, 8, 65], F32) for i in range(3)]
            OBt = P.sb("OBt", [128, 1024], F32)
            xt = P.sb("xt", [128, 1024], F32)
            rden = P.sb("rden", [128, 8], F32)
            oab = P.sb("oab", [128, 512], BF16)
            obb = P.sb("obb", [128, 1024], BF16)
            oaT = P.sb("oaT", [128, 4, 128], BF16)
            obT = P.sb("obT", [128, 8, 128], BF16)
            sg = P.sb("sg", [128, 1024], F32)
            mixed = P.sb("mixed", [128, 1024], F32)
            mixb = P.sb("mixb", [128, 1024], BF16)
            mixT = P.sb("mixT", [128, 8, 128], BF16)
            st4 = P.sb("st4c", [128, 4], F32)
            X1 = X1ALL
            Ubufs = [P.wrap("Ux%d" % g, U_d[g]) for g in range(3)]
            def tile_c(n):
                samp = (n == T // 128)
                rows = NS if samp else 128
                r = slice(0, rows)
                tcol = slice(n * 128, n * 128 + rows)
                if samp:
                    usrc = [U_s.ap()]
                    obsrc = OB_s.ap()
                    xsrc = x_s.ap()
                else:
                    usrc = [U_d[g].ap()[n * 128:(n + 1) * 128, :] for g in range(3)]
                    obsrc = OB_d.ap()[n * 128:(n + 1) * 128, :]
                    xsrc = x_p.ap()[n * 128:(n + 1) * 128, :]
                for i, us in enumerate(usrc):
                    P.dma("sync", [lambda e, i=i, us=us: e.dma_start(out=Ut[i][r].rearrange("p h d -> p (h d)"), in_=us)], Ut[i], reads=[UB_all], writes=[Ut[i]])
                P.dma("sync", [lambda e, obsrc=obsrc: e.dma_start(out=OBt[r, :], in_=obsrc)], OBt, reads=[UB_all], writes=[OBt])
                P.dma("sync", [lambda e, xsrc=xsrc: e.dma_start(out=xt[r, :], in_=xsrc)], xt, writes=[xt])
                for i in range(1, len(usrc)):
                    P.op("vector", lambda e, i=i: e.tensor_tensor(out=Ut[0][r], in0=Ut[0][r], in1=Ut[i][r], op=ALU.add), reads=[Ut[0], Ut[i]], writes=[Ut[0]])
                P.op("vector", lambda e: e.reciprocal(out=rden[r, :], in_=Ut[0][r, :, 64]), reads=[Ut[0]], writes=[rden])
                P.op("vector", lambda e: e.tensor_tensor(out=oab[r, :].rearrange("p (h d) -> p h d", h=8), in0=Ut[0][r, :, 0:64], in1=rden[r, :].unsqueeze(2).to_broadcast([rows, 8, 64]), op=ALU.mult), reads=[Ut[0], rden], writes=[oab])
                P.op("gpsimd", lambda e: e.tensor_copy(out=obb[r, :], in_=OBt[r, :]), reads=[OBt], writes=[obb])
                transpose_to(oaT, oab, rows, 4)
                transpose_to(obT, obb, rows, 8)
                for nh in range(2):
                    csl = slice(nh * 512, (nh + 1) * 512)
                    pg = psn()
                    mmgroup(pg, [(pg[r, :], [(xT[:, k, tcol], Wga[:, k, csl]) for k in range(8)])], reads=[xT, Wga])
                    P.op("scalar", lambda e, pg=pg, csl=csl: e.activation(out=sg[r, csl], in_=pg[r, :], func=AF.Sigmoid), reads=[pg], writes=[sg], partial=True)
                    pa = psn()
                    mmgroup(pa, [(pa[r, :], [(oaT[:, k, 0:rows], Wa[:, k, csl]) for k in range(4)])], reads=[oaT, Wa])
                    P.op("vector", lambda e, pa=pa, csl=csl: e.tensor_tensor(out=mixed[r, csl], in0=pa[r, :], in1=sg[r, csl], op=ALU.mult), reads=[pa, sg], writes=[mixed], partial=True)
                for nh in range(2):
                    csl = slice(nh * 512, (nh + 1) * 512)
                    pg = psn()
                    mmgroup(pg, [(pg[r, :], [(xT[:, k, tcol], Wgb[:, k, csl]) for k in range(8)])], reads=[xT, Wgb])
                    P.op("scalar", lambda e, pg=pg, csl=csl: e.activation(out=sg[r, csl], in_=pg[r, :], func=AF.Sigmoid), reads=[pg, mixed], writes=[sg], partial=True)
                    pb2 = psn()
                    mmgroup(pb2, [(pb2[r, :], [(obT[:, k, 0:rows], Wb[:, k, csl]) for k in range(8)])], reads=[obT, Wb])
                    P.op("vector", lambda e, pb2=pb2, csl=csl: e.tensor_tensor(out=sg[r, csl], in0=pb2[r, :], in1=sg[r, csl], op=ALU.mult), reads=[pb2, sg], writes=[sg], partial=True)
                P.op("gpsimd", lambda e: e.tensor_tensor(out=mixed[r, :], in0=mixed[r, :], in1=sg[r, :], op=ALU.add), reads=[mixed, sg], writes=[mixed])
                P.op("scalar", lambda e: e.copy(out=mixb[r, :], in_=mixed[r, :]), reads=[mixed], writes=[mixb])
                transpose_to(mixT, mixb, rows, 8)
                for nh in range(2):
                    csl = slice(nh * 512, (nh + 1) * 512)
                    py = psn()
                    mmgroup(py, [(py[r, :], [(mixT[:, k, 0:rows], Wo[:, k, csl]) for k in range(8)])], reads=[mixT, Wo])
                    P.op("vector", lambda e, py=py, csl=csl: e.scalar_tensor_tensor(out=xt[r, csl], in0=xt[r, csl], scalar=ALPHA, in1=py[r, :], op0=ALU.mult, op1=ALU.add), reads=[xt, py], writes=[xt], partial=(nh > 0))
                layernorm(rows, xt, sg, st4, G1, B1, mixed)
                P.dma("gpsimd", [lambda e, n=n, rows=rows: e.dma_start(out=X1_d.ap()[n * 128:n * 128 + rows, :], in_=mixed[0:rows, :])], mixed, reads=[mixed], writes=[X1], is_output=debug, partial=True)

            for n in range(T // 128 + (1 if SAMPLE else 0)):
                tile_c(n)
            P.pop()


        def phase_d():
            P.push()
            wpq = P.sb("wpq", [128, 8, 2048], BF16)
            load_wres(wpq, w_pq.ap().rearrange("(k p) n -> p k n", p=128), 8, 2048)
            KTt = P.sb("KTt", [128, 16, 128], F32)
            P.push()
            kraw = P.sb("kraw", [128, 16, 128], F32)
            P.dma("sync", [lambda e: e.dma_start(out=kraw[:], in_=sub_keys.ap().rearrange("h j k d -> k (h j) d"))], kraw, writes=[kraw])
            for q4 in range(4):
                pb = psn()
                fns = []
                for i in range(4):
                    fns.append(lambda e, i=i, q4=q4, pb=pb: e.transpose(out=pb[:, i * 128:(i + 1) * 128], in_=kraw[:, q4 * 4 + i, :], identity=ident[:]))
                P.group("tensor", fns, reads=[kraw, ident], writes=[pb])
                evac(KTt, KTt[:, q4 * 4:(q4 + 1) * 4, :], pb, pb[:, :].rearrange("p (a k) -> p a k", a=4), partial=True)
            P.pop()
            G2 = P.sb("G2", [128, 1024], F32)
            B2 = P.sb("B2", [128, 1024], F32)
            iot = P.sb("iot", [128, 256], I32)
            iotf = P.sb("iotf", [128, 16], F32)
            P.dma("sync", [lambda e: e.dma_start(out=G2[:], in_=ln2_g.ap().partition_broadcast(128))], G2, writes=[G2])
            P.dma("sync", [lambda e: e.dma_start(out=B2[:], in_=ln2_b.ap().partition_broadcast(128))], B2, writes=[B2])
            P.dma("sync", [lambda e: e.dma_start(out=iot[:], in_=c_iota.ap())], iot, writes=[iot])
            P.op("vector", lambda e: e.tensor_copy(out=iotf[:], in_=iot[:, 0:16]), reads=[iot], writes=[iotf])
            x1t_ = [P.sb("x1t%d" % i, [128, 1024], F32) for i in range(2)]
            x1b_ = [P.sb("x1b%d" % i, [128, 1024], BF16) for i in range(2)]
            exi_ = [P.sb("exi%d" % i, [128, 128], I32) for i in range(2)]
            gt_ = [P.sb("gt%d" % i, [128, 8, 16], F32) for i in range(2)]
            x1T = P.sb("x1T", [128, 8, 128], BF16)
            qs = P.sb("qs", [128, 2048], F32)
            junk2 = P.sb("junk2", [128, 2048], F32)
            st8 = P.sb("st8", [128, 4, 8], F32)
            qnT = P.sb("qnT", [128, 16, 128], F32)
            ssb = P.sb("ssb", [128, 2048], F32)
            wk = P.sb("wk", [128, 256], F32)
            v12 = P.sb("v12", [128, 16, 16], F32)
            e12 = P.sb("e12", [128, 16, 16], I32)
            e12f = P.sb("e12f", [128, 16, 16], F32)
            cand = P.sb("cand", [128, 2048], F32)
            sv = P.sb("sv", [128, 8, 16], F32)
            si = P.sb("si", [128, 3, 128], I32)
            sif = P.sb("sif", [128, 2, 128], F32)
            es = P.sb("es", [128, 3, 128], F32)
            dots = P.sb("dots", [128, 128], F32)
            wgt = P.sb("wgt", [128, 128], F32)
            wgb = P.sb("wgb", [128, 128], BF16)
            wd = [P.sb("wd%d" % i, [128, 8, 128], BF16) for i in range(2)]
            junkb = P.sb("junkb", [128, 1024], BF16)
            yout = P.sb("yout", [128, 1024], F32)
            st4 = P.sb("st4d", [128, 4], F32)
            NG = 16
            gb_ = [P.sb("gb%d" % i, [128, 2048], BF16) for i in range(NG)]
            gi = [0]
            YP = P.wrap("y_p", y_p)
            YS = P.wrap("y_s", y_s)

            def front_d(n):
                samp = (n == T // 128)
                rows = NS if samp else 128
                r = slice(0, rows)
                x1t, x1b, exi, gt = x1t_[n % 2], x1b_[n % 2], exi_[n % 2], gt_[n % 2]
                P.dma("sync", [lambda e: e.dma_start(out=x1t[r, :], in_=X1_d.ap()[n * 128:n * 128 + rows, :])], x1t, reads=[X1ALL], writes=[x1t])
                P.op("scalar", lambda e: e.copy(out=x1b[r, :], in_=x1t[r, :]), reads=[x1t], writes=[x1b])
                yield
                transpose_to(x1T, x1b, rows, 8)
                yield
                for nb in range(4):
                    pq = psn()
                    mmgroup(pq, [(pq[r, :], [(x1T[:, k, 0:rows], wpq[:, k, nb * 512:(nb + 1) * 512]) for k in range(8)])], reads=[x1T, wpq])
                    evac(qs, qs[r, nb * 512:(nb + 1) * 512], pq, pq[r, :], partial=(nb > 0), eng="scalar")
                    yield
                qv = qs[r, :].rearrange("p (h d) -> p h d", h=8)
                jv = junk2[r, :].rearrange("p (h d) -> p h d", h=8)
                P.op("vector", lambda e: e.tensor_reduce(out=st8[r, 0, :], in_=qv, axis=AX.X, op=ALU.add), reads=[qs], writes=[st8])
                P.op("vector", lambda e: e.tensor_single_scalar(out=st8[r, 1, :], in_=st8[r, 0, :], scalar=-1.0 / 256.0, op=ALU.mult), reads=[st8], writes=[st8])
                yield
                P.op("vector", lambda e: e.tensor_tensor(out=qv, in0=qv, in1=st8[r, 1, :].unsqueeze(2).to_broadcast([rows, 8, 256]), op=ALU.add), reads=[qs, st8], writes=[qs])
                yield
                P.op("scalar", lambda e: e.activation(out=junk2[r, :], in_=qs[r, :], func=AF.Square), reads=[qs], writes=[junk2])
                P.op("vector", lambda e: e.tensor_reduce(out=st8[r, 2, :], in_=jv, axis=AX.X, op=ALU.add), reads=[junk2], writes=[st8])
                yield
                P.op("vector", lambda e: e.tensor_scalar(out=st8[r, 3, :], in0=st8[r, 2, :], scalar1=1.0 / 256.0, scalar2=1e-5, op0=ALU.mult, op1=ALU.add), reads=[st8], writes=[st8])
                P.op("scalar", lambda e: e.sqrt(out=st8[r, 3, :], in_=st8[r, 3, :]), reads=[st8], writes=[st8])
                P.op("vector", lambda e: e.reciprocal(out=st8[r, 3, :], in_=st8[r, 3, :]), reads=[st8], writes=[st8])
                yield
                P.op("vector", lambda e: e.tensor_tensor(out=qv, in0=qv, in1=st8[r, 3, :].unsqueeze(2).to_broadcast([rows, 8, 256]), op=ALU.mult), reads=[qs, st8], writes=[qs])
                yield
                for q4 in range(4):
                    pb = psn()
                    fns = []
                    for i in range(4):
                        fns.append(lambda e, i=i, q4=q4, pb=pb: e.transpose(out=pb[:, i * 128:i * 128 + rows], in_=qs[r, (q4 * 4 + i) * 128:(q4 * 4 + i + 1) * 128], identity=ident[0:rows, 0:rows]))
                    P.group("tensor", fns, reads=[qs, ident], writes=[pb])
                    evac(qnT, qnT[:, q4 * 4:(q4 + 1) * 4, 0:rows], pb, pb[:, :].rearrange("p (a t) -> p a t", a=4)[:, :, 0:rows], partial=(q4 > 0), eng="scalar")
                    yield
                ssi = ssb[:].bitcast(I32)
                for q4 in range(4):
                    pb = psn()
                    mmgroup(pb, [(pb[r, i * 128:(i + 1) * 128], [(qnT[:, q4 * 4 + i, 0:rows], KTt[:, q4 * 4 + i, :])]) for i in range(4)], reads=[qnT, KTt])
                    P.op("vector", lambda e, pb=pb, q4=q4: e.tensor_single_scalar(out=ssi[r, q4 * 512:(q4 + 1) * 512], in_=pb[r, :].bitcast(I32), scalar=-128, op=ALU.bitwise_and), reads=[pb], writes=[ssb], partial=(q4 > 0))
                    yield
                P.op("vector", lambda e: e.tensor_tensor(out=ssi[r, :].rearrange("p (a k) -> p a k", a=16), in0=ssi[r, :].rearrange("p (a k) -> p a k", a=16), in1=iot[r, 0:128].unsqueeze(1).to_broadcast([rows, 16, 128]), op=ALU.bitwise_or), reads=[ssb, iot], writes=[ssb])
                yield
                for hj in range(16):
                    sl = slice(hj * 128, (hj + 1) * 128)
                    P.op("vector", lambda e, hj=hj, sl=sl: e.max(out=v12[r, hj, 0:8], in_=ssb[r, sl]), reads=[ssb], writes=[v12], partial=(hj > 0))
                    P.op("vector", lambda e, hj=hj, sl=sl: e.match_replace(out=wk[r, 0:128], in_to_replace=v12[r, hj, 0:8], in_values=ssb[r, sl], imm_value=-1e30), reads=[ssb, v12], writes=[wk])
                    P.op("vector", lambda e, hj=hj: e.max(out=v12[r, hj, 8:16], in_=wk[r, 0:128]), reads=[wk], writes=[v12], partial=True)
                    if hj % 2 == 1:
                        yield
                P.op("vector", lambda e: e.tensor_single_scalar(out=e12[r], in_=v12[r].bitcast(I32), scalar=127, op=ALU.bitwise_and), reads=[v12], writes=[e12])
                P.op("vector", lambda e: e.tensor_copy(out=e12f[r], in_=e12[r]), reads=[e12], writes=[e12f])
                yield
                v12v = v12[r].rearrange("p (h j) k -> p h j k", j=2)
                e12v = e12f[r].rearrange("p (h j) k -> p h j k", j=2)
                cv = cand[r, :].rearrange("p (h a b) -> p h a b", h=8, a=16)
                ci = cand[:].bitcast(I32)
                P.op("vector", lambda e: e.tensor_tensor(out=cv, in0=v12v[:, :, 0, :].unsqueeze(3).to_broadcast([rows, 8, 16, 16]), in1=v12v[:, :, 1, :].unsqueeze(2).to_broadcast([rows, 8, 16, 16]), op=ALU.add), reads=[v12], writes=[cand])
                yield
                P.op("vector", lambda e: e.tensor_single_scalar(out=ci[r, :], in_=ci[r, :], scalar=-256, op=ALU.bitwise_and), reads=[cand], writes=[cand])
                yield
                P.op("vector", lambda e: e.tensor_tensor(out=ci[r, :].rearrange("p (h c) -> p h c", h=8), in0=ci[r, :].rearrange("p (h c) -> p h c", h=8), in1=iot[r, :].unsqueeze(1).to_broadcast([rows, 8, 256]), op=ALU.bitwise_or), reads=[cand, iot], writes=[cand])
                yield
                for h in range(8):
                    sl = slice(h * 256, (h + 1) * 256)
                    P.op("vector", lambda e, h=h, sl=sl: e.max(out=sv[r, h, 0:8], in_=cand[r, sl]), reads=[cand], writes=[sv], partial=(h > 0))
                    P.op("vector", lambda e, h=h, sl=sl: e.match_replace(out=wk[r, :], in_to_replace=sv[r, h, 0:8], in_values=cand[r, sl], imm_value=-1e30), reads=[cand, sv], writes=[wk])
                    P.op("vector", lambda e, h=h: e.max(out=sv[r, h, 8:16], in_=wk[r, :]), reads=[wk], writes=[sv], partial=True)
                    yield
                svf = sv[r].rearrange("p h k -> p (h k)")
                P.op("vector", lambda e: e.tensor_single_scalar(out=si[r, 0, :], in_=svf.bitcast(I32), scalar=255, op=ALU.bitwise_and), reads=[sv], writes=[si])
                P.op("vector", lambda e: e.tensor_single_scalar(out=si[r, 1, :], in_=si[r, 0, :], scalar=4, op=ALU.logical_shift_right), reads=[si], writes=[si])
                P.op("vector", lambda e: e.tensor_single_scalar(out=si[r, 2, :], in_=si[r, 0, :], scalar=15, op=ALU.bitwise_and), reads=[si], writes=[si])
                P.op("vector", lambda e: e.tensor_copy(out=sif[r], in_=si[r, 1:3, :]), reads=[si], writes=[sif])
                yield
                ohv = junk2[r, :].rearrange("p (h k i) -> p h k i", h=8, k=16)
                iob = iotf[r, :].unsqueeze(1).unsqueeze(1).to_broadcast([rows, 8, 16, 16])
                for j in range(2):
                    idxv = sif[r, j, :].rearrange("p (h k) -> p h k", h=8).unsqueeze(3).to_broadcast([rows, 8, 16, 16])
                    P.op("vector", lambda e, idxv=idxv: e.tensor_tensor(out=ohv, in0=idxv, in1=iob, op=ALU.is_equal), reads=[sif, iotf], writes=[junk2])
                    yield
                    P.op("vector", lambda e, j=j: e.tensor_tensor(out=ohv, in0=ohv, in1=e12v[:, :, j, :].unsqueeze(2).to_broadcast([rows, 8, 16, 16]), op=ALU.mult), reads=[junk2, e12f], writes=[junk2])
                    yield
                    P.op("vector", lambda e, j=j: e.tensor_reduce(out=es[r, j, :].rearrange("p (h k) -> p h k", h=8), in_=ohv, axis=AX.X, op=ALU.add), reads=[junk2], writes=[es], partial=(j > 0))
                    yield
                P.op("vector", lambda e: e.scalar_tensor_tensor(out=es[r, 2, :], in0=es[r, 0, :], scalar=128.0, in1=es[r, 1, :], op0=ALU.mult, op1=ALU.add), reads=[es], writes=[es])
                P.op("vector", lambda e: e.tensor_copy(out=exi[r, :], in_=es[r, 2, :]), reads=[es], writes=[exi])
                yield
                P.op("vector", lambda e: e.tensor_reduce(out=st8[r, 0, :], in_=sv[r], axis=AX.X, op=ALU.max), reads=[sv], writes=[st8])
                P.op("vector", lambda e: e.tensor_tensor(out=gt[r], in0=sv[r], in1=st8[r, 0, :].unsqueeze(2).to_broadcast([rows, 8, 16]), op=ALU.subtract), reads=[sv, st8], writes=[gt])
                P.op("scalar", lambda e: e.activation(out=gt[r], in_=gt[r], func=AF.Exp), reads=[gt], writes=[gt])
                yield
                P.op("vector", lambda e: e.tensor_reduce(out=st8[r, 1, :], in_=gt[r], axis=AX.X, op=ALU.add), reads=[gt], writes=[st8])
                P.op("vector", lambda e: e.reciprocal(out=st8[r, 1, :], in_=st8[r, 1, :]), reads=[st8], writes=[st8])
                P.op("vector", lambda e: e.tensor_tensor(out=gt[r], in0=gt[r], in1=st8[r, 1, :].unsqueeze(2).to_broadcast([rows, 8, 16]), op=ALU.mult), reads=[gt, st8], writes=[gt])
                yield

            def back_d(n):
                samp = (n == T // 128)
                rows = NS if samp else 128
                r = slice(0, rows)
                x1t, x1b, exi, gt = x1t_[n % 2], x1b_[n % 2], exi_[n % 2], gt_[n % 2]
                gtf = gt[r].rearrange("p h k -> p (h k)")
                P.op("gpsimd", lambda e: e.memset(dots[r, :], 0.0), writes=[dots])
                pacc = [psn(), psn()]
                ps_reserved.update(p_.name for p_ in pacc)
                for g8 in range(16):
                    bufs = []
                    for j in range(8):
                        hk = g8 * 8 + j
                        g = gb_[gi[0] % NG]
                        gi[0] += 1
                        bufs.append(g)
                        P.dma("gpsimd", [lambda e, g=g, hk=hk: e.indirect_dma_start(out=g[r, :], out_offset=None, in_=UV_d.ap(), in_offset=bass.IndirectOffsetOnAxis(ap=exi[r, hk:hk + 1], axis=0))], g, reads=[exi, TBL], writes=[g])
                        P.op("vector", lambda e, g=g, hk=hk: e.scalar_tensor_tensor(out=junkb[r, :], in0=g[r, 0:1024], scalar=1.0, in1=x1b[r, :], op0=ALU.mult, op1=ALU.mult, accum_out=dots[r, hk:hk + 1]), reads=[g, x1b, dots], writes=[junkb, dots])
                        yield
                    hs = slice(g8 * 8, g8 * 8 + 8)
                    P.op("scalar", lambda e, hs=hs: e.activation(out=wgt[r, hs], in_=dots[r, hs], func=AF.Gelu), reads=[dots], writes=[wgt])
                    P.op("vector", lambda e, hs=hs: e.tensor_tensor(out=wgb[r, hs], in0=wgt[r, hs], in1=gtf[:, hs], op=ALU.mult), reads=[wgt, gt], writes=[wgb])
                    wdc = wd[g8 % 2]
                    P.op("vector", lambda e, wdc=wdc, hs=hs: e.tensor_tensor(out=wdc[r, :, 0:rows], in0=identb[r, 0:rows].unsqueeze(1).to_broadcast([rows, 8, rows]), in1=wgb[r, hs].unsqueeze(2).to_broadcast([rows, 8, rows]), op=ALU.mult), reads=[identb, wgb], writes=[wdc])
                    fns = []
                    for j in range(8):
                        hk = g8 * 8 + j
                        for hf in range(2):
                            fns.append(lambda e, hf=hf, g=bufs[j], wdc=wdc, j=j, hk=hk: e.matmul(pacc[hf][r, :], lhsT=wdc[r, j, 0:rows], rhs=g[r, 1024 + hf * 512:1024 + (hf + 1) * 512], start=(hk == 0), stop=(hk == 127)))
                    P.group("tensor", fns, reads=[wdc] + bufs, writes=pacc)
                for hf in range(2):
                    P.op("vector", lambda e, hf=hf: e.scalar_tensor_tensor(out=x1t[r, hf * 512:(hf + 1) * 512], in0=x1t[r, hf * 512:(hf + 1) * 512], scalar=ALPHA, in1=pacc[hf][r, :], op0=ALU.mult, op1=ALU.add), reads=[x1t, pacc[hf]], writes=[x1t], partial=(hf > 0))
                ps_reserved.clear()
                yield
                layernorm(rows, x1t, yout, st4, G2, B2, yout)
                if samp:
                    P.dma("sync", [lambda e: e.dma_start(out=y_s.ap(), in_=yout[r, :])], yout, reads=[yout], writes=[YS], is_output=True, partial=True)
                else:
                    P.dma("sync", [lambda e: e.dma_start(out=y_p.ap()[n * 128:(n + 1) * 128, :], in_=yout[r, :])], yout, reads=[yout], writes=[YP], is_output=True, partial=True)
                yield

            tl = list(TILES_D if TILES_D is not None else range(NTILES_D if NTILES_D else (T // 128 + (1 if SAMPLE else 0))))
            for _ in front_d(tl[0]):
                pass
            for i, n in enumerate(tl):
                bk = back_d(n)
                fr = front_d(tl[i + 1]) if i + 1 < len(tl) else None
                nstep = 0
                for _ in bk:
                    nstep += 1
                    if fr is not None and nstep <= 128 and nstep % 2 == 0:
                        try:
                            next(fr)
                        except StopIteration:
                            fr = None
                    if fr is not None and nstep == 128:
                        for _ in fr:
                            pass
                        fr = None
            P.pop()

        def phase_s():
            P.push()
            alloc_wbf("S")
            NPC = COL_GA
            psall = P.sb("psall", [16, NPC], F32)
            SC8 = 0.125
            for c0 in range(0, NPC, 256):
                ncol = min(256, NPC - c0)
                w = get_w(c0, ncol)
                pb = psn()
                mmgroup(pb, [(pb[0:16, 0:ncol], [(xT[:, k, T:T + NS], w[:, k, 0:ncol]) for k in range(8)])], reads=[w, xT])
                evac(psall, psall[:, c0:c0 + ncol], pb, pb[0:16, 0:ncol], partial=(c0 > 0))
            esel = P.sb("esel", [128, 16, 16], F32)
            sel16 = P.sb("sel16", [16, 16, 128], F32)
            ohs = P.sb("ohs", [32, 3, 128], F32)
            relb = P.sb("relb", [32, 24], F32)
            rel0 = P.sb("rel0", [16, 24], F32)
            biasP = P.sb("biasP", [128, 24], F32)
            for (dst, src) in ((esel, c_esel), (sel16, c_sel16), (ohs, c_ohs), (relb, rel_bias)):
                P.dma("sync", [lambda e, dst=dst, src=src: e.dma_start(out=dst[:], in_=src.ap())], dst, writes=[dst])
            P.dma("sync", [lambda e: e.dma_start(out=rel0[:], in_=rel_bias.ap()[0:1, :].partition_broadcast(16))], rel0, writes=[rel0])
            for g in range(3):
                pb = psn()
                mmgroup(pb, [(pb[:, 0:8], [(ohs[:, g, :], relb[:, g * 8:(g + 1) * 8])])], reads=[ohs, relb])
                evac(biasP, biasP[:, g * 8:(g + 1) * 8], pb, pb[:, 0:8], partial=(g > 0), eng="vector")
            KVS = [P.wrap("kvs%d" % g, kv_s[g]) for g in range(3)]
            EXT = P.wrap("ext", ext_d)
            for g in range(3):
                wb_ = WBS[g]
                fns = []
                for b in range(4):
                    fns.append(lambda e, g=g, b=b, wb_=wb_: e.dma_start(out=kv_s[g].ap()[b, 0:wb_ - 4, :], in_=cache[g].ap()[b, 4:wb_, :]))
                P.dma("gpsimd", fns, KVS[g], writes=[KVS[g]], is_output=True, partial=True)
                for part, col in ((0, COL_KA), (1, COL_VA)):
                    dst = bass.AP(kv_s[g], (wb_ - 4) * 1024 + part * 512, [[wb_ * 1024, 4], [1024, 4], [1, 512]])
                    P.dma("gpsimd", [lambda e, dst=dst, col=col, g=g: e.dma_start(out=dst, in_=psall[:, col + g * 512:col + (g + 1) * 512])], KVS[g], reads=[psall], writes=[KVS[g]], is_output=True, partial=True)
            P.dma("gpsimd", [lambda e, b=b: e.dma_start(out=ext_d.ap()[b, 0:128, :], in_=cache[0].ap()[b, :, :]) for b in range(4)], EXT, writes=[EXT], partial=True)
            for part, col in ((0, COL_KA), (1, COL_VA)):
                dst = bass.AP(ext_d, 128 * 1024 + part * 512, [[132 * 1024, 4], [1024, 4], [1, 512]])
                P.dma("gpsimd", [lambda e, dst=dst, col=col: e.dma_start(out=dst, in_=psall[:, col:col + 512])], EXT, reads=[psall], writes=[EXT], partial=True)
            kvg = [P.sb("kvg%d" % i, [128, 1024], F32) for i in range(2)]
            qbc = [P.sb("qbc%d" % i, [128, 512], F32) for i in range(2)]
            jk = P.sb("jks", [128, 512], F32)
            sc = [P.sb("scs%d" % i, [128, 8], F32) for i in range(2)]
            pvs = [P.sb("pvs%d" % i, [128, 8, 65], F32) for i in range(2)]
            pacc = [psn(), psn()]
            ps_reserved.update(p_.name for p_ in pacc)
            cnt = 0
            for g in range(3):
                dil = A_DIL[g]
                for bs in range(16):
                    b, s_ = bs // 4, bs % 4
                    i2 = cnt % 2
                    src_t = ext_d if g == 0 else cache[g]
                    rows_t = 132 if g == 0 else WBS[g]
                    src = bass.AP(src_t, (b * rows_t + s_) * 1024, [[dil * 1024, 128], [1, 1024]])
                    P.dma("sync", [lambda e, src=src, i2=i2: e.dma_start(out=kvg[i2][:], in_=src)], kvg[i2], reads=([EXT] if g == 0 else []), writes=[kvg[i2]])
                    pq = psn()
                    mmgroup(pq, [(pq[:, :], [(sel16[:, bs, :], psall[:, COL_QA + g * 512:COL_QA + (g + 1) * 512])])], reads=[sel16, psall])
                    P.op("scalar", lambda e, pq=pq, i2=i2: e.mul(out=qbc[i2][:], in_=pq[:, :], mul=SC8), reads=[pq], writes=[qbc[i2]])
                    P.op("vector", lambda e, i2=i2: e.tensor_tensor(out=jk[:], in0=kvg[i2][:, 0:512], in1=qbc[i2][:], op=ALU.mult), reads=[kvg[i2], qbc[i2]], writes=[jk])
                    P.op("vector", lambda e, i2=i2: e.tensor_reduce(out=sc[i2][:], in_=jk[:].rearrange("p (h d) -> p h d", h=8), axis=AX.X, op=ALU.add), reads=[jk], writes=[sc[i2]])
                    P.op("vector", lambda e, i2=i2, g=g: e.tensor_tensor(out=sc[i2][:], in0=sc[i2][:], in1=biasP[:, g * 8:(g + 1) * 8], op=ALU.add), reads=[sc[i2], biasP], writes=[sc[i2]])
                    P.op("scalar", lambda e, i2=i2: e.activation(out=pvs[i2][:, :, 64], in_=sc[i2][:], func=AF.Exp), reads=[sc[i2]], writes=[pvs[i2]])
                    P.op("vector", lambda e, i2=i2: e.tensor_tensor(out=pvs[i2][:, :, 0:64], in0=kvg[i2][:, 512:1024].rearrange("p (h d) -> p h d", h=8), in1=pvs[i2][:, :, 64:65].to_broadcast([128, 8, 64]), op=ALU.mult), reads=[kvg[i2], pvs[i2]], writes=[pvs[i2]], partial=True)
                    first, last = (cnt == 0), (cnt == 47)
                    pvf = pvs[i2][:].rearrange("p h d -> p (h d)")
                    fns = []
                    for hf in range(2):
                        fns.append(lambda e, hf=hf, pvf=pvf, bs=bs, first=first, last=last: e.matmul(pacc[hf][0:16, 0:260], lhsT=esel[:, bs, :], rhs=pvf[:, hf * 260:(hf + 1) * 260], start=first, stop=last))
                    P.group("tensor", fns, reads=[esel, pvs[i2]], writes=pacc)
                    cnt += 1
            Us = P.sb("Us", [16, 8, 65], F32)
            jks = P.sb("jk16", [16, 512], F32)
            scs = P.sb("sc16", [16, 8], F32)
            pw = P.sb("pw16", [16, 8, 65], F32)
            for hf in range(2):
                evac(Us, Us[:].rearrange("p h d -> p (h d)")[:, hf * 260:(hf + 1) * 260], pacc[hf], pacc[hf][0:16, 0:260], partial=(hf > 0), eng="vector")
            ps_reserved.clear()
            for g in range(3):
                qsl = slice(COL_QA + g * 512, COL_QA + (g + 1) * 512)
                ksl = slice(COL_KA + g * 512, COL_KA + (g + 1) * 512)
                vsl = slice(COL_VA + g * 512, COL_VA + (g + 1) * 512)
                P.op("vector", lambda e, qsl=qsl, ksl=ksl: e.tensor_tensor(out=jks[:], in0=psall[:, qsl], in1=psall[:, ksl], op=ALU.mult), reads=[psall], writes=[jks])
                P.op("vector", lambda e: e.tensor_reduce(out=scs[:], in_=jks[:].rearrange("p (h d) -> p h d", h=8), axis=AX.X, op=ALU.add), reads=[jks], writes=[scs])
                P.op("vector", lambda e, g=g: e.scalar_tensor_tensor(out=scs[:], in0=scs[:], scalar=SC8, in1=rel0[:, g * 8:(g + 1) * 8], op0=ALU.mult, op1=ALU.add), reads=[scs, rel0], writes=[scs])
                P.op("scalar", lambda e: e.activation(out=pw[:, :, 64], in_=scs[:], func=AF.Exp), reads=[scs], writes=[pw])
                P.op("vector", lambda e, vsl=vsl: e.tensor_tensor(out=pw[:, :, 0:64], in0=psall[:, vsl].rearrange("p (h d) -> p h d", h=8), in1=pw[:, :, 64:65].to_broadcast([16, 8, 64]), op=ALU.mult), reads=[psall, pw], writes=[pw], partial=True)
                P.op("vector", lambda e: e.tensor_tensor(out=Us[:], in0=Us[:], in1=pw[:], op=ALU.add), reads=[Us, pw], writes=[Us])
            P.dma("gpsimd", [lambda e: e.dma_start(out=U_s.ap(), in_=Us[:].rearrange("p h d -> p (h d)"))], Us, reads=[Us], writes=[UB_all], partial=True)

            tric16 = P.sb("tric16", [16, 16], F32)
            trirev16 = P.sb("trirev16", [16, 16], F32)
            mask16 = P.sb("mask16", [16, 16], F32)
            colmask = P.sb("colmask", [128, 4, 16], F32)
            rowmask = P.sb("rowmask", [16, 4], F32)
            w2e = P.sb("w2es", [17, 512], F32)
            Gb = P.sb("Gbs", [16, 1024], F32)
            for (dst, src) in ((tric16, c_tric16), (trirev16, c_trirev16), (mask16, c_mask16), (colmask, c_colmask), (rowmask, c_rowmask)):
                P.dma("sync", [lambda e, dst=dst, src=src: e.dma_start(out=dst[:], in_=src.ap())], dst, writes=[dst])
            P.dma("sync", [lambda e: e.dma_start(out=w2e[0:16, :], in_=w_g2.ap()), lambda e: e.dma_start(out=w2e[16:17, :], in_=b_g.ap())], w2e, writes=[w2e])
            P.dma("sync", [lambda e: e.dma_start(out=Gb[:], in_=g_norm.ap().partition_broadcast(16))], Gb, writes=[Gb])
            abTs = P.sb("abTs", [17, 16], F32)
            P.op("vector", lambda e: e.memset(abTs[:], 1.0), writes=[abTs])
            wab = get_w(COL_AB, 16)
            pb = psn()
            mmgroup(pb, [(pb[0:16, 0:16], [(wab[:, k, 0:16], xT[:, k, T:T + NS]) for k in range(8)])], reads=[wab, xT])
            evac(abTs, abTs[0:16, :], pb, pb[0:16, 0:16], eng="vector")
            SC = 128.0 ** -0.5
            t1 = P.sb("t1s", [16, 128], F32)
            la = P.sb("las", [16, 128], F32)
            eb = P.sb("ebs", [128, 16], F32)
            enb = P.sb("enbs", [128, 16], F32)
            erev = P.sb("erevs", [16, 128], F32)
            qeT = P.sb("qeTs", [128, 16], BF16)
            qeTm = P.sb("qeTms", [128, 4, 16], BF16)
            keT = P.sb("keTs", [128, 16], BF16)
            kdf = P.sb("kdfs", [16, 128], F32)
            kdm = P.sb("kdms", [16, 4, 128], BF16)
            vb_ = P.sb("vbs", [16, 256], BF16)
            attm = P.sb("attms", [16, 16], BF16)
            S0 = [P.sb("S0_%d" % i, [128, 256], F32) for i in range(4)]
            S0b = [P.sb("S0b_%d" % i, [128, 256], BF16) for i in range(4)]
            Sn = [P.sb("Sn_%d" % i, [128, 256], F32) for i in range(2)]
            srs = P.sb("srs", [16, 256], F32)
            osq = P.sb("osqs", [16, 256], F32)
            ss = P.sb("sss", [16, 2], F32)
            ot = P.sb("ots", [16, 256], F32)
            oo = P.sb("oos", [16, 256], F32)
            STS = P.wrap("st_s", st_s)
            for h in range(4):
                wq = get_w(COL_QB + h * 128, 128)
                wk = get_w(COL_KB + h * 128, 128)
                for b in range(4):
                    P.dma("sync", [lambda e, b=b, h=h: e.dma_start(out=S0[b][:], in_=state_in.ap()[b, h])], S0[b], writes=[S0[b]])
                    P.op("gpsimd", lambda e, b=b: e.tensor_copy(out=S0b[b][:], in_=S0[b][:]), reads=[S0[b]], writes=[S0b[b]])
                pz = psn()
                mmgroup(pz, [(pz[0:16, 0:128], [(abTs[:, :], w2e[:, h * 128:(h + 1) * 128])])], reads=[abTs, w2e])
                P.op("scalar", lambda e, pz=pz: e.activation(out=t1[:], in_=pz[0:16, 0:128], func=AF.Exp, scale=-1.0), reads=[pz], writes=[t1])
                P.op("scalar", lambda e: e.activation(out=la[:], in_=t1[:], func=AF.Ln, bias=1.0, scale=1.0), reads=[t1], writes=[la])
                pbt = psn()
                mmgroup(pbt, [(pbt[:, 0:16], [(la[:, :], tric16[:])])], reads=[la, tric16])
                P.op("scalar", lambda e, pbt=pbt: e.activation(out=eb[:], in_=pbt[:, 0:16], func=AF.Exp), reads=[pbt], writes=[eb])
                P.op("scalar", lambda e, pbt=pbt: e.activation(out=enb[:], in_=pbt[:, 0:16], func=AF.Exp, scale=-1.0), reads=[pbt], writes=[enb])
                prv = psn()
                mmgroup(prv, [(prv[0:16, 0:128], [(trirev16[:], la[:, :])])], reads=[la, trirev16])
                P.op("scalar", lambda e, prv=prv: e.activation(out=erev[:], in_=prv[0:16, 0:128], func=AF.Exp), reads=[prv], writes=[erev])
                pq = psn()
                mmgroup(pq, [(pq[:, 0:16], [(wq[:, k, 0:128], xT[:, k, T:T + NS]) for k in range(8)])], reads=[wq, xT])
                P.op("vector", lambda e, pq=pq: e.scalar_tensor_tensor(out=qeT[:], in0=pq[:, 0:16], scalar=SC, in1=eb[:], op0=ALU.mult, op1=ALU.mult), reads=[pq, eb], writes=[qeT])
                P.op("vector", lambda e: e.tensor_tensor(out=qeTm[:], in0=colmask[:], in1=qeT[:].unsqueeze(1).to_broadcast([128, 4, 16]), op=ALU.mult), reads=[qeT, colmask], writes=[qeTm])
                pk = psn()
                mmgroup(pk, [(pk[:, 0:16], [(wk[:, k, 0:128], xT[:, k, T:T + NS]) for k in range(8)])], reads=[wk, xT])
                P.op("vector", lambda e, pk=pk: e.tensor_tensor(out=keT[:], in0=pk[:, 0:16], in1=enb[:], op=ALU.mult), reads=[pk, enb], writes=[keT])
                P.op("vector", lambda e, h=h: e.tensor_tensor(out=kdf[:], in0=psall[:, COL_KB + h * 128:COL_KB + (h + 1) * 128], in1=erev[:], op=ALU.mult), reads=[psall, erev], writes=[kdf])
                for b in range(4):
                    P.op("vector", lambda e, b=b: e.tensor_scalar(out=kdm[:, b, :], in0=kdf[:], scalar1=rowmask[:, b:b + 1], scalar2=None, op0=ALU.mult), reads=[kdf, rowmask], writes=[kdm], partial=(b > 0))
                P.op("vector", lambda e, h=h: e.tensor_copy(out=vb_[:], in_=psall[:, COL_VB + h * 256:COL_VB + (h + 1) * 256]), reads=[psall], writes=[vb_])
                pa = psn()
                mmgroup(pa, [(pa[0:16, 0:16], [(keT[:, :], qeT[:, :])])], reads=[keT, qeT])
                P.op("vector", lambda e, pa=pa: e.tensor_tensor(out=attm[:], in0=pa[0:16, 0:16], in1=mask16[:], op=ALU.mult), reads=[pa, mask16], writes=[attm])
                po = psn()
                mmgroup(po, [(po[0:16, 0:256], [(attm[:, :], vb_[:, :])] + [(qeTm[:, b, :], S0b[b][:]) for b in range(4)])], reads=[attm, vb_, qeTm] + S0b)
                for b in range(4):
                    pS = psn()
                    mmgroup(pS, [(pS[:, 0:256], [(kdm[:, b, :], vb_[:, :])])], reads=[kdm, vb_])
                    sn = Sn[b % 2]
                    P.op("vector", lambda e, b=b, pS=pS, sn=sn: e.scalar_tensor_tensor(out=sn[:], in0=S0[b][:], scalar=eb[:, 4 * b + 3:4 * b + 4], in1=pS[:, 0:256], op0=ALU.mult, op1=ALU.add), reads=[S0[b], eb, pS], writes=[sn])
                    P.dma("gpsimd", [lambda e, b=b, h=h, sn=sn: e.dma_start(out=st_s.ap()[b, h], in_=sn[:])], sn, reads=[sn], writes=[STS], is_output=True, partial=True)
                P.op("scalar", lambda e, h=h: e.activation(out=srs[:], in_=psall[:, COL_RB + h * 256:COL_RB + (h + 1) * 256], func=AF.Silu), reads=[psall], writes=[srs])
                P.op("gpsimd", lambda e: e.memset(ss[:], 0.0), writes=[ss])
                P.op("scalar", lambda e, po=po: e.activation(out=osq[:], in_=po[0:16, 0:256], func=AF.Square, accum_out=ss[:, 0:1]), reads=[po, ss], writes=[osq, ss])
                P.op("vector", lambda e: e.tensor_scalar(out=ss[:, 1:2], in0=ss[:, 0:1], scalar1=1.0 / 256.0, scalar2=1e-5, op0=ALU.mult, op1=ALU.add), reads=[ss], writes=[ss])
                P.op("scalar", lambda e: e.sqrt(out=ss[:, 1:2], in_=ss[:, 1:2]), reads=[ss], writes=[ss])
                P.op("vector", lambda e: e.reciprocal(out=ss[:, 1:2], in_=ss[:, 1:2]), reads=[ss], writes=[ss])
                P.op("vector", lambda e, po=po, h=h: e.scalar_tensor_tensor(out=ot[:], in0=po[0:16, 0:256], scalar=ss[:, 1:2], in1=Gb[:, h * 256:(h + 1) * 256], op0=ALU.mult, op1=ALU.mult), reads=[po, ss, Gb], writes=[ot])
                P.op("vector", lambda e: e.tensor_tensor(out=oo[:], in0=ot[:], in1=srs[:], op=ALU.mult), reads=[ot, srs], writes=[oo])
                P.dma("gpsimd", [lambda e, h=h: e.dma_start(out=OB_s.ap()[:, h * 256:(h + 1) * 256], in_=oo[:])], oo, reads=[oo], writes=[UB_all], partial=True)
            P.pop()

        P.push()
        alloc_wbf("A")
        tst = [P.sb("tst%d" % i, [128, 2048], F32) for i in range(2)]
        tbf = [P.sb("tbf%d" % i, [128, 2048], BF16) for i in range(2)]

        def prepass_gen():
            ti = 0
            for (src_t, c0_) in ((peer_u, 0), (peer_v, 1024)):
                sv_ = src_t.ap().rearrange("(i p j) d -> i p (j d)", p=128, j=2)
                dv_ = UV_d.ap()[:, c0_:c0_ + 1024].rearrange("(i p j) d -> i p j d", p=128, j=2)
                for i in range(64):
                    a, b = tst[ti % 2], tbf[ti % 2]
                    P.dma("sync", [lambda e, a=a, i=i, sv_=sv_: e.dma_start(out=a[:], in_=sv_[i])], a, writes=[a])
                    if ti % 2 == 0:
                        P.op("vector", lambda e, a=a, b=b: e.tensor_copy(out=b[:], in_=a[:]), reads=[a], writes=[b])
                    else:
                        P.op("scalar", lambda e, a=a, b=b: e.copy(out=b[:], in_=a[:]), reads=[a], writes=[b])
                    P.dma("gpsimd", [lambda e, b=b, i=i, dv_=dv_: e.dma_start(out=dv_[i], in_=b[:].rearrange("p (j d) -> p j d", j=2))], b, reads=[b], writes=[TBL], partial=True)
                    ti += 1
                    yield

        prepass = prepass_gen()

        def prepass_step():
            try:
                next(prepass)
            except StopIteration:
                pass
        QT = P.sb("QT", [128, 2, T], BF16)
        KT = P.sb("KT", [128, 2, T], BF16)
        Vb = P.sb("Vb", [128, 32, 4, 72], BF16)
        Hk = P.sb("Hk", [128, 8, 2, 128], F32)
        BT = P.sb("BT", [128, 8, 2, 128], F32)
        Sf = [P.sb("Sf%d" % i, [128, 2, 2, 128], F32) for i in range(2)]
        PT = [P.sb("PT%d" % i, [128, 2, 2, 128], BF16) for i in range(4)]
        Ost = [P.sb("Ost%d" % i, [128, 260], F32) for i in range(2)]
        KVst = [P.sb("KVst%d" % i, [128, 2, 256], F32) for i in range(2)]
        P.op("vector", lambda e: e.memset(Vb[:], 1.0), writes=[Vb])
        Ub = [UB_all for g in range(3)]
        KVo = [P.wrap("kvo%d" % g, kv_p[g]) for g in range(3)]
        cnt = [0]

        for g in range(3):
            dil = A_DIL[g]
            span = 128 * dil
            nspan = T // span
            hfn = []
            for h8 in range(8):
                hsrc = bass.AP(vec_d, (g * 8 + h8) * 384, [[1, 128], [128, 2], [1, 128]])
                hfn.append(lambda e, hsrc=hsrc, h8=h8: e.dma_start(out=Hk[:, h8, :, :], in_=hsrc))
            P.dma("sync", hfn, Hk, reads=[VEC], writes=[Hk])
            if STOP == 2:
                break
            Hkf = Hk[:].rearrange("p h a q -> p (h a q)")
            BTf = BT[:].rearrange("p h a q -> p (h a q)")
            for cc in range(4):
                pb = psn()
                mmgroup(pb, [(pb[:, :], [(flip[:], Hkf[:, cc * 512:(cc + 1) * 512])])], reads=[flip, Hk])
                evac(BT, BTf[:, cc * 512:(cc + 1) * 512], pb, pb[:, :], partial=(cc > 0))

            def tok_slice(blk):
                s, r = blk // dil, blk % dil
                base = s * span + r
                return slice(base, base + 127 * dil + 1, dil) if dil > 1 else slice(base, base + 128)

            for hh in range(2):
                wq = get_w(COL_QA + g * 512 + hh * 256)
                wk = get_w(COL_KA + g * 512 + hh * 256)
                wv = get_w(COL_VA + g * 512 + hh * 256)
                for (dst, w, sc) in ((QT, wq, 0.125), (KT, wk, None)):
                    for c in range(2):
                        for tc in range(T // 512):
                            pb = psn()
                            mmgroup(pb, [(pb[:, :], [(w[:, k, c * 128:(c + 1) * 128], xT[:, k, tc * 512:(tc + 1) * 512]) for k in range(8)])], reads=[w, xT])
                            evac(dst, dst[:, c, tc * 512:(tc + 1) * 512], pb, pb[:, :], scale=sc, partial=True)
                if STOP == 3:
                    break
                for blk in range(32):
                    ts_ = tok_slice(blk)
                    pb = psn()
                    mmgroup(pb, [(pb[:, 0:256], [(xT[:, k, ts_], wv[:, k, :]) for k in range(8)])], reads=[wv, xT])
                    evac(Vb, Vb[:, blk, :, 0:64], pb, pb[:, 0:256].rearrange("p (h d) -> p h d", h=4), partial=True)
                    s, r = blk // dil, blk % dil
                    if s == nspan - 1:
                        kst = KVst[cnt[0] % 2]
                        cnt[0] += 1
                        pk = psn()
                        mmgroup(pk, [(pk[:, 0:256], [(xT[:, k, ts_], wk[:, k, :]) for k in range(8)])], reads=[wk, xT])
                        evac(kst, kst[:, 0, :], pk, pk[:, 0:256])
                        evac(kst, kst[:, 1, :], pb, pb[:, 0:256], partial=True)
                        dst = bass.AP(kv_p[g], r * 1024 + hh * 256, [[dil * 1024, 128], [512, 2], [1, 256]])
                        P.dma("gpsimd", [lambda e, dst=dst, kst=kst: e.dma_start(out=dst, in_=kst[:])], kst, reads=[kst], writes=[KVo[g]], is_output=True, partial=True)
                if STOP == 4:
                    break
                for blk in range(32 if STOP < 5 else 2):
                    prepass_step()
                    s, r = blk // dil, blk % dil
                    tq = tok_slice(blk)
                    tp = tok_slice(blk - dil) if s > 0 else None
                    po = psn()
                    pts = []
                    na = 2 if s > 0 else 1
                    for j in range(2):
                        pb = psn()
                        pbv = pb[:].rearrange("p (c a q) -> p c a q", c=2, a=2)
                        pr = slice(j * 64, (j + 1) * 64)
                        specs = []
                        for c in range(2):
                            specs.append((pbv[:, c, 0, :], [(KT[pr, c, tq], QT[pr, c, tq])]))
                            if s > 0:
                                specs.append((pbv[:, c, 1, :], [(KT[pr, c, tp], QT[pr, c, tq])]))
                        mmgroup(pb, specs, reads=[KT, QT])
                        sf = Sf[cnt[0] % 2]
                        pt = PT[cnt[0] % 4]
                        cnt[0] += 1
                        hsl = slice(hh * 4 + j, hh * 4 + j + 3, 2)
                        P.op("vector", lambda e, sf=sf, pbv=pbv, na=na, hsl=hsl: e.tensor_tensor(out=sf[:, :, 0:na, :], in0=pbv[:, :, 0:na, :], in1=BT[:, hsl, 0:na, :], op=ALU.add), reads=[pb, BT], writes=[sf])
                        P.op("scalar", lambda e, sf=sf, pt=pt, na=na: e.activation(out=pt[:, :, 0:na, :], in_=sf[:, :, 0:na, :], func=AF.Exp), reads=[sf], writes=[pt])
                        pts.append(pt)
                    pov = po[:, 0:288].rearrange("p (h d) -> p h d", h=4)
                    specs = []
                    for hl in range(4):
                        c, j = hl // 2, hl % 2
                        pt = pts[j]
                        items = [(pt[:, c, 0, :], Vb[:, blk, hl, :])]
                        if s > 0:
                            items.append((pt[:, c, 1, :], Vb[:, blk - dil, hl, :]))
                        specs.append((pov[:, hl, :], items))
                    mmgroup(po, specs, reads=[pts[0], pts[1], Vb])
                    ost = Ost[cnt[0] % 2]
                    evac(ost, ost[:, :].rearrange("p (h d) -> p h d", h=4), po, pov[:, :, 0:65])
                    udst = bass.AP(U_d[g], (s * span + r) * 520 + hh * 260, [[dil * 520, 128], [1, 260]])
                    P.dma("gpsimd", [lambda e, udst=udst, ost=ost: e.dma_start(out=udst, in_=ost[:])], ost, reads=[ost], writes=[Ub[g]], is_output=debug, partial=True)

            if STOP >= 2:
                break
        for _ in prepass:
            pass
        P.pop()
        if STOP == 0:
            phase_b()
            if SAMPLE:
                phase_s()
            phase_c()
        P.pop()
        if STOP == 0:
            phase_d()
        P.finish()
        print("instructions:", P.ninstr, "sems:", P.nsem)
    return nc


def core_inputs(inp, c, consts=None):
    if consts is None:
        consts = make_consts()
    g = lambda k: np.asarray(inp[k])
    m = dict(x_p=g("x_prompt")[c], x_s=g("x_sample")[4 * c:4 * c + 4].reshape(NS, D),
             rel_bias=g("rel_bias"), w_in=g("w_in")[0], w_gla_gate2=g("w_gla_gate2")[0],
             b_gla_gate=g("b_gla_gate"), g_gla_norm=g("g_gla_norm"),
             w_branch_a=g("w_branch_a")[0], w_branch_b=g("w_branch_b")[0], w_out=g("w_out")[0],
             ln1_g=g("ln1_g"), ln1_b=g("ln1_b"), ln2_g=g("ln2_g"), ln2_b=g("ln2_b"),
             w_peer_query=g("w_peer_query")[0], peer_sub_keys=g("peer_sub_keys")[0],
             peer_u=g("peer_u")[0], peer_v=g("peer_v")[0],
             cache1=g("cache_kv_a1")[0, 4 * c:4 * c + 4].reshape(4, 128, 1024),
             cache2=g("cache_kv_a2")[0, 4 * c:4 * c + 4].reshape(4, 512, 1024),
             cache3=g("cache_kv_a3")[0, 4 * c:4 * c + 4].reshape(4, 2048, 1024),
             state=g("state_gla")[0, 4 * c:4 * c + 4])
    m.update(consts)
    return m


_CACHE = {}


def kernel(**inputs):
    if "nc" not in _CACHE:
        _CACHE["nc"] = build_program()
        _CACHE["consts"] = make_consts()
    nc = _CACHE["nc"]
    consts = _CACHE["consts"]
    inp = {k: np.asarray(v) for k, v in inputs.items()}
    maps = [core_inputs(inp, c, consts) for c in range(NCORES)]
    res = run_bass_kernel_spmd(nc, maps, core_ids=list(range(NCORES)))
    R = res.results
    f = np.float32
    y_p = np.stack([R[c]["y_p"] for c in range(NCORES)]).astype(f)
    y_s = np.concatenate([R[c]["y_s"].reshape(4, 4, D) for c in range(NCORES)]).astype(f)
    outs = [y_p, y_s]
    for g in range(3):
        wb = 128 * A_DIL[g]
        outs.append(np.stack([R[c]["kv%d_p" % (g + 1)].reshape(wb, 2, 8, 64) for c in range(NCORES)])[None].astype(f))
    outs.append(np.stack([R[c]["st_p"] for c in range(NCORES)])[None].astype(f))
    for g in range(3):
        wb = 128 * A_DIL[g]
        outs.append(np.concatenate([R[c]["kv%d_s" % (g + 1)].reshape(4, wb, 2, 8, 64) for c in range(NCORES)])[None].astype(f))
    outs.append(np.concatenate([R[c]["st_s"] for c in range(NCORES)])[None].astype(f))
    return tuple(outs)
```

```python
import math
from contextlib import ExitStack
import numpy as np
import concourse.bass as bass
import concourse.mybir as mybir
from concourse.bass_utils import run_bass_kernel_spmd

F32 = mybir.dt.float32
BF16 = mybir.dt.bfloat16
I32 = mybir.dt.int32
AF = mybir.ActivationFunctionType
ALU = mybir.AluOpType
AX = mybir.AxisListType

NCORES = 8
T = 4096
NS = 16
D = 1024
NEG = -30000.0


class Buf:
    __slots__ = ("name", "t", "base_w", "part_w", "readers", "dsem", "dcount")

    def __init__(self, name, t):
        self.name = name
        self.t = t
        self.base_w = {}
        self.part_w = {}
        self.readers = {}
        self.dsem = None
        self.dcount = 0

    def __getitem__(self, idx):
        return self.t[idx]


class Eng:
    def __init__(self, name, sem):
        self.name = name
        self.sem = sem
        self.count = 0
        self.waited = {}
        self.ops = []


class Prog:
    def __init__(self, nc, stack):
        self.nc = nc
        self.stack = stack
        self.engs = {}
        for n in ("sync", "scalar", "gpsimd", "vector", "tensor"):
            s = stack.enter_context(nc.semaphore("e_" + n))
            self.engs[n] = Eng(n, s)
        self.out_toks = []
        self.ninstr = 0
        self.nsem = 5
        self.root = stack
        self.dbufs = []

    def push(self):
        if not hasattr(self, "stk"):
            self.stk = []
        self.stk.append(self.stack)
        self.stack = ExitStack()
        self.stack.__enter__()

    def pop(self):
        self.barrier()
        self.stack.__exit__(None, None, None)
        self.stack = self.stk.pop()

    def barrier(self):
        fin = {}
        for e in self.engs.values():
            if e.count:
                fin[e.sem] = e.count
        for b in self.dbufs:
            fin[b.dsem] = b.dcount
        for e in self.engs.values():
            for s, v in fin.items():
                if e.waited.get(s, 0) < v:
                    e.waited[s] = v
                    e.ops.append(("w", s, v))

    def sb(self, name, shape, dt):
        t = self.stack.enter_context(self.nc.sbuf_tensor("s_" + name, list(shape), dt))
        return Buf(name, t)

    def ps(self, name, shape, dt=F32):
        t = self.root.enter_context(self.nc.psum_tensor("p_" + name, list(shape), dt))
        return Buf(name, t)

    def wrap(self, name, t):
        return Buf(name, t)

    def _need(self, eng, reads, writes, partial):
        need = {}
        for r in reads:
            for dd in (r.base_w, r.part_w):
                for s, v in dd.items():
                    if need.get(s, 0) < v:
                        need[s] = v
        for w in writes:
            dds = (w.base_w, w.readers) if partial else (w.base_w, w.part_w, w.readers)
            for dd in dds:
                for s, v in dd.items():
                    if need.get(s, 0) < v:
                        need[s] = v
        for s, v in need.items():
            if eng.waited.get(s, 0) < v:
                eng.waited[s] = v
                eng.ops.append(("w", s, v))

    def _commit(self, tok, reads, writes, partial):
        s, v = tok
        for r in reads:
            if r.readers.get(s, 0) < v:
                r.readers[s] = v
        for w in writes:
            if partial:
                if w.part_w.get(s, 0) < v:
                    w.part_w[s] = v
            else:
                w.base_w = {s: v}
                w.part_w = {}
                w.readers = {}

    def op(self, ename, fn, reads=(), writes=(), partial=False):
        eng = self.engs[ename]
        self._need(eng, reads, writes, partial)
        eng.count += 1
        eng.ops.append(("i", fn))
        tok = (eng.sem, eng.count)
        self._commit(tok, reads, writes, partial)
        self.ninstr += 1
        return tok

    def group(self, ename, fns, reads=(), writes=(), partial=False):
        eng = self.engs[ename]
        self._need(eng, reads, writes, partial)
        for f in fns[:-1]:
            eng.ops.append(("n", f))
        eng.count += 1
        eng.ops.append(("i", fns[-1]))
        tok = (eng.sem, eng.count)
        self._commit(tok, reads, writes, partial)
        self.ninstr += len(fns)
        return tok

    def dma(self, ename, fns, owner, reads=(), writes=(), is_output=False, partial=False):
        eng = self.engs[ename]
        if owner.dsem is None:
            owner.dsem = self.root.enter_context(self.nc.semaphore("d_" + owner.name))
            self.nsem += 1
            self.dbufs.append(owner)
        self._need(eng, reads, writes, partial)
        for f in fns:
            owner.dcount += 16
            eng.ops.append(("d", f, owner.dsem))
        tok = (owner.dsem, owner.dcount)
        self._commit(tok, reads, writes, partial)
        if is_output:
            self.out_toks.append(tok)
        self.ninstr += len(fns)
        return tok

    def finish(self):
        eng = self.engs["sync"]
        fin = {}
        for e in self.engs.values():
            if e.count:
                fin[e.sem] = e.count
        for (s, v) in self.out_toks:
            if fin.get(s, 0) < v:
                fin[s] = v
        for s, v in fin.items():
            if eng.waited.get(s, 0) < v and s is not eng.sem:
                eng.ops.append(("w", s, v))
        engs = self.engs

        def replay(e, ne):
            for o in e.ops:
                k = o[0]
                if k == "w":
                    ne.wait_ge(o[1], o[2])
                elif k == "i":
                    o[1](ne).then_inc(e.sem, 1)
                elif k == "n":
                    o[1](ne)
                else:
                    o[1](ne).then_inc(o[2], 16)

        with self.nc.Block() as block:
            @block.sync
            def _(x):
                replay(engs["sync"], x)

            @block.scalar
            def _(x):
                replay(engs["scalar"], x)

            @block.gpsimd
            def _(x):
                replay(engs["gpsimd"], x)

            @block.vector
            def _(x):
                replay(engs["vector"], x)

            @block.tensor
            def _(x):
                replay(engs["tensor"], x)


A_DIL = (1, 4, 16)
COL_QA, COL_KA, COL_VA = 0, 1536, 3072
COL_QB, COL_KB, COL_VB = 4608, 5120, 5632
COL_AB, COL_RB, COL_GA, COL_GB = 6656, 6672, 7696, 8720


def _rel_bucket(dist):
    exact = 16
    d = np.maximum(dist, 1).astype(np.float32)
    large = exact + (np.log(d / np.float32(exact)) / np.float32(math.log(2048 / exact)) * np.float32(32 - exact)).astype(np.int32)
    large = np.minimum(large, 31)
    return np.where(dist < exact, dist, large)


def make_consts():
    c = {}
    c["ident"] = np.eye(128, dtype=np.float32)
    c["flip"] = np.ascontiguousarray(np.eye(128, dtype=np.float32)[::-1])
    oh = np.zeros((3, 33, 384), np.float32)
    for g, dil in enumerate(A_DIL):
        for m in range(384):
            rel = m - 127
            if 0 <= rel <= 128:
                oh[g, int(_rel_bucket(np.int32(rel * dil))), m] = 1.0
            else:
                oh[g, 32, m] = 1.0
    c["oh_bias"] = oh.transpose(1, 0, 2).copy()
    j = np.arange(128)[:, None]
    i = np.arange(128)[None, :]
    same = (j // 64) == (i // 64)
    c["tri_c"] = np.where(same & (j <= i), -1.0 / 16.0, 0.0).astype(np.float32)
    c["tri_rev"] = np.where(same & (j > i), -1.0 / 16.0, 0.0).astype(np.float32)
    i64 = np.arange(64)[None, :]
    c["mask_t"] = ((j % 64) <= i64).astype(np.float32)
    c["iota_c"] = np.tile(np.arange(256, dtype=np.int32)[None, :], (128, 1))
    c["e_sel"] = np.tile(np.eye(16, dtype=np.float32)[None], (128, 1, 1))
    c["sel16"] = np.tile(np.eye(16, dtype=np.float32)[:, :, None], (1, 1, 128))
    ohs = np.zeros((32, 3, 128), np.float32)
    for g, dil in enumerate(A_DIL):
        for p in range(128):
            ohs[int(_rel_bucket(np.int32((128 - p) * dil))), g, p] = 1.0
    c["ohs"] = ohs
    j16 = np.arange(16)[:, None]
    i16 = np.arange(16)[None, :]
    same16 = (j16 // 4) == (i16 // 4)
    c["tri_c16"] = np.where(same16 & (j16 <= i16), -1.0 / 16.0, 0.0).astype(np.float32)
    c["tri_rev16"] = np.where(same16 & (j16 > i16), -1.0 / 16.0, 0.0).astype(np.float32)
    c["mask16"] = (same16 & (j16 <= i16)).astype(np.float32)
    c["colmask"] = np.tile(((np.arange(16)[None, :] // 4) == np.arange(4)[:, None]).astype(np.float32)[None], (128, 1, 1))
    c["rowmask"] = ((np.arange(16)[:, None] // 4) == np.arange(4)[None, :]).astype(np.float32)
    return c


def build_program(debug=False, STOP=0, SAMPLE=True, NTILES_D=0, TILES_D=None):
    nc = bass.Bass("TRN2", target_bir_lowering=False)

    def din(name, shape, dt=F32):
        return nc.dram_tensor(name, list(shape), dt, kind="ExternalInput")

    def dout(name, shape, dt=F32):
        return nc.dram_tensor(name, list(shape), dt, kind="ExternalOutput")

    x_p = din("x_p", [T, D])
    x_s = din("x_s", [NS, D])
    rel_bias = din("rel_bias", [32, 24])
    w_in = din("w_in", [D, 9744])
    c_ident = din("ident", [128, 128])
    c_flip = din("flip", [128, 128])
    c_oh = din("oh_bias", [33, 3, 384])
    c_tric = din("tri_c", [128, 128])
    c_trirev = din("tri_rev", [128, 128])
    c_maskt = din("mask_t", [128, 64])
    w_g2 = din("w_gla_gate2", [16, 512])
    b_g = din("b_gla_gate", [1, 512])
    g_norm = din("g_gla_norm", [1, 1024])
    st_p = dout("st_p", [4, 128, 256])
    w_ba = din("w_branch_a", [512, 1024])
    w_bb = din("w_branch_b", [1024, 1024])
    w_o = din("w_out", [1024, 1024])
    ln1_g = din("ln1_g", [1, 1024])
    ln1_b = din("ln1_b", [1, 1024])
    ln2_g = din("ln2_g", [1, 1024])
    ln2_b = din("ln2_b", [1, 1024])
    w_pq = din("w_peer_query", [1024, 2048])
    sub_keys = din("peer_sub_keys", [8, 2, 128, 128])
    peer_u = din("peer_u", [16384, 1024])
    peer_v = din("peer_v", [16384, 1024])
    c_iota = din("iota_c", [128, 256], I32)
    c_esel = din("e_sel", [128, 16, 16])
    c_sel16 = din("sel16", [16, 16, 128])
    c_ohs = din("ohs", [32, 3, 128])
    c_tric16 = din("tri_c16", [16, 16])
    c_trirev16 = din("tri_rev16", [16, 16])
    c_mask16 = din("mask16", [16, 16])
    c_colmask = din("colmask", [128, 4, 16])
    c_rowmask = din("rowmask", [16, 4])
    WBS = (128, 512, 2048)
    cache = [din("cache%d" % (g + 1), [4, WBS[g], 1024]) for g in range(3)]
    state_in = din("state", [4, 4, 128, 256])
    kv_s = [dout("kv%d_s" % (g + 1), [4, WBS[g], 1024]) for g in range(3)]
    st_s = dout("st_s", [4, 4, 128, 256])
    ext_d = nc.dram_tensor("ext_d", [4, 132, 1024], F32, kind="Internal")
    y_p = dout("y_p", [T, D])
    y_s = dout("y_s", [NS, D])
    if debug:
        X1_d = dout("X1", [T + NS, D])
    else:
        X1_d = nc.dram_tensor("X1", [T + NS, D], F32, kind="Internal")
    UV_d = nc.dram_tensor("UV_d", [16384, 2048], BF16, kind="Internal")
    U_s = nc.dram_tensor("U_s", [NS, 520], F32, kind="Internal")
    OB_s = nc.dram_tensor("OB_s", [NS, 1024], F32, kind="Internal")
    if debug:
        OB_d = dout("OB", [T, 1024])
    else:
        OB_d = nc.dram_tensor("OB", [T, 1024], F32, kind="Internal")

    kv_p = [dout("kv%d_p" % (g + 1), [128 * A_DIL[g], 1024]) for g in range(3)]
    if debug:
        U_d = [dout("U%d" % g, [T, 520]) for g in range(3)]
    else:
        U_d = [nc.dram_tensor("U%d" % g, [T, 520], F32, kind="Internal") for g in range(3)]
    vec_d = nc.dram_tensor("vec_d", [24, 384], F32, kind="Internal")

    with ExitStack() as st:
        P = Prog(nc, st)
        ident = P.sb("ident", [128, 128], F32)
        identb = P.sb("identb", [128, 128], BF16)
        flip = P.sb("flip", [128, 128], F32)
        wst = [P.sb("wst%d" % i, [128, 8, 256], F32) for i in range(2)]
        P.push()
        xT = P.sb("xT", [128, 8, T + NS], BF16)
        TBL = P.wrap("TBL", None)
        P.push()
        rext = P.sb("rext", [33, 24], F32)
        oh = P.sb("oh", [33, 3, 384], F32)
        PS = [P.ps("b%d" % i, [128, 512], F32) for i in range(8)]
        psi = [0]

        ps_reserved = set()

        def psn():
            while True:
                b = PS[psi[0] % 8]
                psi[0] += 1
                if b.name not in ps_reserved:
                    return b

        ev = [0]

        def evac(out_buf, out_ap, in_buf, in_ap, scale=None, partial=False, eng=None):
            if eng is None:
                eng = "vector" if ev[0] % 2 == 0 else "scalar"
                ev[0] += 1
            if eng == "vector":
                if scale is None:
                    P.op("vector", lambda e: e.tensor_copy(out=out_ap, in_=in_ap), reads=[in_buf], writes=[out_buf], partial=partial)
                else:
                    P.op("vector", lambda e: e.tensor_single_scalar(out=out_ap, in_=in_ap, scalar=scale, op=ALU.mult), reads=[in_buf], writes=[out_buf], partial=partial)
            else:
                if scale is None:
                    P.op("scalar", lambda e: e.copy(out=out_ap, in_=in_ap), reads=[in_buf], writes=[out_buf], partial=partial)
                else:
                    P.op("scalar", lambda e: e.mul(out=out_ap, in_=in_ap, mul=scale), reads=[in_buf], writes=[out_buf], partial=partial)

        def mmgroup(out_buf, specs, reads):
            fns = []
            for out_ap, items in specs:
                n = len(items)
                for i, (l, r) in enumerate(items):
                    fns.append(lambda e, o=out_ap, l=l, r=r, a=(i == 0), b=(i == n - 1): e.matmul(o, lhsT=l, rhs=r, start=a, stop=b))
            P.group("tensor", fns, reads=reads, writes=[out_buf])

        P.dma("sync", [lambda e: e.dma_start(out=ident[:], in_=c_ident.ap())], ident, writes=[ident])
        P.dma("sync", [lambda e: e.dma_start(out=flip[:], in_=c_flip.ap())], flip, writes=[flip])
        P.dma("sync", [lambda e: e.dma_start(out=oh[:], in_=c_oh.ap())], oh, writes=[oh])
        P.op("vector", lambda e: e.memset(rext[:], NEG), writes=[rext])
        P.dma("sync", [lambda e: e.dma_start(out=rext[0:32, :], in_=rel_bias.ap())], rext, writes=[rext])
        P.op("vector", lambda e: e.tensor_copy(out=identb[:], in_=ident[:]), reads=[ident], writes=[identb])

        vecs = P.sb("vecs", [8, 3, 384], F32)
        VEC = P.wrap("vec_d", vec_d)
        for g in range(3):
            pb = psn()
            mmgroup(pb, [(pb[0:8, 0:384], [(rext[:, g * 8:(g + 1) * 8], oh[:, g, :])])], reads=[rext, oh])
            evac(vecs, vecs[:, g, :], pb, pb[0:8, 0:384], partial=True, eng="vector")
        P.dma("gpsimd", [lambda e: e.dma_start(out=vec_d.ap().rearrange("(g h) m -> h g m", g=3), in_=vecs[:])], vecs, reads=[vecs], writes=[VEC])

        xin = [P.sb("xin%d" % i, [128, D], F32) for i in range(2)]
        xbf = [P.sb("xbf%d" % i, [128, D], BF16) for i in range(2)]
        NTILE = T // 128
        for n in range(NTILE + 1):
            xi = xin[n % 2]
            xb = xbf[n % 2]
            rows = 128 if n < NTILE else NS
            src = x_p.ap()[n * 128:(n + 1) * 128, :] if n < NTILE else x_s.ap()
            P.dma("sync", [lambda e, xi=xi, src=src, rows=rows: e.dma_start(out=xi[0:rows, :], in_=src)], xi, writes=[xi])
            P.op("gpsimd", lambda e, xi=xi, xb=xb, rows=rows: e.tensor_copy(out=xb[0:rows, :], in_=xi[0:rows, :]), reads=[xi], writes=[xb])
            pb = psn()
            pbv = pb[:].bitcast(BF16)
            fns = []
            for k in range(8):
                fns.append(lambda e, k=k, xb=xb, pbv=pbv, rows=rows: e.transpose(out=pbv[:, k * 128:k * 128 + rows], in_=xb[0:rows, k * 128:(k + 1) * 128], identity=identb[0:rows, 0:rows]))
            P.group("tensor", fns, reads=[xb, identb], writes=[pb])
            evac(xT, xT[:, :, n * 128:n * 128 + rows], pb, pbv.rearrange("p (k t) -> p k t", k=8)[:, :, 0:rows], partial=True)

        if STOP == 1:
            P.finish()
            return nc
        P.pop()
        wi = [0, 0]
        w_in_v = w_in.ap().rearrange("(k p) n -> p k n", p=128)

        wbf_ring = [None]
        UB_all = P.wrap("UB_all", None)
        X1ALL = P.wrap("X1ALL", None)

        def alloc_wbf(tag):
            wbf_ring[0] = [P.sb("wbf%s%d" % (tag, i), [128, 8, 256], BF16) for i in range(4)]

        def get_w(c0, ncol=256):
            s = wst[wi[0] % 2]
            wi[0] += 1
            b = wbf_ring[0][wi[1] % 4]
            wi[1] += 1
            P.dma("sync", [lambda e: e.dma_start(out=s[:, :, 0:ncol], in_=w_in_v[:, :, c0:c0 + ncol])], s, writes=[s])
            ce = ("gpsimd", "vector", "scalar")[wi[1] % 3]
            if ce == "scalar":
                P.op("scalar", lambda e: e.copy(out=b[:, :, 0:ncol], in_=s[:, :, 0:ncol]), reads=[s], writes=[b])
            else:
                P.op(ce, lambda e: e.tensor_copy(out=b[:, :, 0:ncol], in_=s[:, :, 0:ncol]), reads=[s], writes=[b])
            return b


        def phase_b():
            P.push()
            alloc_wbf("B")
            tric = P.sb("tric", [128, 128], F32)
            trirev = P.sb("trirev", [128, 128], F32)
            maskt = P.sb("maskt", [128, 64], F32)
            w2e = P.sb("w2e", [17, 512], F32)
            abT = P.sb("abT", [17, T], F32)
            Gb = P.sb("Gb", [64, 1024], F32)
            P.dma("sync", [lambda e: e.dma_start(out=tric[:], in_=c_tric.ap())], tric, writes=[tric])
            P.dma("sync", [lambda e: e.dma_start(out=trirev[:], in_=c_trirev.ap())], trirev, writes=[trirev])
            P.dma("sync", [lambda e: e.dma_start(out=maskt[:], in_=c_maskt.ap())], maskt, writes=[maskt])
            P.dma("sync", [lambda e: e.dma_start(out=w2e[0:16, :], in_=w_g2.ap()), lambda e: e.dma_start(out=w2e[16:17, :], in_=b_g.ap())], w2e, writes=[w2e])
            P.dma("sync", [lambda e: e.dma_start(out=Gb[:], in_=g_norm.ap().partition_broadcast(64))], Gb, writes=[Gb])
            P.op("vector", lambda e: e.memset(abT[:], 1.0), writes=[abT])
            wab = get_w(COL_AB, 16)
            for tc in range(T // 512):
                pb = psn()
                mmgroup(pb, [(pb[0:16, :], [(wab[:, k, 0:16], xT[:, k, tc * 512:(tc + 1) * 512]) for k in range(8)])], reads=[wab, xT])
                evac(abT, abT[0:16, tc * 512:(tc + 1) * 512], pb, pb[0:16, :], partial=(tc > 0))
            qeT = P.sb("qeT", [128, T], BF16)
            keT = P.sb("keT", [128, T], BF16)
            kd = P.sb("kd", [128, T // 128, 128], BF16)
            vv = P.sb("vv", [128, T // 128, 256], BF16)
            dec = P.sb("dec", [128, T // 64], F32)
            t1 = [P.sb("t1_%d" % i, [128, 4, 128], F32) for i in range(2)]
            la = [P.sb("la%d" % i, [128, 4, 128], F32) for i in range(2)]
            eb = [P.sb("eb%d" % i, [128, 512], F32) for i in range(2)]
            enb = [P.sb("enb%d" % i, [128, 512], F32) for i in range(2)]
            erev = [P.sb("erev%d" % i, [128, 4, 128], F32) for i in range(2)]
            Sst = P.sb("Sst", [128, 256], F32)
            Sbf = P.sb("Sbf", [128, 256], BF16)
            attm = [P.sb("attm%d" % i, [128, 64], BF16) for i in range(2)]
            sr = [P.sb("sr%d" % i, [64, 256], F32) for i in range(2)]
            osq = P.sb("osq", [64, 256], F32)
            ss = [P.sb("ss%d" % i, [64, 2], F32) for i in range(2)]
            ot = [P.sb("ot%d" % i, [64, 256], F32) for i in range(2)]
            oo = [P.sb("oo%d" % i, [64, 256], F32) for i in range(2)]
            OB = UB_all
            STP = P.wrap("st_p", st_p)
            SC = 128.0 ** -0.5
            for h in range(4):
                wq = get_w(COL_QB + h * 128, 128)
                wk = get_w(COL_KB + h * 128, 128)
                wv = get_w(COL_VB + h * 256, 256)
                wr = get_w(COL_RB + h * 256, 256)
                for tc in range(T // 512):
                    i2 = tc % 2
                    tsl = slice(tc * 512, (tc + 1) * 512)
                    pz = psn()
                    mmgroup(pz, [(pz[:, i * 128:(i + 1) * 128], [(abT[:, tc * 512 + i * 128:tc * 512 + (i + 1) * 128], w2e[:, h * 128:(h + 1) * 128])]) for i in range(4)], reads=[abT, w2e])
                    P.op("scalar", lambda e, i2=i2, pz=pz: e.activation(out=t1[i2][:].rearrange("p a d -> p (a d)"), in_=pz[:, :], func=AF.Exp, scale=-1.0), reads=[pz], writes=[t1[i2]])
                    P.op("scalar", lambda e, i2=i2: e.activation(out=la[i2][:].rearrange("p a d -> p (a d)"), in_=t1[i2][:].rearrange("p a d -> p (a d)"), func=AF.Ln, bias=1.0, scale=1.0), reads=[t1[i2]], writes=[la[i2]])
                    pbt = psn()
                    mmgroup(pbt, [(pbt[:, i * 128:(i + 1) * 128], [(la[i2][:, i, :], tric[:])]) for i in range(4)], reads=[la[i2], tric])
                    P.op("scalar", lambda e, i2=i2, pbt=pbt: e.activation(out=eb[i2][:], in_=pbt[:, :], func=AF.Exp), reads=[pbt], writes=[eb[i2]])
                    P.op("scalar", lambda e, i2=i2, pbt=pbt: e.activation(out=enb[i2][:], in_=pbt[:, :], func=AF.Exp, scale=-1.0), reads=[pbt], writes=[enb[i2]])
                    P.op("vector", lambda e, i2=i2, tc=tc: e.tensor_copy(out=dec[:, tc * 8:(tc + 1) * 8], in_=eb[i2][:, 63:512:64]), reads=[eb[i2]], writes=[dec], partial=True)
                    prv = psn()
                    mmgroup(prv, [(prv[:, i * 128:(i + 1) * 128], [(trirev[:], la[i2][:, i, :])]) for i in range(4)], reads=[la[i2], trirev])
                    P.op("scalar", lambda e, i2=i2, prv=prv: e.activation(out=erev[i2][:].rearrange("p a d -> p (a d)"), in_=prv[:, :], func=AF.Exp), reads=[prv], writes=[erev[i2]])
                    pq = psn()
                    mmgroup(pq, [(pq[:, :], [(wq[:, k, 0:128], xT[:, k, tsl]) for k in range(8)])], reads=[wq, xT])
                    P.op("vector", lambda e, i2=i2, pq=pq, tsl=tsl: e.scalar_tensor_tensor(out=qeT[:, tsl], in0=pq[:, :], scalar=SC, in1=eb[i2][:], op0=ALU.mult, op1=ALU.mult), reads=[pq, eb[i2]], writes=[qeT], partial=True)
                    pk = psn()
                    mmgroup(pk, [(pk[:, :], [(wk[:, k, 0:128], xT[:, k, tsl]) for k in range(8)])], reads=[wk, xT])
                    P.op("vector", lambda e, i2=i2, pk=pk, tsl=tsl: e.tensor_tensor(out=keT[:, tsl], in0=pk[:, :], in1=enb[i2][:], op=ALU.mult), reads=[pk, enb[i2]], writes=[keT], partial=True)
                    pkt = psn()
                    mmgroup(pkt, [(pkt[:, i * 128:(i + 1) * 128], [(xT[:, k, tc * 512 + i * 128:tc * 512 + (i + 1) * 128], wk[:, k, 0:128]) for k in range(8)]) for i in range(4)], reads=[wk, xT])
                    P.op("vector", lambda e, i2=i2, pkt=pkt, tc=tc: e.tensor_tensor(out=kd[:, tc * 4:(tc + 1) * 4, :], in0=pkt[:, :].rearrange("p (a d) -> p a d", a=4), in1=erev[i2][:], op=ALU.mult), reads=[pkt, erev[i2]], writes=[kd], partial=True)
                    for i in range(2):
                        pv = psn()
                        mmgroup(pv, [(pv[:, a * 256:(a + 1) * 256], [(xT[:, k, tc * 512 + (i * 2 + a) * 128:tc * 512 + (i * 2 + a + 1) * 128], wv[:, k, 0:256]) for k in range(8)]) for a in range(2)], reads=[wv, xT])
                        evac(vv, vv[:, tc * 4 + i * 2:tc * 4 + i * 2 + 2, :], pv, pv[:, :].rearrange("p (a d) -> p a d", a=2), partial=True)
                P.op("vector", lambda e: e.memset(Sst[:], 0.0), writes=[Sst])
                P.op("vector", lambda e: e.memset(Sbf[:], 0.0), writes=[Sbf])
                for c in range(T // 64):
                    n, half = c // 2, c % 2
                    pr = slice(half * 64, half * 64 + 64)
                    csl = slice(c * 64, c * 64 + 64)
                    i2 = c % 2
                    prb = psn()
                    mmgroup(prb, [(prb[0:64, 0:256], [(xT[:, k, csl], wr[:, k, 0:256]) for k in range(8)])], reads=[wr, xT])
                    P.op("scalar", lambda e, i2=i2, prb=prb: e.activation(out=sr[i2][:], in_=prb[0:64, 0:256], func=AF.Silu), reads=[prb], writes=[sr[i2]])
                    pa = psn()
                    mmgroup(pa, [(pa[:, 0:64], [(keT[:, n * 128:(n + 1) * 128], qeT[:, csl])])], reads=[keT, qeT])
                    P.op("vector", lambda e, i2=i2, pa=pa, pr=pr: e.tensor_tensor(out=attm[i2][pr, :], in0=pa[pr, 0:64], in1=maskt[pr, :], op=ALU.mult), reads=[pa, maskt], writes=[attm[i2]])
                    po = psn()
                    mmgroup(po, [(po[0:64, 0:256], [(attm[i2][pr, :], vv[pr, n, :]), (qeT[:, csl], Sbf[:])])], reads=[attm[i2], vv, qeT, Sbf])
                    pS = psn()
                    mmgroup(pS, [(pS[:, 0:256], [(kd[pr, n, :], vv[pr, n, :])])], reads=[kd, vv])
                    P.op("vector", lambda e, c=c, pS=pS: e.scalar_tensor_tensor(out=Sst[:], in0=Sst[:], scalar=dec[:, c:c + 1], in1=pS[:, 0:256], op0=ALU.mult, op1=ALU.add), reads=[Sst, dec, pS], writes=[Sst])
                    P.op("scalar", lambda e: e.copy(out=Sbf[:], in_=Sst[:]), reads=[Sst], writes=[Sbf])
                    P.op("gpsimd", lambda e, i2=i2: e.memset(ss[i2][:], 0.0), writes=[ss[i2]])
                    P.op("scalar", lambda e, i2=i2, po=po: e.activation(out=osq[:], in_=po[0:64, 0:256], func=AF.Square, accum_out=ss[i2][:, 0:1]), reads=[po], writes=[osq, ss[i2]])
                    P.op("vector", lambda e, i2=i2: e.tensor_scalar(out=ss[i2][:, 1:2], in0=ss[i2][:, 0:1], scalar1=1.0 / 256.0, scalar2=1e-5, op0=ALU.mult, op1=ALU.add), reads=[ss[i2]], writes=[ss[i2]])
                    P.op("scalar", lambda e, i2=i2: e.sqrt(out=ss[i2][:, 1:2], in_=ss[i2][:, 1:2]), reads=[ss[i2]], writes=[ss[i2]])
                    P.op("vector", lambda e, i2=i2: e.reciprocal(out=ss[i2][:, 1:2], in_=ss[i2][:, 1:2]), reads=[ss[i2]], writes=[ss[i2]])
                    P.op("vector", lambda e, i2=i2, po=po, h=h: e.scalar_tensor_tensor(out=ot[i2][:], in0=po[0:64, 0:256], scalar=ss[i2][:, 1:2], in1=Gb[:, h * 256:(h + 1) * 256], op0=ALU.mult, op1=ALU.mult), reads=[po, ss[i2], Gb], writes=[ot[i2]])
                    P.op("gpsimd", lambda e, i2=i2: e.tensor_tensor(out=oo[i2][:], in0=ot[i2][:], in1=sr[i2][:], op=ALU.mult), reads=[ot[i2], sr[i2]], writes=[oo[i2]])
                    P.dma("sync", [lambda e, i2=i2, c=c, h=h: e.dma_start(out=OB_d.ap()[c * 64:(c + 1) * 64, h * 256:(h + 1) * 256], in_=oo[i2][:])], oo[i2], reads=[oo[i2]], writes=[OB], is_output=debug, partial=True)
                P.dma("sync", [lambda e, h=h: e.dma_start(out=st_p.ap()[h], in_=Sst[:])], Sst, reads=[Sst], writes=[STP], is_output=True, partial=True)
            P.pop()


        ALPHA = 2.0 ** 0.25

        def load_wres(dst, src_view, K, ncols):
            for c0 in range(0, ncols, 256):
                st_ = wst[wi[0] % 2]
                wi[0] += 1
                P.dma("sync", [lambda e, st_=st_, c0=c0: e.dma_start(out=st_[:, 0:K, :], in_=src_view[:, :, c0:c0 + 256])], st_, writes=[st_])
                ce = ("gpsimd", "vector", "scalar")[(c0 // 256) % 3]
                if ce == "scalar":
                    P.op("scalar", lambda e, st_=st_, c0=c0: e.copy(out=dst[:, :, c0:c0 + 256], in_=st_[:, 0:K, :]), reads=[st_], writes=[dst], partial=True)
                else:
                    P.op(ce, lambda e, st_=st_, c0=c0: e.tensor_copy(out=dst[:, :, c0:c0 + 256], in_=st_[:, 0:K, :]), reads=[st_], writes=[dst], partial=True)

        def layernorm(rows, y, junk, st4, Gt, Bt, out):
            r = slice(0, rows)
            P.op("vector", lambda e: e.tensor_reduce(out=st4[r, 0:1], in_=y[r, :], axis=AX.X, op=ALU.add), reads=[y], writes=[st4])
            P.op("vector", lambda e: e.tensor_single_scalar(out=st4[r, 1:2], in_=st4[r, 0:1], scalar=-1.0 / 1024.0, op=ALU.mult), reads=[st4], writes=[st4])
            P.op("vector", lambda e: e.tensor_scalar(out=y[r, :], in0=y[r, :], scalar1=st4[r, 1:2], scalar2=None, op0=ALU.add), reads=[y, st4], writes=[y])
            P.op("gpsimd", lambda e: e.memset(st4[r, 2:3], 0.0), reads=[], writes=[st4], partial=True)
            P.op("vector", lambda e: e.scalar_tensor_tensor(out=junk[r, :], in0=y[r, :], scalar=1.0, in1=y[r, :], op0=ALU.mult, op1=ALU.mult, accum_out=st4[r, 2:3]), reads=[y, st4], writes=[junk, st4])
            P.op("vector", lambda e: e.tensor_scalar(out=st4[r, 3:4], in0=st4[r, 2:3], scalar1=1.0 / 1024.0, scalar2=1e-5, op0=ALU.mult, op1=ALU.add), reads=[st4], writes=[st4])
            P.op("scalar", lambda e: e.sqrt(out=st4[r, 3:4], in_=st4[r, 3:4]), reads=[st4], writes=[st4])
            P.op("vector", lambda e: e.reciprocal(out=st4[r, 3:4], in_=st4[r, 3:4]), reads=[st4], writes=[st4])
            P.op("vector", lambda e: e.scalar_tensor_tensor(out=out[r, :], in0=y[r, :], scalar=st4[r, 3:4], in1=Gt[r, :], op0=ALU.mult, op1=ALU.mult), reads=[y, st4, Gt], writes=[out])
            P.op("gpsimd", lambda e: e.tensor_tensor(out=out[r, :], in0=out[r, :], in1=Bt[r, :], op=ALU.add), reads=[out, Bt], writes=[out])

        def transpose_to(dstT, src_bf, rows, nk):
            pb = psn()
            pbv = pb[:].bitcast(BF16)
            fns = []
            for k in range(nk):
                fns.append(lambda e, k=k: e.transpose(out=pbv[:, k * 128:k * 128 + rows], in_=src_bf[0:rows, k * 128:(k + 1) * 128], identity=identb[0:rows, 0:rows]))
            P.group("tensor", fns, reads=[src_bf, identb], writes=[pb])
            evac(dstT, dstT[:, 0:nk, 0:rows], pb, pbv[:, 0:nk * 128].rearrange("p (k t) -> p k t", k=nk)[:, :, 0:rows])

        def phase_c():
            P.push()
            Wa = P.sb("Wa", [128, 4, 1024], BF16)
            Wb = P.sb("Wb", [128, 8, 1024], BF16)
            Wo = P.sb("Wo", [128, 8, 1024], BF16)
            Wga = P.sb("Wga", [128, 8, 1024], BF16)
            Wgb = P.sb("Wgb", [128, 8, 1024], BF16)
            load_wres(Wa, w_ba.ap().rearrange("(k p) n -> p k n", p=128), 4, 1024)
            load_wres(Wb, w_bb.ap().rearrange("(k p) n -> p k n", p=128), 8, 1024)
            load_wres(Wo, w_o.ap().rearrange("(k p) n -> p k n", p=128), 8, 1024)
            load_wres(Wga, w_in_v[:, :, COL_GA:COL_GA + 1024], 8, 1024)
            load_wres(Wgb, w_in_v[:, :, COL_GB:COL_GB + 1024], 8, 1024)
            G1 = P.sb("G1", [128, 1024], F32)
            B1 = P.sb("B1", [128, 1024], F32)
            P.dma("sync", [lambda e: e.dma_start(out=G1[:], in_=ln1_g.ap().partition_broadcast(128))], G1, writes=[G1])
            P.dma("sync", [lambda e: e.dma_start(out=B1[:], in_=ln1_b.ap().partition_broadcast(128))], B1, writes=[B1])
            Ut = [P.sb("Ut%d" % i, [128, 8, 65], F32) for i in range(3)]
            OBt = P.sb("OBt", [128, 1024], F32)
            xt = P.sb("xt", [128, 1024], F32)
            rden = P.sb("rden", [128, 8], F32)
            oab = P.sb("oab", [128, 512], BF16)
            obb = P.sb("obb", [128, 1024], BF16)
            oaT = P.sb("oaT", [128, 4, 128], BF16)
            obT = P.sb("obT", [128, 8, 128], BF16)
            sg = P.sb("sg", [128, 1024], F32)
            mixed = P.sb("mixed", [128, 1024], F32)
            mixb = P.sb("mixb", [128, 1024], BF16)
            mixT = P.sb("mixT", [128, 8, 128], BF16)
            st4 = P.sb("st4c", [128, 4], F32)
            X1 = X1ALL
            Ubufs = [P.wrap("Ux%d" % g, U_d[g]) for g in range(3)]
            def tile_c(n):
                samp = (n == T // 128)
                rows = NS if samp else 128
                r = slice(0, rows)
                tcol = slice(n * 128, n * 128 + rows)
                if samp:
                    usrc = [U_s.ap()]
                    obsrc = OB_s.ap()
                    xsrc = x_s.ap()
                else:
                    usrc = [U_d[g].ap()[n * 128:(n + 1) * 128, :] for g in range(3)]
                    obsrc = OB_d.ap()[n * 128:(n + 1) * 128, :]
                    xsrc = x_p.ap()[n * 128:(n + 1) * 128, :]
                for i, us in enumerate(usrc):
                    P.dma("sync", [lambda e, i=i, us=us: e.dma_start(out=Ut[i][r].rearrange("p h d -> p (h d)"), in_=us)], Ut[i], reads=[UB_all], writes=[Ut[i]])
                P.dma("sync", [lambda e, obsrc=obsrc: e.dma_start(out=OBt[r, :], in_=obsrc)], OBt, reads=[UB_all], writes=[OBt])
                P.dma("sync", [lambda e, xsrc=xsrc: e.dma_start(out=xt[r, :], in_=xsrc)], xt, writes=[xt])
                for i in range(1, len(usrc)):
                    P.op("vector", lambda e, i=i: e.tensor_tensor(out=Ut[0][r], in0=Ut[0][r], in1=Ut[i][r], op=ALU.add), reads=[Ut[0], Ut[i]], writes=[Ut[0]])
                P.op("vector", lambda e: e.reciprocal(out=rden[r, :], in_=Ut[0][r, :, 64]), reads=[Ut[0]], writes=[rden])
                P.op("vector", lambda e: e.tensor_tensor(out=oab[r, :].rearrange("p (h d) -> p h d", h=8), in0=Ut[0][r, :, 0:64], in1=rden[r, :].unsqueeze(2).to_broadcast([rows, 8, 64]), op=ALU.mult), reads=[Ut[0], rden], writes=[oab])
                P.op("gpsimd", lambda e: e.tensor_copy(out=obb[r, :], in_=OBt[r, :]), reads=[OBt], writes=[obb])
                transpose_to(oaT, oab, rows, 4)
                transpose_to(obT, obb, rows, 8)
                for nh in range(2):
                    csl = slice(nh * 512, (nh + 1) * 512)
                    pg = psn()
                    mmgroup(pg, [(pg[r, :], [(xT[:, k, tcol], Wga[:, k, csl]) for k in range(8)])], reads=[xT, Wga])
                    P.op("scalar", lambda e, pg=pg, csl=csl: e.activation(out=sg[r, csl], in_=pg[r, :], func=AF.Sigmoid), reads=[pg], writes=[sg], partial=True)
                    pa = psn()
                    mmgroup(pa, [(pa[r, :], [(oaT[:, k, 0:rows], Wa[:, k, csl]) for k in range(4)])], reads=[oaT, Wa])
                    P.op("vector", lambda e, pa=pa, csl=csl: e.tensor_tensor(out=mixed[r, csl], in0=pa[r, :], in1=sg[r, csl], op=ALU.mult), reads=[pa, sg], writes=[mixed], partial=True)
                for nh in range(2):
                    csl = slice(nh * 512, (nh + 1) * 512)
                    pg = psn()
                    mmgroup(pg, [(pg[r, :], [(xT[:, k, tcol], Wgb[:, k, csl]) for k in range(8)])], reads=[xT, Wgb])
                    P.op("scalar", lambda e, pg=pg, csl=csl: e.activation(out=sg[r, csl], in_=pg[r, :], func=AF.Sigmoid), reads=[pg, mixed], writes=[sg], partial=True)
                    pb2 = psn()
                    mmgroup(pb2, [(pb2[r, :], [(obT[:, k, 0:rows], Wb[:, k, csl]) for k in range(8)])], reads=[obT, Wb])
                    P.op("vector", lambda e, pb2=pb2, csl=csl: e.tensor_tensor(out=sg[r, csl], in0=pb2[r, :], in1=sg[r, csl], op=ALU.mult), reads=[pb2, sg], writes=[sg], partial=True)
                P.op("gpsimd", lambda e: e.tensor_tensor(out=mixed[r, :], in0=mixed[r, :], in1=sg[r, :], op=ALU.add), reads=[mixed, sg], writes=[mixed])
                P.op("scalar", lambda e: e.copy(out=mixb[r, :], in_=mixed[r, :]), reads=[mixed], writes=[mixb])
                transpose_to(mixT, mixb, rows, 8)
                for nh in range(2):
                    csl = slice(nh * 512, (nh + 1) * 512)
                    py = psn()
                    mmgroup(py, [(py[r, :], [(mixT[:, k, 0:rows], Wo[:, k, csl]) for k in range(8)])], reads=[mixT, Wo])
                    P.op("vector", lambda e, py=py, csl=csl: e.scalar_tensor_tensor(out=xt[r, csl], in0=xt[r, csl], scalar=ALPHA, in1=py[r, :], op0=ALU.mult, op1=ALU.add), reads=[xt, py], writes=[xt], partial=(nh > 0))
                layernorm(rows, xt, sg, st4, G1, B1, mixed)
                P.dma("gpsimd", [lambda e, n=n, rows=rows: e.dma_start(out=X1_d.ap()[n * 128:n * 128 + rows, :], in_=mixed[0:rows, :])], mixed, reads=[mixed], writes=[X1], is_output=debug, partial=True)

            for n in range(T // 128 + (1 if SAMPLE else 0)):
                tile_c(n)
            P.pop()


        def phase_d():
            P.push()
            wpq = P.sb("wpq", [128, 8, 2048], BF16)
            load_wres(wpq, w_pq.ap().rearrange("(k p) n -> p k n", p=128), 8, 2048)
            KTt = P.sb("KTt", [128, 16, 128], F32)
            P.push()
            kraw = P.sb("kraw", [128, 16, 128], F32)
            P.dma("sync", [lambda e: e.dma_start(out=kraw[:], in_=sub_keys.ap().rearrange("h j k d -> k (h j) d"))], kraw, writes=[kraw])
            for q4 in range(4):
                pb = psn()
                fns = []
                for i in range(4):
                    fns.append(lambda e, i=i, q4=q4, pb=pb: e.transpose(out=pb[:, i * 128:(i + 1) * 128], in_=kraw[:, q4 * 4 + i, :], identity=ident[:]))
                P.group("tensor", fns, reads=[kraw, ident], writes=[pb])
                evac(KTt, KTt[:, q4 * 4:(q4 + 1) * 4, :], pb, pb[:, :].rearrange("p (a k) -> p a k", a=4), partial=True)
            P.pop()
            G2 = P.sb("G2", [128, 1024], F32)
            B2 = P.sb("B2", [128, 1024], F32)
            iot = P.sb("iot", [128, 256], I32)
            iotf = P.sb("iotf", [128, 16], F32)
            P.dma("sync", [lambda e: e.dma_start(out=G2[:], in_=ln2_g.ap().partition_broadcast(128))], G2, writes=[G2])
            P.dma("sync", [lambda e: e.dma_start(out=B2[:], in_=ln2_b.ap().partition_broadcast(128))], B2, writes=[B2])
            P.dma("sync", [lambda e: e.dma_start(out=iot[:], in_=c_iota.ap())], iot, writes=[iot])
            P.op("vector", lambda e: e.tensor_copy(out=iotf[:], in_=iot[:, 0:16]), reads=[iot], writes=[iotf])
            x1t_ = [P.sb("x1t%d" % i, [128, 1024], F32) for i in range(2)]
            x1b_ = [P.sb("x1b%d" % i, [128, 1024], BF16) for i in range(2)]
            exi_ = [P.sb("exi%d" % i, [128, 128], I32) for i in range(2)]
            gt_ = [P.sb("gt%d" % i, [128, 8, 16], F32) for i in range(2)]
            x1T = P.sb("x1T", [128, 8, 128], BF16)
            qs = P.sb("qs", [128, 2048], F32)
            junk2 = P.sb("junk2", [128, 2048], F32)
            st8 = P.sb("st8", [128, 4, 8], F32)
            qnT = P.sb("qnT", [128, 16, 128], F32)
            ssb = P.sb("ssb", [128, 2048], F32)
            wk = P.sb("wk", [128, 256], F32)
            v12 = P.sb("v12", [128, 16, 16], F32)
            e12 = P.sb("e12", [128, 16, 16], I32)
            e12f = P.sb("e12f", [128, 16, 16], F32)
            cand = P.sb("cand", [128, 2048], F32)
            sv = P.sb("sv", [128, 8, 16], F32)
            si = P.sb("si", [128, 3, 128], I32)
            sif = P.sb("sif", [128, 2, 128], F32)
            es = P.sb("es", [128, 3, 128], F32)
            dots = P.sb("dots", [128, 128], F32)
            wgt = P.sb("wgt", [128, 128], F32)
            wgb = P.sb("wgb", [128, 128], BF16)
            wd = [P.sb("wd%d" % i, [128, 8, 128], BF16) for i in range(2)]
            junkb = P.sb("junkb", [128, 1024], BF16)
            yout = P.sb("yout", [128, 1024], F32)
            st4 = P.sb("st4d", [128, 4], F32)
            NG = 16
            gb_ = [P.sb("gb%d" % i, [128, 2048], BF16) for i in range(NG)]
            gi = [0]
            YP = P.wrap("y_p", y_p)
            YS = P.wrap("y_s", y_s)

            def front_d(n):
                samp = (n == T // 128)
                rows = NS if samp else 128
                r = slice(0, rows)
                x1t, x1b, exi, gt = x1t_[n % 2], x1b_[n % 2], exi_[n % 2], gt_[n % 2]
                P.dma("sync", [lambda e: e.dma_start(out=x1t[r, :], in_=X1_d.ap()[n * 128:n * 128 + rows, :])], x1t, reads=[X1ALL], writes=[x1t])
                P.op("scalar", lambda e: e.copy(out=x1b[r, :], in_=x1t[r, :]), reads=[x1t], writes=[x1b])
                yield
                transpose_to(x1T, x1b, rows, 8)
                yield
                for nb in range(4):
                    pq = psn()
                    mmgroup(pq, [(pq[r, :], [(x1T[:, k, 0:rows], wpq[:, k, nb * 512:(nb + 1) * 512]) for k in range(8)])], reads=[x1T, wpq])
                    evac(qs, qs[r, nb * 512:(nb + 1) * 512], pq, pq[r, :], partial=(nb > 0), eng="scalar")
                    yield
                qv = qs[r, :].rearrange("p (h d) -> p h d", h=8)
                jv = junk2[r, :].rearrange("p (h d) -> p h d", h=8)
                P.op("vector", lambda e: e.tensor_reduce(out=st8[r, 0, :], in_=qv, axis=AX.X, op=ALU.add), reads=[qs], writes=[st8])
                P.op("vector", lambda e: e.tensor_single_scalar(out=st8[r, 1, :], in_=st8[r, 0, :], scalar=-1.0 / 256.0, op=ALU.mult), reads=[st8], writes=[st8])
                yield
                P.op("vector", lambda e: e.tensor_tensor(out=qv, in0=qv, in1=st8[r, 1, :].unsqueeze(2).to_broadcast([rows, 8, 256]), op=ALU.add), reads=[qs, st8], writes=[qs])
                yield
                P.op("scalar", lambda e: e.activation(out=junk2[r, :], in_=qs[r, :], func=AF.Square), reads=[qs], writes=[junk2])
                P.op("vector", lambda e: e.tensor_reduce(out=st8[r, 2, :], in_=jv, axis=AX.X, op=ALU.add), reads=[junk2], writes=[st8])
                yield
                P.op("vector", lambda e: e.tensor_scalar(out=st8[r, 3, :], in0=st8[r, 2, :], scalar1=1.0 / 256.0, scalar2=1e-5, op0=ALU.mult, op1=ALU.add), reads=[st8], writes=[st8])
                P.op("scalar", lambda e: e.sqrt(out=st8[r, 3, :], in_=st8[r, 3, :]), reads=[st8], writes=[st8])
                P.op("vector", lambda e: e.reciprocal(out=st8[r, 3, :], in_=st8[r, 3, :]), reads=[st8], writes=[st8])
                yield
                P.op("vector", lambda e: e.tensor_tensor(out=qv, in0=qv, in1=st8[r, 3, :].unsqueeze(2).to_broadcast([rows, 8, 256]), op=ALU.mult), reads=[qs, st8], writes=[qs])
                yield
                for q4 in range(4):
                    pb = psn()
                    fns = []
                    for i in range(4):
                        fns.append(lambda e, i=i, q4=q4, pb=pb: e.transpose(out=pb[:, i * 128:i * 128 + rows], in_=qs[r, (q4 * 4 + i) * 128:(q4 * 4 + i + 1) * 128], identity=ident[0:rows, 0:rows]))
                    P.group("tensor", fns, reads=[qs, ident], writes=[pb])
                    evac(qnT, qnT[:, q4 * 4:(q4 + 1) * 4, 0:rows], pb, pb[:, :].rearrange("p (a t) -> p a t", a=4)[:, :, 0:rows], partial=(q4 > 0), eng="scalar")
                    yield
                ssi = ssb[:].bitcast(I32)
                for q4 in range(4):
                    pb = psn()
                    mmgroup(pb, [(pb[r, i * 128:(i + 1) * 128], [(qnT[:, q4 * 4 + i, 0:rows], KTt[:, q4 * 4 + i, :])]) for i in range(4)], reads=[qnT, KTt])
                    P.op("vector", lambda e, pb=pb, q4=q4: e.tensor_single_scalar(out=ssi[r, q4 * 512:(q4 + 1) * 512], in_=pb[r, :].bitcast(I32), scalar=-128, op=ALU.bitwise_and), reads=[pb], writes=[ssb], partial=(q4 > 0))
                    yield
                P.op("vector", lambda e: e.tensor_tensor(out=ssi[r, :].rearrange("p (a k) -> p a k", a=16), in0=ssi[r, :].rearrange("p (a k) -> p a k", a=16), in1=iot[r, 0:128].unsqueeze(1).to_broadcast([rows, 16, 128]), op=ALU.bitwise_or), reads=[ssb, iot], writes=[ssb])
                yield
                for hj in range(16):
                    sl = slice(hj * 128, (hj + 1) * 128)
                    P.op("vector", lambda e, hj=hj, sl=sl: e.max(out=v12[r, hj, 0:8], in_=ssb[r, sl]), reads=[ssb], writes=[v12], partial=(hj > 0))
                    P.op("vector", lambda e, hj=hj, sl=sl: e.match_replace(out=wk[r, 0:128], in_to_replace=v12[r, hj, 0:8], in_values=ssb[r, sl], imm_value=-1e30), reads=[ssb, v12], writes=[wk])
                    P.op("vector", lambda e, hj=hj: e.max(out=v12[r, hj, 8:16], in_=wk[r, 0:128]), reads=[wk], writes=[v12], partial=True)
                    if hj % 2 == 1:
                        yield
                P.op("vector", lambda e: e.tensor_single_scalar(out=e12[r], in_=v12[r].bitcast(I32), scalar=127, op=ALU.bitwise_and), reads=[v12], writes=[e12])
                P.op("vector", lambda e: e.tensor_copy(out=e12f[r], in_=e12[r]), reads=[e12], writes=[e12f])
                yield
                v12v = v12[r].rearrange("p (h j) k -> p h j k", j=2)
                e12v = e12f[r].rearrange("p (h j) k -> p h j k", j=2)
                cv = cand[r, :].rearrange("p (h a b) -> p h a b", h=8, a=16)
                ci = cand[:].bitcast(I32)
                P.op("vector", lambda e: e.tensor_tensor(out=cv, in0=v12v[:, :, 0, :].unsqueeze(3).to_broadcast([rows, 8, 16, 16]), in1=v12v[:, :, 1, :].unsqueeze(2).to_broadcast([rows, 8, 16, 16]), op=ALU.add), reads=[v12], writes=[cand])
                yield
                P.op("vector", lambda e: e.tensor_single_scalar(out=ci[r, :], in_=ci[r, :], scalar=-256, op=ALU.bitwise_and), reads=[cand], writes=[cand])
                yield
                P.op("vector", lambda e: e.tensor_tensor(out=ci[r, :].rearrange("p (h c) -> p h c", h=8), in0=ci[r, :].rearrange("p (h c) -> p h c", h=8), in1=iot[r, :].unsqueeze(1).to_broadcast([rows, 8, 256]), op=ALU.bitwise_or), reads=[cand, iot], writes=[cand])
                yield
                for h in range(8):
                    sl = slice(h * 256, (h + 1) * 256)
                    P.op("vector", lambda e, h=h, sl=sl: e.max(out=sv[r, h, 0:8], in_=cand[r, sl]), reads=[cand], writes=[sv], partial=(h > 0))
                    P.op("vector", lambda e, h=h, sl=sl: e.match_replace(out=wk[r, :], in_to_replace=sv[r, h, 0:8], in_values=cand[r, sl], imm_value=-1e30), reads=[cand, sv], writes=[wk])
                    P.op("vector", lambda e, h=h: e.max(out=sv[r, h, 8:16], in_=wk[r, :]), reads=[wk], writes=[sv], partial=True)
                    yield
                svf = sv[r].rearrange("p h k -> p (h k)")
                P.op("vector", lambda e: e.tensor_single_scalar(out=si[r, 0, :], in_=svf.bitcast(I32), scalar=255, op=ALU.bitwise_and), reads=[sv], writes=[si])
                P.op("vector", lambda e: e.tensor_single_scalar(out=si[r, 1, :], in_=si[r, 0, :], scalar=4, op=ALU.logical_shift_right), reads=[si], writes=[si])
                P.op("vector", lambda e: e.tensor_single_scalar(out=si[r, 2, :], in_=si[r, 0, :], scalar=15, op=ALU.bitwise_and), reads=[si], writes=[si])
                P.op("vector", lambda e: e.tensor_copy(out=sif[r], in_=si[r, 1:3, :]), reads=[si], writes=[sif])
                yield
                ohv = junk2[r, :].rearrange("p (h k i) -> p h k i", h=8, k=16)
                iob = iotf[r, :].unsqueeze(1).unsqueeze(1).to_broadcast([rows, 8, 16, 16])
                for j in range(2):
                    idxv = sif[r, j, :].rearrange("p (h k) -> p h k", h=8).unsqueeze(3).to_broadcast([rows, 8, 16, 16])
                    P.op("vector", lambda e, idxv=idxv: e.tensor_tensor(out=ohv, in0=idxv, in1=iob, op=ALU.is_equal), reads=[sif, iotf], writes=[junk2])
                    yield
                    P.op("vector", lambda e, j=j: e.tensor_tensor(out=ohv, in0=ohv, in1=e12v[:, :, j, :].unsqueeze(2).to_broadcast([rows, 8, 16, 16]), op=ALU.mult), reads=[junk2, e12f], writes=[junk2])
                    yield
                    P.op("vector", lambda e, j=j: e.tensor_reduce(out=es[r, j, :].rearrange("p (h k) -> p h k", h=8), in_=ohv, axis=AX.X, op=ALU.add), reads=[junk2], writes=[es], partial=(j > 0))
                    yield
                P.op("vector", lambda e: e.scalar_tensor_tensor(out=es[r, 2, :], in0=es[r, 0, :], scalar=128.0, in1=es[r, 1, :], op0=ALU.mult, op1=ALU.add), reads=[es], writes=[es])
                P.op("vector", lambda e: e.tensor_copy(out=exi[r, :], in_=es[r, 2, :]), reads=[es], writes=[exi])
                yield
                P.op("vector", lambda e: e.tensor_reduce(out=st8[r, 0, :], in_=sv[r], axis=AX.X, op=ALU.max), reads=[sv], writes=[st8])
                P.op("vector", lambda e: e.tensor_tensor(out=gt[r], in0=sv[r], in1=st8[r, 0, :].unsqueeze(2).to_broadcast([rows, 8, 16]), op=ALU.subtract), reads=[sv, st8], writes=[gt])
                P.op("scalar", lambda e: e.activation(out=gt[r], in_=gt[r], func=AF.Exp), reads=[gt], writes=[gt])
                yield
                P.op("vector", lambda e: e.tensor_reduce(out=st8[r, 1, :], in_=gt[r], axis=AX.X, op=ALU.add), reads=[gt], writes=[st8])
                P.op("vector", lambda e: e.reciprocal(out=st8[r, 1, :], in_=st8[r, 1, :]), reads=[st8], writes=[st8])
                P.op("vector", lambda e: e.tensor_tensor(out=gt[r], in0=gt[r], in1=st8[r, 1, :].unsqueeze(2).to_broadcast([rows, 8, 16]), op=ALU.mult), reads=[gt, st8], writes=[gt])
                yield

            def back_d(n):
                samp = (n == T // 128)
                rows = NS if samp else 128
                r = slice(0, rows)
                x1t, x1b, exi, gt = x1t_[n % 2], x1b_[n % 2], exi_[n % 2], gt_[n % 2]
                gtf = gt[r].rearrange("p h k -> p (h k)")
                P.op("gpsimd", lambda e: e.memset(dots[r, :], 0.0), writes=[dots])
                pacc = [psn(), psn()]
                ps_reserved.update(p_.name for p_ in pacc)
                for g8 in range(16):
                    bufs = []
                    for j in range(8):
                        hk = g8 * 8 + j
                        g = gb_[gi[0] % NG]
                        gi[0] += 1
                        bufs.append(g)
                        P.dma("gpsimd", [lambda e, g=g, hk=hk: e.indirect_dma_start(out=g[r, :], out_offset=None, in_=UV_d.ap(), in_offset=bass.IndirectOffsetOnAxis(ap=exi[r, hk:hk + 1], axis=0))], g, reads=[exi, TBL], writes=[g])
                        P.op("vector", lambda e, g=g, hk=hk: e.scalar_tensor_tensor(out=junkb[r, :], in0=g[r, 0:1024], scalar=1.0, in1=x1b[r, :], op0=ALU.mult, op1=ALU.mult, accum_out=dots[r, hk:hk + 1]), reads=[g, x1b, dots], writes=[junkb, dots])
                        yield
                    hs = slice(g8 * 8, g8 * 8 + 8)
                    P.op("scalar", lambda e, hs=hs: e.activation(out=wgt[r, hs], in_=dots[r, hs], func=AF.Gelu), reads=[dots], writes=[wgt])
                    P.op("vector", lambda e, hs=hs: e.tensor_tensor(out=wgb[r, hs], in0=wgt[r, hs], in1=gtf[:, hs], op=ALU.mult), reads=[wgt, gt], writes=[wgb])
                    wdc = wd[g8 % 2]
                    P.op("vector", lambda e, wdc=wdc, hs=hs: e.tensor_tensor(out=wdc[r, :, 0:rows], in0=identb[r, 0:rows].unsqueeze(1).to_broadcast([rows, 8, rows]), in1=wgb[r, hs].unsqueeze(2).to_broadcast([rows, 8, rows]), op=ALU.mult), reads=[identb, wgb], writes=[wdc])
                    fns = []
                    for j in range(8):
                        hk = g8 * 8 + j
                        for hf in range(2):
                            fns.append(lambda e, hf=hf, g=bufs[j], wdc=wdc, j=j, hk=hk: e.matmul(pacc[hf][r, :], lhsT=wdc[r, j, 0:rows], rhs=g[r, 1024 + hf * 512:1024 + (hf + 1) * 512], start=(hk == 0), stop=(hk == 127)))
                    P.group("tensor", fns, reads=[wdc] + bufs, writes=pacc)
                for hf in range(2):
                    P.op("vector", lambda e, hf=hf: e.scalar_tensor_tensor(out=x1t[r, hf * 512:(hf + 1) * 512], in0=x1t[r, hf * 512:(hf + 1) * 512], scalar=ALPHA, in1=pacc[hf][r, :], op0=ALU.mult, op1=ALU.add), reads=[x1t, pacc[hf]], writes=[x1t], partial=(hf > 0))
                ps_reserved.clear()
                yield
                layernorm(rows, x1t, yout, st4, G2, B2, yout)
                if samp:
                    P.dma("sync", [lambda e: e.dma_start(out=y_s.ap(), in_=yout[r, :])], yout, reads=[yout], writes=[YS], is_output=True, partial=True)
                else:
                    P.dma("sync", [lambda e: e.dma_start(out=y_p.ap()[n * 128:(n + 1) * 128, :], in_=yout[r, :])], yout, reads=[yout], writes=[YP], is_output=True, partial=True)
                yield

            tl = list(TILES_D if TILES_D is not None else range(NTILES_D if NTILES_D else (T // 128 + (1 if SAMPLE else 0))))
            for _ in front_d(tl[0]):
                pass
            for i, n in enumerate(tl):
                bk = back_d(n)
                fr = front_d(tl[i + 1]) if i + 1 < len(tl) else None
                nstep = 0
                for _ in bk:
                    nstep += 1
                    if fr is not None and nstep <= 128 and nstep % 2 == 0:
                        try:
                            next(fr)
                        except StopIteration:
                            fr = None
                    if fr is not None and nstep == 128:
                        for _ in fr:
                            pass
                        fr = None
            P.pop()

        def phase_s():
            P.push()
            alloc_wbf("S")
            NPC = COL_GA
            psall = P.sb("psall", [16, NPC], F32)
            SC8 = 0.125
            for c0 in range(0, NPC, 256):
                ncol = min(256, NPC - c0)
                w = get_w(c0, ncol)
                pb = psn()
                mmgroup(pb, [(pb[0:16, 0:ncol], [(xT[:, k, T:T + NS], w[:, k, 0:ncol]) for k in range(8)])], reads=[w, xT])
                evac(psall, psall[:, c0:c0 + ncol], pb, pb[0:16, 0:ncol], partial=(c0 > 0))
            esel = P.sb("esel", [128, 16, 16], F32)
            sel16 = P.sb("sel16", [16, 16, 128], F32)
            ohs = P.sb("ohs", [32, 3, 128], F32)
            relb = P.sb("relb", [32, 24], F32)
            rel0 = P.sb("rel0", [16, 24], F32)
            biasP = P.sb("biasP", [128, 24], F32)
            for (dst, src) in ((esel, c_esel), (sel16, c_sel16), (ohs, c_ohs), (relb, rel_bias)):
                P.dma("sync", [lambda e, dst=dst, src=src: e.dma_start(out=dst[:], in_=src.ap())], dst, writes=[dst])
            P.dma("sync", [lambda e: e.dma_start(out=rel0[:], in_=rel_bias.ap()[0:1, :].partition_broadcast(16))], rel0, writes=[rel0])
            for g in range(3):
                pb = psn()
                mmgroup(pb, [(pb[:, 0:8], [(ohs[:, g, :], relb[:, g * 8:(g + 1) * 8])])], reads=[ohs, relb])
                evac(biasP, biasP[:, g * 8:(g + 1) * 8], pb, pb[:, 0:8], partial=(g > 0), eng="vector")
            KVS = [P.wrap("kvs%d" % g, kv_s[g]) for g in range(3)]
            EXT = P.wrap("ext", ext_d)
            for g in range(3):
                wb_ = WBS[g]
                fns = []
                for b in range(4):
                    fns.append(lambda e, g=g, b=b, wb_=wb_: e.dma_start(out=kv_s[g].ap()[b, 0:wb_ - 4, :], in_=cache[g].ap()[b, 4:wb_, :]))
                P.dma("gpsimd", fns, KVS[g], writes=[KVS[g]], is_output=True, partial=True)
                for part, col in ((0, COL_KA), (1, COL_VA)):
                    dst = bass.AP(kv_s[g], (wb_ - 4) * 1024 + part * 512, [[wb_ * 1024, 4], [1024, 4], [1, 512]])
                    P.dma("gpsimd", [lambda e, dst=dst, col=col, g=g: e.dma_start(out=dst, in_=psall[:, col + g * 512:col + (g + 1) * 512])], KVS[g], reads=[psall], writes=[KVS[g]], is_output=True, partial=True)
            P.dma("gpsimd", [lambda e, b=b: e.dma_start(out=ext_d.ap()[b, 0:128, :], in_=cache[0].ap()[b, :, :]) for b in range(4)], EXT, writes=[EXT], partial=True)
            for part, col in ((0, COL_KA), (1, COL_VA)):
                dst = bass.AP(ext_d, 128 * 1024 + part * 512, [[132 * 1024, 4], [1024, 4], [1, 512]])
                P.dma("gpsimd", [lambda e, dst=dst, col=col: e.dma_start(out=dst, in_=psall[:, col:col + 512])], EXT, reads=[psall], writes=[EXT], partial=True)
            kvg = [P.sb("kvg%d" % i, [128, 1024], F32) for i in range(2)]
            qbc = [P.sb("qbc%d" % i, [128, 512], F32) for i in range(2)]
            jk = P.sb("jks", [128, 512], F32)
            sc = [P.sb("scs%d" % i, [128, 8], F32) for i in range(2)]
            pvs = [P.sb("pvs%d" % i, [128, 8, 65], F32) for i in range(2)]
            pacc = [psn(), psn()]
            ps_reserved.update(p_.name for p_ in pacc)
            cnt = 0
            for g in range(3):
                dil = A_DIL[g]
                for bs in range(16):
                    b, s_ = bs // 4, bs % 4
                    i2 = cnt % 2
                    src_t = ext_d if g == 0 else cache[g]
                    rows_t = 132 if g == 0 else WBS[g]
                    src = bass.AP(src_t, (b * rows_t + s_) * 1024, [[dil * 1024, 128], [1, 1024]])
                    P.dma("sync", [lambda e, src=src, i2=i2: e.dma_start(out=kvg[i2][:], in_=src)], kvg[i2], reads=([EXT] if g == 0 else []), writes=[kvg[i2]])
                    pq = psn()
                    mmgroup(pq, [(pq[:, :], [(sel16[:, bs, :], psall[:, COL_QA + g * 512:COL_QA + (g + 1) * 512])])], reads=[sel16, psall])
                    P.op("scalar", lambda e, pq=pq, i2=i2: e.mul(out=qbc[i2][:], in_=pq[:, :], mul=SC8), reads=[pq], writes=[qbc[i2]])
                    P.op("vector", lambda e, i2=i2: e.tensor_tensor(out=jk[:], in0=kvg[i2][:, 0:512], in1=qbc[i2][:], op=ALU.mult), reads=[kvg[i2], qbc[i2]], writes=[jk])
                    P.op("vector", lambda e, i2=i2: e.tensor_reduce(out=sc[i2][:], in_=jk[:].rearrange("p (h d) -> p h d", h=8), axis=AX.X, op=ALU.add), reads=[jk], writes=[sc[i2]])
                    P.op("vector", lambda e, i2=i2, g=g: e.tensor_tensor(out=sc[i2][:], in0=sc[i2][:], in1=biasP[:, g * 8:(g + 1) * 8], op=ALU.add), reads=[sc[i2], biasP], writes=[sc[i2]])
                    P.op("scalar", lambda e, i2=i2: e.activation(out=pvs[i2][:, :, 64], in_=sc[i2][:], func=AF.Exp), reads=[sc[i2]], writes=[pvs[i2]])
                    P.op("vector", lambda e, i2=i2: e.tensor_tensor(out=pvs[i2][:, :, 0:64], in0=kvg[i2][:, 512:1024].rearrange("p (h d) -> p h d", h=8), in1=pvs[i2][:, :, 64:65].to_broadcast([128, 8, 64]), op=ALU.mult), reads=[kvg[i2], pvs[i2]], writes=[pvs[i2]], partial=True)
                    first, last = (cnt == 0), (cnt == 47)
                    pvf = pvs[i2][:].rearrange("p h d -> p (h d)")
                    fns = []
                    for hf in range(2):
                        fns.append(lambda e, hf=hf, pvf=pvf, bs=bs, first=first, last=last: e.matmul(pacc[hf][0:16, 0:260], lhsT=esel[:, bs, :], rhs=pvf[:, hf * 260:(hf + 1) * 260], start=first, stop=last))
                    P.group("tensor", fns, reads=[esel, pvs[i2]], writes=pacc)
                    cnt += 1
            Us = P.sb("Us", [16, 8, 65], F32)
            jks = P.sb("jk16", [16, 512], F32)
            scs = P.sb("sc16", [16, 8], F32)
            pw = P.sb("pw16", [16, 8, 65], F32)
            for hf in range(2):
                evac(Us, Us[:].rearrange("p h d -> p (h d)")[:, hf * 260:(hf + 1) * 260], pacc[hf], pacc[hf][0:16, 0:260], partial=(hf > 0), eng="vector")
            ps_reserved.clear()
            for g in range(3):
                qsl = slice(COL_QA + g * 512, COL_QA + (g + 1) * 512)
                ksl = slice(COL_KA + g * 512, COL_KA + (g + 1) * 512)
                vsl = slice(COL_VA + g * 512, COL_VA + (g + 1) * 512)
                P.op("vector", lambda e, qsl=qsl, ksl=ksl: e.tensor_tensor(out=jks[:], in0=psall[:, qsl], in1=psall[:, ksl], op=ALU.mult), reads=[psall], writes=[jks])
                P.op("vector", lambda e: e.tensor_reduce(out=scs[:], in_=jks[:].rearrange("p (h d) -> p h d", h=8), axis=AX.X, op=ALU.add), reads=[jks], writes=[scs])
                P.op("vector", lambda e, g=g: e.scalar_tensor_tensor(out=scs[:], in0=scs[:], scalar=SC8, in1=rel0[:, g * 8:(g + 1) * 8], op0=ALU.mult, op1=ALU.add), reads=[scs, rel0], writes=[scs])
                P.op("scalar", lambda e: e.activation(out=pw[:, :, 64], in_=scs[:], func=AF.Exp), reads=[scs], writes=[pw])
                P.op("vector", lambda e, vsl=vsl: e.tensor_tensor(out=pw[:, :, 0:64], in0=psall[:, vsl].rearrange("p (h d) -> p h d", h=8), in1=pw[:, :, 64:65].to_broadcast([16, 8, 64]), op=ALU.mult), reads=[psall, pw], writes=[pw], partial=True)
                P.op("vector", lambda e: e.tensor_tensor(out=Us[:], in0=Us[:], in1=pw[:], op=ALU.add), reads=[Us, pw], writes=[Us])
            P.dma("gpsimd", [lambda e: e.dma_start(out=U_s.ap(), in_=Us[:].rearrange("p h d -> p (h d)"))], Us, reads=[Us], writes=[UB_all], partial=True)

            tric16 = P.sb("tric16", [16, 16], F32)
            trirev16 = P.sb("trirev16", [16, 16], F32)
            mask16 = P.sb("mask16", [16, 16], F32)
            colmask = P.sb("colmask", [128, 4, 16], F32)
            rowmask = P.sb("rowmask", [16, 4], F32)
            w2e = P.sb("w2es", [17, 512], F32)
            Gb = P.sb("Gbs", [16, 1024], F32)
            for (dst, src) in ((tric16, c_tric16), (trirev16, c_trirev16), (mask16, c_mask16), (colmask, c_colmask), (rowmask, c_rowmask)):
                P.dma("sync", [lambda e, dst=dst, src=src: e.dma_start(out=dst[:], in_=src.ap())], dst, writes=[dst])
            P.dma("sync", [lambda e: e.dma_start(out=w2e[0:16, :], in_=w_g2.ap()), lambda e: e.dma_start(out=w2e[16:17, :], in_=b_g.ap())], w2e, writes=[w2e])
            P.dma("sync", [lambda e: e.dma_start(out=Gb[:], in_=g_norm.ap().partition_broadcast(16))], Gb, writes=[Gb])
            abTs = P.sb("abTs", [17, 16], F32)
            P.op("vector", lambda e: e.memset(abTs[:], 1.0), writes=[abTs])
            wab = get_w(COL_AB, 16)
            pb = psn()
            mmgroup(pb, [(pb[0:16, 0:16], [(wab[:, k, 0:16], xT[:, k, T:T + NS]) for k in range(8)])], reads=[wab, xT])
            evac(abTs, abTs[0:16, :], pb, pb[0:16, 0:16], eng="vector")
            SC = 128.0 ** -0.5
            t1 = P.sb("t1s", [16, 128], F32)
            la = P.sb("las", [16, 128], F32)
            eb = P.sb("ebs", [128, 16], F32)
            enb = P.sb("enbs", [128, 16], F32)
            erev = P.sb("erevs", [16, 128], F32)
            qeT = P.sb("qeTs", [128, 16], BF16)
            qeTm = P.sb("qeTms", [128, 4, 16], BF16)
            keT = P.sb("keTs", [128, 16], BF16)
            kdf = P.sb("kdfs", [16, 128], F32)
            kdm = P.sb("kdms", [16, 4, 128], BF16)
            vb_ = P.sb("vbs", [16, 256], BF16)
            attm = P.sb("attms", [16, 16], BF16)
            S0 = [P.sb("S0_%d" % i, [128, 256], F32) for i in range(4)]
            S0b = [P.sb("S0b_%d" % i, [128, 256], BF16) for i in range(4)]
            Sn = [P.sb("Sn_%d" % i, [128, 256], F32) for i in range(2)]
            srs = P.sb("srs", [16, 256], F32)
            osq = P.sb("osqs", [16, 256], F32)
            ss = P.sb("sss", [16, 2], F32)
            ot = P.sb("ots", [16, 256], F32)
            oo = P.sb("oos", [16, 256], F32)
            STS = P.wrap("st_s", st_s)
            for h in range(4):
                wq = get_w(COL_QB + h * 128, 128)
                wk = get_w(COL_KB + h * 128, 128)
                for b in range(4):
                    P.dma("sync", [lambda e, b=b, h=h: e.dma_start(out=S0[b][:], in_=state_in.ap()[b, h])], S0[b], writes=[S0[b]])
                    P.op("gpsimd", lambda e, b=b: e.tensor_copy(out=S0b[b][:], in_=S0[b][:]), reads=[S0[b]], writes=[S0b[b]])
                pz = psn()
                mmgroup(pz, [(pz[0:16, 0:128], [(abTs[:, :], w2e[:, h * 128:(h + 1) * 128])])], reads=[abTs, w2e])
                P.op("scalar", lambda e, pz=pz: e.activation(out=t1[:], in_=pz[0:16, 0:128], func=AF.Exp, scale=-1.0), reads=[pz], writes=[t1])
                P.op("scalar", lambda e: e.activation(out=la[:], in_=t1[:], func=AF.Ln, bias=1.0, scale=1.0), reads=[t1], writes=[la])
                pbt = psn()
                mmgroup(pbt, [(pbt[:, 0:16], [(la[:, :], tric16[:])])], reads=[la, tric16])
                P.op("scalar", lambda e, pbt=pbt: e.activation(out=eb[:], in_=pbt[:, 0:16], func=AF.Exp), reads=[pbt], writes=[eb])
                P.op("scalar", lambda e, pbt=pbt: e.activation(out=enb[:], in_=pbt[:, 0:16], func=AF.Exp, scale=-1.0), reads=[pbt], writes=[enb])
                prv = psn()
                mmgroup(prv, [(prv[0:16, 0:128], [(trirev16[:], la[:, :])])], reads=[la, trirev16])
                P.op("scalar", lambda e, prv=prv: e.activation(out=erev[:], in_=prv[0:16, 0:128], func=AF.Exp), reads=[prv], writes=[erev])
                pq = psn()
                mmgroup(pq, [(pq[:, 0:16], [(wq[:, k, 0:128], xT[:, k, T:T + NS]) for k in range(8)])], reads=[wq, xT])
                P.op("vector", lambda e, pq=pq: e.scalar_tensor_tensor(out=qeT[:], in0=pq[:, 0:16], scalar=SC, in1=eb[:], op0=ALU.mult, op1=ALU.mult), reads=[pq, eb], writes=[qeT])
                P.op("vector", lambda e: e.tensor_tensor(out=qeTm[:], in0=colmask[:], in1=qeT[:].unsqueeze(1).to_broadcast([128, 4, 16]), op=ALU.mult), reads=[qeT, colmask], writes=[qeTm])
                pk = psn()
                mmgroup(pk, [(pk[:, 0:16], [(wk[:, k, 0:128], xT[:, k, T:T + NS]) for k in range(8)])], reads=[wk, xT])
                P.op("vector", lambda e, pk=pk: e.tensor_tensor(out=keT[:], in0=pk[:, 0:16], in1=enb[:], op=ALU.mult), reads=[pk, enb], writes=[keT])
                P.op("vector", lambda e, h=h: e.tensor_tensor(out=kdf[:], in0=psall[:, COL_KB + h * 128:COL_KB + (h + 1) * 128], in1=erev[:], op=ALU.mult), reads=[psall, erev], writes=[kdf])
                for b in range(4):
                    P.op("vector", lambda e, b=b: e.tensor_scalar(out=kdm[:, b, :], in0=kdf[:], scalar1=rowmask[:, b:b + 1], scalar2=None, op0=ALU.mult), reads=[kdf, rowmask], writes=[kdm], partial=(b > 0))
                P.op("vector", lambda e, h=h: e.tensor_copy(out=vb_[:], in_=psall[:, COL_VB + h * 256:COL_VB + (h + 1) * 256]), reads=[psall], writes=[vb_])
                pa = psn()
                mmgroup(pa, [(pa[0:16, 0:16], [(keT[:, :], qeT[:, :])])], reads=[keT, qeT])
                P.op("vector", lambda e, pa=pa: e.tensor_tensor(out=attm[:], in0=pa[0:16, 0:16], in1=mask16[:], op=ALU.mult), reads=[pa, mask16], writes=[attm])
                po = psn()
                mmgroup(po, [(po[0:16, 0:256], [(attm[:, :], vb_[:, :])] + [(qeTm[:, b, :], S0b[b][:]) for b in range(4)])], reads=[attm, vb_, qeTm] + S0b)
                for b in range(4):
                    pS = psn()
                    mmgroup(pS, [(pS[:, 0:256], [(kdm[:, b, :], vb_[:, :])])], reads=[kdm, vb_])
                    sn = Sn[b % 2]
                    P.op("vector", lambda e, b=b, pS=pS, sn=sn: e.scalar_tensor_tensor(out=sn[:], in0=S0[b][:], scalar=eb[:, 4 * b + 3:4 * b + 4], in1=pS[:, 0:256], op0=ALU.mult, op1=ALU.add), reads=[S0[b], eb, pS], writes=[sn])
                    P.dma("gpsimd", [lambda e, b=b, h=h, sn=sn: e.dma_start(out=st_s.ap()[b, h], in_=sn[:])], sn, reads=[sn], writes=[STS], is_output=True, partial=True)
                P.op("scalar", lambda e, h=h: e.activation(out=srs[:], in_=psall[:, COL_RB + h * 256:COL_RB + (h + 1) * 256], func=AF.Silu), reads=[psall], writes=[srs])
                P.op("gpsimd", lambda e: e.memset(ss[:], 0.0), writes=[ss])
                P.op("scalar", lambda e, po=po: e.activation(out=osq[:], in_=po[0:16, 0:256], func=AF.Square, accum_out=ss[:, 0:1]), reads=[po, ss], writes=[osq, ss])
                P.op("vector", lambda e: e.tensor_scalar(out=ss[:, 1:2], in0=ss[:, 0:1], scalar1=1.0 / 256.0, scalar2=1e-5, op0=ALU.mult, op1=ALU.add), reads=[ss], writes=[ss])
                P.op("scalar", lambda e: e.sqrt(out=ss[:, 1:2], in_=ss[:, 1:2]), reads=[ss], writes=[ss])
                P.op("vector", lambda e: e.reciprocal(out=ss[:, 1:2], in_=ss[:, 1:2]), reads=[ss], writes=[ss])
                P.op("vector", lambda e, po=po, h=h: e.scalar_tensor_tensor(out=ot[:], in0=po[0:16, 0:256], scalar=ss[:, 1:2], in1=Gb[:, h * 256:(h + 1) * 256], op0=ALU.mult, op1=ALU.mult), reads=[po, ss, Gb], writes=[ot])
                P.op("vector", lambda e: e.tensor_tensor(out=oo[:], in0=ot[:], in1=srs[:], op=ALU.mult), reads=[ot, srs], writes=[oo])
                P.dma("gpsimd", [lambda e, h=h: e.dma_start(out=OB_s.ap()[:, h * 256:(h + 1) * 256], in_=oo[:])], oo, reads=[oo], writes=[UB_all], partial=True)
            P.pop()

        P.push()
        alloc_wbf("A")
        tst = [P.sb("tst%d" % i, [128, 2048], F32) for i in range(2)]
        tbf = [P.sb("tbf%d" % i, [128, 2048], BF16) for i in range(2)]

        def prepass_gen():
            ti = 0
            for (src_t, c0_) in ((peer_u, 0), (peer_v, 1024)):
                sv_ = src_t.ap().rearrange("(i p j) d -> i p (j d)", p=128, j=2)
                dv_ = UV_d.ap()[:, c0_:c0_ + 1024].rearrange("(i p j) d -> i p j d", p=128, j=2)
                for i in range(64):
                    a, b = tst[ti % 2], tbf[ti % 2]
                    P.dma("sync", [lambda e, a=a, i=i, sv_=sv_: e.dma_start(out=a[:], in_=sv_[i])], a, writes=[a])
                    if ti % 2 == 0:
                        P.op("vector", lambda e, a=a, b=b: e.tensor_copy(out=b[:], in_=a[:]), reads=[a], writes=[b])
                    else:
                        P.op("scalar", lambda e, a=a, b=b: e.copy(out=b[:], in_=a[:]), reads=[a], writes=[b])
                    P.dma("gpsimd", [lambda e, b=b, i=i, dv_=dv_: e.dma_start(out=dv_[i], in_=b[:].rearrange("p (j d) -> p j d", j=2))], b, reads=[b], writes=[TBL], partial=True)
                    ti += 1
                    yield

        prepass = prepass_gen()

        def prepass_step():
            try:
                next(prepass)
            except StopIteration:
                pass
        QT = P.sb("QT", [128, 2, T], BF16)
        KT = P.sb("KT", [128, 2, T], BF16)
        Vb = P.sb("Vb", [128, 32, 4, 72], BF16)
        Hk = P.sb("Hk", [128, 8, 2, 128], F32)
        BT = P.sb("BT", [128, 8, 2, 128], F32)
        Sf = [P.sb("Sf%d" % i, [128, 2, 2, 128], F32) for i in range(2)]
        PT = [P.sb("PT%d" % i, [128, 2, 2, 128], BF16) for i in range(2)]
        Ost = [P.sb("Ost%d" % i, [128, 260], F32) for i in range(2)]
        KVst = [P.sb("KVst%d" % i, [128, 2, 256], F32) for i in range(2)]
        P.op("vector", lambda e: e.memset(Vb[:], 1.0), writes=[Vb])
        Ub = [UB_all for g in range(3)]
        KVo = [P.wrap("kvo%d" % g, kv_p[g]) for g in range(3)]
        cnt = [0]

        for g in range(3):
            dil = A_DIL[g]
            span = 128 * dil
            nspan = T // span
            hfn = []
            for h8 in range(8):
                hsrc = bass.AP(vec_d, (g * 8 + h8) * 384, [[1, 128], [128, 2], [1, 128]])
                hfn.append(lambda e, hsrc=hsrc, h8=h8: e.dma_start(out=Hk[:, h8, :, :], in_=hsrc))
            P.dma("sync", hfn, Hk, reads=[VEC], writes=[Hk])
            if STOP == 2:
                break
            Hkf = Hk[:].rearrange("p h a q -> p (h a q)")
            BTf = BT[:].rearrange("p h a q -> p (h a q)")
            for cc in range(4):
                pb = psn()
                mmgroup(pb, [(pb[:, :], [(flip[:], Hkf[:, cc * 512:(cc + 1) * 512])])], reads=[flip, Hk])
                evac(BT, BTf[:, cc * 512:(cc + 1) * 512], pb, pb[:, :], partial=(cc > 0))

            def tok_slice(blk):
                s, r = blk // dil, blk % dil
                base = s * span + r
                return slice(base, base + 127 * dil + 1, dil) if dil > 1 else slice(base, base + 128)

            for hh in range(2):
                wq = get_w(COL_QA + g * 512 + hh * 256)
                wk = get_w(COL_KA + g * 512 + hh * 256)
                wv = get_w(COL_VA + g * 512 + hh * 256)
                for (dst, w, sc) in ((QT, wq, 0.125), (KT, wk, None)):
                    for c in range(2):
                        for tc in range(T // 512):
                            pb = psn()
                            mmgroup(pb, [(pb[:, :], [(w[:, k, c * 128:(c + 1) * 128], xT[:, k, tc * 512:(tc + 1) * 512]) for k in range(8)])], reads=[w, xT])
                            evac(dst, dst[:, c, tc * 512:(tc + 1) * 512], pb, pb[:, :], scale=sc, partial=True)
                if STOP == 3:
                    break
                for blk in range(32):
                    ts_ = tok_slice(blk)
                    pb = psn()
                    mmgroup(pb, [(pb[:, 0:256], [(xT[:, k, ts_], wv[:, k, :]) for k in range(8)])], reads=[wv, xT])
                    evac(Vb, Vb[:, blk, :, 0:64], pb, pb[:, 0:256].rearrange("p (h d) -> p h d", h=4), partial=True)
                    s, r = blk // dil, blk % dil
                    if s == nspan - 1:
                        kst = KVst[cnt[0] % 2]
                        cnt[0] += 1
                        pk = psn()
                        mmgroup(pk, [(pk[:, 0:256], [(xT[:, k, ts_], wk[:, k, :]) for k in range(8)])], reads=[wk, xT])
                        evac(kst, kst[:, 0, :], pk, pk[:, 0:256])
                        evac(kst, kst[:, 1, :], pb, pb[:, 0:256], partial=True)
                        dst = bass.AP(kv_p[g], r * 1024 + hh * 256, [[dil * 1024, 128], [512, 2], [1, 256]])
                        P.dma("gpsimd", [lambda e, dst=dst, kst=kst: e.dma_start(out=dst, in_=kst[:])], kst, reads=[kst], writes=[KVo[g]], is_output=True, partial=True)
                if STOP == 4:
                    break
                for blk in range(32 if STOP < 5 else 2):
                    prepass_step()
                    s, r = blk // dil, blk % dil
                    tq = tok_slice(blk)
                    tp = tok_slice(blk - dil) if s > 0 else None
                    po = psn()
                    pts = []
                    na = 2 if s > 0 else 1
                    for j in range(2):
                        pb = psn()
                        pbv = pb[:].rearrange("p (c a q) -> p c a q", c=2, a=2)
                        pr = slice(j * 64, (j + 1) * 64)
                        specs = []
                        for c in range(2):
                            specs.append((pbv[:, c, 0, :], [(KT[pr, c, tq], QT[pr, c, tq])]))
                            if s > 0:
                                specs.append((pbv[:, c, 1, :], [(KT[pr, c, tp], QT[pr, c, tq])]))
                        mmgroup(pb, specs, reads=[KT, QT])
                        sf = Sf[cnt[0] % 2]
                        pt = PT[cnt[0] % 2]
                        cnt[0] += 1
                        hsl = slice(hh * 4 + j, hh * 4 + j + 3, 2)
                        P.op("vector", lambda e, sf=sf, pbv=pbv, na=na, hsl=hsl: e.tensor_tensor(out=sf[:, :, 0:na, :], in0=pbv[:, :, 0:na, :], in1=BT[:, hsl, 0:na, :], op=ALU.add), reads=[pb, BT], writes=[sf])
                        P.op("scalar", lambda e, sf=sf, pt=pt, na=na: e.activation(out=pt[:, :, 0:na, :], in_=sf[:, :, 0:na, :], func=AF.Exp), reads=[sf], writes=[pt])
                        pts.append(pt)
                    pov = po[:, 0:288].rearrange("p (h d) -> p h d", h=4)
                    specs = []
                    for hl in range(4):
                        c, j = hl // 2, hl % 2
                        pt = pts[j]
                        items = [(pt[:, c, 0, :], Vb[:, blk, hl, :])]
                        if s > 0:
                            items.append((pt[:, c, 1, :], Vb[:, blk - dil, hl, :]))
                        specs.append((pov[:, hl, :], items))
                    mmgroup(po, specs, reads=[pts[0], pts[1], Vb])
                    ost = Ost[cnt[0] % 2]
                    evac(ost, ost[:, :].rearrange("p (h d) -> p h d", h=4), po, pov[:, :, 0:65])
                    udst = bass.AP(U_d[g], (s * span + r) * 520 + hh * 260, [[dil * 520, 128], [1, 260]])
                    P.dma("gpsimd", [lambda e, udst=udst, ost=ost: e.dma_start(out=udst, in_=ost[:])], ost, reads=[ost], writes=[Ub[g]], is_output=debug, partial=True)

            if STOP >= 2:
                break
        for _ in prepass:
            pass
        P.pop()
        if STOP == 0:
            phase_b()
            if SAMPLE:
                phase_s()
            phase_c()
        P.pop()
        if STOP == 0:
            phase_d()
        P.finish()
        print("instructions:", P.ninstr, "sems:", P.nsem)
    return nc


def core_inputs(inp, c, consts=None):
    if consts is None:
        consts = make_consts()
    g = lambda k: np.asarray(inp[k])
    m = dict(x_p=g("x_prompt")[c], x_s=g("x_sample")[4 * c:4 * c + 4].reshape(NS, D),
             rel_bias=g("rel_bias"), w_in=g("w_in")[0], w_gla_gate2=g("w_gla_gate2")[0],
             b_gla_gate=g("b_gla_gate"), g_gla_norm=g("g_gla_norm"),
             w_branch_a=g("w_branch_a")[0], w_branch_b=g("w_branch_b")[0], w_out=g("w_out")[0],
             ln1_g=g("ln1_g"), ln1_b=g("ln1_b"), ln2_g=g("ln2_g"), ln2_b=g("ln2_b"),
             w_peer_query=g("w_peer_query")[0], peer_sub_keys=g("peer_sub_keys")[0],
             peer_u=g("peer_u")[0], peer_v=g("peer_v")[0],
             cache1=g("cache_kv_a1")[0, 4 * c:4 * c + 4].reshape(4, 128, 1024),
             cache2=g("cache_kv_a2")[0, 4 * c:4 * c + 4].reshape(4, 512, 1024),
             cache3=g("cache_kv_a3")[0, 4 * c:4 * c + 4].reshape(4, 2048, 1024),
             state=g("state_gla")[0, 4 * c:4 * c + 4])
    m.update(consts)
    return m


_CACHE = {}


def kernel(**inputs):
    if "nc" not in _CACHE:
        _CACHE["nc"] = build_program()
        _CACHE["consts"] = make_consts()
    nc = _CACHE["nc"]
    consts = _CACHE["consts"]
    inp = {k: np.asarray(v) for k, v in inputs.items()}
    maps = [core_inputs(inp, c, consts) for c in range(NCORES)]
    res = run_bass_kernel_spmd(nc, maps, core_ids=list(range(NCORES)))
    R = res.results
    f = np.float32
    y_p = np.stack([R[c]["y_p"] for c in range(NCORES)]).astype(f)
    y_s = np.concatenate([R[c]["y_s"].reshape(4, 4, D) for c in range(NCORES)]).astype(f)
    outs = [y_p, y_s]
    for g in range(3):
        wb = 128 * A_DIL[g]
        outs.append(np.stack([R[c]["kv%d_p" % (g + 1)].reshape(wb, 2, 8, 64) for c in range(NCORES)])[None].astype(f))
    outs.append(np.stack([R[c]["st_p"] for c in range(NCORES)])[None].astype(f))
    for g in range(3):
        wb = 128 * A_DIL[g]
        outs.append(np.concatenate([R[c]["kv%d_s" % (g + 1)].reshape(4, wb, 2, 8, 64) for c in range(NCORES)])[None].astype(f))
    outs.append(np.concatenate([R[c]["st_s"] for c in range(NCORES)])[None].astype(f))
    return tuple(outs)
```

```python
import math
from contextlib import ExitStack
import numpy as np
import concourse.bass as bass
import concourse.mybir as mybir
from concourse.bass_utils import run_bass_kernel_spmd

F32 = mybir.dt.float32
BF16 = mybir.dt.bfloat16
I32 = mybir.dt.int32
AF = mybir.ActivationFunctionType
ALU = mybir.AluOpType
AX = mybir.AxisListType

NCORES = 8
T = 4096
NS = 16
D = 1024
NEG = -30000.0


class Buf:
    __slots__ = ("name", "t", "base_w", "part_w", "readers", "dsem", "dcount")

    def __init__(self, name, t):
        self.name = name
        self.t = t
        self.base_w = {}
        self.part_w = {}
        self.readers = {}
        self.dsem = None
        self.dcount = 0

    def __getitem__(self, idx):
        return self.t[idx]


class Eng:
    def __init__(self, name, sem):
        self.name = name
        self.sem = sem
        self.count = 0
        self.waited = {}
        self.ops = []


class Prog:
    def __init__(self, nc, stack):
        self.nc = nc
        self.stack = stack
        self.engs = {}
        for n in ("sync", "scalar", "gpsimd", "vector", "tensor"):
            s = stack.enter_context(nc.semaphore("e_" + n))
            self.engs[n] = Eng(n, s)
        self.out_toks = []
        self.ninstr = 0
        self.nsem = 5
        self.root = stack
        self.dbufs = []

    def push(self):
        if not hasattr(self, "stk"):
            self.stk = []
        self.stk.append(self.stack)
        self.stack = ExitStack()
        self.stack.__enter__()

    def pop(self):
        self.barrier()
        self.stack.__exit__(None, None, None)
        self.stack = self.stk.pop()

    def barrier(self):
        fin = {}
        for e in self.engs.values():
            if e.count:
                fin[e.sem] = e.count
        for b in self.dbufs:
            fin[b.dsem] = b.dcount
        for e in self.engs.values():
            for s, v in fin.items():
                if e.waited.get(s, 0) < v:
                    e.waited[s] = v
                    e.ops.append(("w", s, v))

    def sb(self, name, shape, dt):
        t = self.stack.enter_context(self.nc.sbuf_tensor("s_" + name, list(shape), dt))
        return Buf(name, t)

    def ps(self, name, shape, dt=F32):
        t = self.root.enter_context(self.nc.psum_tensor("p_" + name, list(shape), dt))
        return Buf(name, t)

    def wrap(self, name, t):
        return Buf(name, t)

    def _need(self, eng, reads, writes, partial):
        need = {}
        for r in reads:
            for dd in (r.base_w, r.part_w):
                for s, v in dd.items():
                    if need.get(s, 0) < v:
                        need[s] = v
        for w in writes:
            dds = (w.base_w, w.readers) if partial else (w.base_w, w.part_w, w.readers)
            for dd in dds:
                for s, v in dd.items():
                    if need.get(s, 0) < v:
                        need[s] = v
        for s, v in need.items():
            if eng.waited.get(s, 0) < v:
                eng.waited[s] = v
                eng.ops.append(("w", s, v))

    def _commit(self, tok, reads, writes, partial):
        s, v = tok
        for r in reads:
            if r.readers.get(s, 0) < v:
                r.readers[s] = v
        for w in writes:
            if partial:
                if w.part_w.get(s, 0) < v:
                    w.part_w[s] = v
            else:
                w.base_w = {s: v}
                w.part_w = {}
                w.readers = {}

    def op(self, ename, fn, reads=(), writes=(), partial=False):
        eng = self.engs[ename]
        self._need(eng, reads, writes, partial)
        eng.count += 1
        eng.ops.append(("i", fn))
        tok = (eng.sem, eng.count)
        self._commit(tok, reads, writes, partial)
        self.ninstr += 1
        return tok

    def group(self, ename, fns, reads=(), writes=(), partial=False):
        eng = self.engs[ename]
        self._need(eng, reads, writes, partial)
        for f in fns[:-1]:
            eng.ops.append(("n", f))
        eng.count += 1
        eng.ops.append(("i", fns[-1]))
        tok = (eng.sem, eng.count)
        self._commit(tok, reads, writes, partial)
        self.ninstr += len(fns)
        return tok

    def dma(self, ename, fns, owner, reads=(), writes=(), is_output=False, partial=False):
        eng = self.engs[ename]
        if owner.dsem is None:
            owner.dsem = self.root.enter_context(self.nc.semaphore("d_" + owner.name))
            self.nsem += 1
            self.dbufs.append(owner)
        self._need(eng, reads, writes, partial)
        for f in fns:
            owner.dcount += 16
            eng.ops.append(("d", f, owner.dsem))
        tok = (owner.dsem, owner.dcount)
        self._commit(tok, reads, writes, partial)
        if is_output:
            self.out_toks.append(tok)
        self.ninstr += len(fns)
        return tok

    def finish(self):
        eng = self.engs["sync"]
        fin = {}
        for e in self.engs.values():
            if e.count:
                fin[e.sem] = e.count
        for (s, v) in self.out_toks:
            if fin.get(s, 0) < v:
                fin[s] = v
        for s, v in fin.items():
            if eng.waited.get(s, 0) < v and s is not eng.sem:
                eng.ops.append(("w", s, v))
        engs = self.engs

        def replay(e, ne):
            for o in e.ops:
                k = o[0]
                if k == "w":
                    ne.wait_ge(o[1], o[2])
                elif k == "i":
                    o[1](ne).then_inc(e.sem, 1)
                elif k == "n":
                    o[1](ne)
                else:
                    o[1](ne).then_inc(o[2], 16)

        with self.nc.Block() as block:
            @block.sync
            def _(x):
                replay(engs["sync"], x)

            @block.scalar
            def _(x):
                replay(engs["scalar"], x)

            @block.gpsimd
            def _(x):
                replay(engs["gpsimd"], x)

            @block.vector
            def _(x):
                replay(engs["vector"], x)

            @block.tensor
            def _(x):
                replay(engs["tensor"], x)


A_DIL = (1, 4, 16)
COL_QA, COL_KA, COL_VA = 0, 1536, 3072
COL_QB, COL_KB, COL_VB = 4608, 5120, 5632
COL_AB, COL_RB, COL_GA, COL_GB = 6656, 6672, 7696, 8720


def _rel_bucket(dist):
    exact = 16
    d = np.maximum(dist, 1).astype(np.float32)
    large = exact + (np.log(d / np.float32(exact)) / np.float32(math.log(2048 / exact)) * np.float32(32 - exact)).astype(np.int32)
    large = np.minimum(large, 31)
    return np.where(dist < exact, dist, large)


def make_consts():
    c = {}
    c["ident"] = np.eye(128, dtype=np.float32)
    c["flip"] = np.ascontiguousarray(np.eye(128, dtype=np.float32)[::-1])
    oh = np.zeros((3, 33, 384), np.float32)
    for g, dil in enumerate(A_DIL):
        for m in range(384):
            rel = m - 127
            if 0 <= rel <= 128:
                oh[g, int(_rel_bucket(np.int32(rel * dil))), m] = 1.0
            else:
                oh[g, 32, m] = 1.0
    c["oh_bias"] = oh.transpose(1, 0, 2).copy()
    j = np.arange(128)[:, None]
    i = np.arange(128)[None, :]
    same = (j // 64) == (i // 64)
    c["tri_c"] = np.where(same & (j <= i), -1.0 / 16.0, 0.0).astype(np.float32)
    c["tri_rev"] = np.where(same & (j > i), -1.0 / 16.0, 0.0).astype(np.float32)
    i64 = np.arange(64)[None, :]
    c["mask_t"] = ((j % 64) <= i64).astype(np.float32)
    c["iota_c"] = np.tile(np.arange(256, dtype=np.int32)[None, :], (128, 1))
    c["e_sel"] = np.tile(np.eye(16, dtype=np.float32)[None], (128, 1, 1))
    c["sel16"] = np.tile(np.eye(16, dtype=np.float32)[:, :, None], (1, 1, 128))
    ohs = np.zeros((32, 3, 128), np.float32)
    for g, dil in enumerate(A_DIL):
        for p in range(128):
            ohs[int(_rel_bucket(np.int32((128 - p) * dil))), g, p] = 1.0
    c["ohs"] = ohs
    j16 = np.arange(16)[:, None]
    i16 = np.arange(16)[None, :]
    same16 = (j16 // 4) == (i16 // 4)
    c["tri_c16"] = np.where(same16 & (j16 <= i16), -1.0 / 16.0, 0.0).astype(np.float32)
    c["tri_rev16"] = np.where(same16 & (j16 > i16), -1.0 / 16.0, 0.0).astype(np.float32)
    c["mask16"] = (same16 & (j16 <= i16)).astype(np.float32)
    c["colmask"] = np.tile(((np.arange(16)[None, :] // 4) == np.arange(4)[:, None]).astype(np.float32)[None], (128, 1, 1))
    c["rowmask"] = ((np.arange(16)[:, None] // 4) == np.arange(4)[None, :]).astype(np.float32)
    return c


def build_program(debug=False, STOP=0, SAMPLE=True, NTILES_D=0, TILES_D=None):
    nc = bass.Bass("TRN2", target_bir_lowering=False)

    def din(name, shape, dt=F32):
        return nc.dram_tensor(name, list(shape), dt, kind="ExternalInput")

    def dout(name, shape, dt=F32):
        return nc.dram_tensor(name, list(shape), dt, kind="ExternalOutput")

    x_p = din("x_p", [T, D])
    x_s = din("x_s", [NS, D])
    rel_bias = din("rel_bias", [32, 24])
    w_in = din("w_in", [D, 9744])
    c_ident = din("ident", [128, 128])
    c_flip = din("flip", [128, 128])
    c_oh = din("oh_bias", [33, 3, 384])
    c_tric = din("tri_c", [128, 128])
    c_trirev = din("tri_rev", [128, 128])
    c_maskt = din("mask_t", [128, 64])
    w_g2 = din("w_gla_gate2", [16, 512])
    b_g = din("b_gla_gate", [1, 512])
    g_norm = din("g_gla_norm", [1, 1024])
    st_p = dout("st_p", [4, 128, 256])
    w_ba = din("w_branch_a", [512, 1024])
    w_bb = din("w_branch_b", [1024, 1024])
    w_o = din("w_out", [1024, 1024])
    ln1_g = din("ln1_g", [1, 1024])
    ln1_b = din("ln1_b", [1, 1024])
    ln2_g = din("ln2_g", [1, 1024])
    ln2_b = din("ln2_b", [1, 1024])
    w_pq = din("w_peer_query", [1024, 2048])
    sub_keys = din("peer_sub_keys", [8, 2, 128, 128])
    peer_u = din("peer_u", [16384, 1024])
    peer_v = din("peer_v", [16384, 1024])
    c_iota = din("iota_c", [128, 256], I32)
    c_esel = din("e_sel", [128, 16, 16])
    c_sel16 = din("sel16", [16, 16, 128])
    c_ohs = din("ohs", [32, 3, 128])
    c_tric16 = din("tri_c16", [16, 16])
    c_trirev16 = din("tri_rev16", [16, 16])
    c_mask16 = din("mask16", [16, 16])
    c_colmask = din("colmask", [128, 4, 16])
    c_rowmask = din("rowmask", [16, 4])
    WBS = (128, 512, 2048)
    cache = [din("cache%d" % (g + 1), [4, WBS[g], 1024]) for g in range(3)]
    state_in = din("state", [4, 4, 128, 256])
    kv_s = [dout("kv%d_s" % (g + 1), [4, WBS[g], 1024]) for g in range(3)]
    st_s = dout("st_s", [4, 4, 128, 256])
    ext_d = nc.dram_tensor("ext_d", [4, 132, 1024], F32, kind="Internal")
    y_p = dout("y_p", [T, D])
    y_s = dout("y_s", [NS, D])
    if debug:
        X1_d = dout("X1", [T + NS, D])
    else:
        X1_d = nc.dram_tensor("X1", [T + NS, D], F32, kind="Internal")
    UV_d = nc.dram_tensor("UV_d", [16384, 2048], BF16, kind="Internal")
    U_s = nc.dram_tensor("U_s", [NS, 520], F32, kind="Internal")
    OB_s = nc.dram_tensor("OB_s", [NS, 1024], F32, kind="Internal")
    if debug:
        OB_d = dout("OB", [T, 1024])
    else:
        OB_d = nc.dram_tensor("OB", [T, 1024], F32, kind="Internal")

    kv_p = [dout("kv%d_p" % (g + 1), [128 * A_DIL[g], 1024]) for g in range(3)]
    if debug:
        U_d = [dout("U%d" % g, [T, 520]) for g in range(3)]
    else:
        U_d = [nc.dram_tensor("U%d" % g, [T, 520], F32, kind="Internal") for g in range(3)]
    vec_d = nc.dram_tensor("vec_d", [24, 384], F32, kind="Internal")

    with ExitStack() as st:
        P = Prog(nc, st)
        ident = P.sb("ident", [128, 128], F32)
        identb = P.sb("identb", [128, 128], BF16)
        flip = P.sb("flip", [128, 128], F32)
        wst = [P.sb("wst%d" % i, [128, 8, 256], F32) for i in range(2)]
        P.push()
        xT = P.sb("xT", [128, 8, T + NS], BF16)
        TBL = P.wrap("TBL", None)
        P.push()
        rext = P.sb("rext", [33, 24], F32)
        oh = P.sb("oh", [33, 3, 384], F32)
        PS = [P.ps("b%d" % i, [128, 512], F32) for i in range(8)]
        psi = [0]

        ps_reserved = set()

        def psn():
            while True:
                b = PS[psi[0] % 8]
                psi[0] += 1
                if b.name not in ps_reserved:
                    return b

        ev = [0]

        def evac(out_buf, out_ap, in_buf, in_ap, scale=None, partial=False, eng=None):
            if eng is None:
                eng = "vector" if ev[0] % 2 == 0 else "scalar"
                ev[0] += 1
            if eng == "vector":
                if scale is None:
                    P.op("vector", lambda e: e.tensor_copy(out=out_ap, in_=in_ap), reads=[in_buf], writes=[out_buf], partial=partial)
                else:
                    P.op("vector", lambda e: e.tensor_single_scalar(out=out_ap, in_=in_ap, scalar=scale, op=ALU.mult), reads=[in_buf], writes=[out_buf], partial=partial)
            else:
                if scale is None:
                    P.op("scalar", lambda e: e.copy(out=out_ap, in_=in_ap), reads=[in_buf], writes=[out_buf], partial=partial)
                else:
                    P.op("scalar", lambda e: e.mul(out=out_ap, in_=in_ap, mul=scale), reads=[in_buf], writes=[out_buf], partial=partial)

        def mmgroup(out_buf, specs, reads):
            fns = []
            for out_ap, items in specs:
                n = len(items)
                for i, (l, r) in enumerate(items):
                    fns.append(lambda e, o=out_ap, l=l, r=r, a=(i == 0), b=(i == n - 1): e.matmul(o, lhsT=l, rhs=r, start=a, stop=b))
            P.group("tensor", fns, reads=reads, writes=[out_buf])

        P.dma("sync", [lambda e: e.dma_start(out=ident[:], in_=c_ident.ap())], ident, writes=[ident])
        P.dma("sync", [lambda e: e.dma_start(out=flip[:], in_=c_flip.ap())], flip, writes=[flip])
        P.dma("sync", [lambda e: e.dma_start(out=oh[:], in_=c_oh.ap())], oh, writes=[oh])
        P.op("vector", lambda e: e.memset(rext[:], NEG), writes=[rext])
        P.dma("sync", [lambda e: e.dma_start(out=rext[0:32, :], in_=rel_bias.ap())], rext, writes=[rext])
        P.op("vector", lambda e: e.tensor_copy(out=identb[:], in_=ident[:]), reads=[ident], writes=[identb])

        vecs = P.sb("vecs", [8, 3, 384], F32)
        VEC = P.wrap("vec_d", vec_d)
        for g in range(3):
            pb = psn()
            mmgroup(pb, [(pb[0:8, 0:384], [(rext[:, g * 8:(g + 1) * 8], oh[:, g, :])])], reads=[rext, oh])
            evac(vecs, vecs[:, g, :], pb, pb[0:8, 0:384], partial=True, eng="vector")
        P.dma("gpsimd", [lambda e: e.dma_start(out=vec_d.ap().rearrange("(g h) m -> h g m", g=3), in_=vecs[:])], vecs, reads=[vecs], writes=[VEC])

        xin = [P.sb("xin%d" % i, [128, D], F32) for i in range(2)]
        xbf = [P.sb("xbf%d" % i, [128, D], BF16) for i in range(2)]
        NTILE = T // 128
        for n in range(NTILE + 1):
            xi = xin[n % 2]
            xb = xbf[n % 2]
            rows = 128 if n < NTILE else NS
            src = x_p.ap()[n * 128:(n + 1) * 128, :] if n < NTILE else x_s.ap()
            P.dma("sync", [lambda e, xi=xi, src=src, rows=rows: e.dma_start(out=xi[0:rows, :], in_=src)], xi, writes=[xi])
            P.op("gpsimd", lambda e, xi=xi, xb=xb, rows=rows: e.tensor_copy(out=xb[0:rows, :], in_=xi[0:rows, :]), reads=[xi], writes=[xb])
            pb = psn()
            pbv = pb[:].bitcast(BF16)
            fns = []
            for k in range(8):
                fns.append(lambda e, k=k, xb=xb, pbv=pbv, rows=rows: e.transpose(out=pbv[:, k * 128:k * 128 + rows], in_=xb[0:rows, k * 128:(k + 1) * 128], identity=identb[0:rows, 0:rows]))
            P.group("tensor", fns, reads=[xb, identb], writes=[pb])
            evac(xT, xT[:, :, n * 128:n * 128 + rows], pb, pbv.rearrange("p (k t) -> p k t", k=8)[:, :, 0:rows], partial=True)

        if STOP == 1:
            P.finish()
            return nc
        P.pop()
        wi = [0, 0]
        w_in_v = w_in.ap().rearrange("(k p) n -> p k n", p=128)

        wbf_ring = [None]
        UB_all = P.wrap("UB_all", None)
        X1ALL = P.wrap("X1ALL", None)

        def alloc_wbf(tag):
            wbf_ring[0] = [P.sb("wbf%s%d" % (tag, i), [128, 8, 256], BF16) for i in range(4)]

        def get_w(c0, ncol=256):
            s = wst[wi[0] % 2]
            wi[0] += 1
            b = wbf_ring[0][wi[1] % 4]
            wi[1] += 1
            P.dma("sync", [lambda e: e.dma_start(out=s[:, :, 0:ncol], in_=w_in_v[:, :, c0:c0 + ncol])], s, writes=[s])
            ce = ("gpsimd", "vector", "scalar")[wi[1] % 3]
            if ce == "scalar":
                P.op("scalar", lambda e: e.copy(out=b[:, :, 0:ncol], in_=s[:, :, 0:ncol]), reads=[s], writes=[b])
            else:
                P.op(ce, lambda e: e.tensor_copy(out=b[:, :, 0:ncol], in_=s[:, :, 0:ncol]), reads=[s], writes=[b])
            return b


        def phase_b():
            P.push()
            alloc_wbf("B")
            tric = P.sb("tric", [128, 128], F32)
            trirev = P.sb("trirev", [128, 128], F32)
            maskt = P.sb("maskt", [128, 64], F32)
            w2e = P.sb("w2e", [17, 512], F32)
            abT = P.sb("abT", [17, T], F32)
            Gb = P.sb("Gb", [64, 1024], F32)
            P.dma("sync", [lambda e: e.dma_start(out=tric[:], in_=c_tric.ap())], tric, writes=[tric])
            P.dma("sync", [lambda e: e.dma_start(out=trirev[:], in_=c_trirev.ap())], trirev, writes=[trirev])
            P.dma("sync", [lambda e: e.dma_start(out=maskt[:], in_=c_maskt.ap())], maskt, writes=[maskt])
            P.dma("sync", [lambda e: e.dma_start(out=w2e[0:16, :], in_=w_g2.ap()), lambda e: e.dma_start(out=w2e[16:17, :], in_=b_g.ap())], w2e, writes=[w2e])
            P.dma("sync", [lambda e: e.dma_start(out=Gb[:], in_=g_norm.ap().partition_broadcast(64))], Gb, writes=[Gb])
            P.op("vector", lambda e: e.memset(abT[:], 1.0), writes=[abT])
            wab = get_w(COL_AB, 16)
            for tc in range(T // 512):
                pb = psn()
                mmgroup(pb, [(pb[0:16, :], [(wab[:, k, 0:16], xT[:, k, tc * 512:(tc + 1) * 512]) for k in range(8)])], reads=[wab, xT])
                evac(abT, abT[0:16, tc * 512:(tc + 1) * 512], pb, pb[0:16, :], partial=(tc > 0))
            qeT = P.sb("qeT", [128, T], BF16)
            keT = P.sb("keT", [128, T], BF16)
            kd = P.sb("kd", [128, T // 128, 128], BF16)
            vv = P.sb("vv", [128, T // 128, 256], BF16)
            dec = P.sb("dec", [128, T // 64], F32)
            t1 = [P.sb("t1_%d" % i, [128, 4, 128], F32) for i in range(2)]
            la = [P.sb("la%d" % i, [128, 4, 128], F32) for i in range(2)]
            eb = [P.sb("eb%d" % i, [128, 512], F32) for i in range(2)]
            enb = [P.sb("enb%d" % i, [128, 512], F32) for i in range(2)]
            erev = [P.sb("erev%d" % i, [128, 4, 128], F32) for i in range(2)]
            Sst = P.sb("Sst", [128, 256], F32)
            Sbf = P.sb("Sbf", [128, 256], BF16)
            attm = [P.sb("attm%d" % i, [128, 64], BF16) for i in range(2)]
            sr = [P.sb("sr%d" % i, [64, 256], F32) for i in range(2)]
            osq = P.sb("osq", [64, 256], F32)
            ss = [P.sb("ss%d" % i, [64, 2], F32) for i in range(2)]
            ot = [P.sb("ot%d" % i, [64, 256], F32) for i in range(2)]
            oo = [P.sb("oo%d" % i, [64, 256], F32) for i in range(2)]
            OB = UB_all
            STP = P.wrap("st_p", st_p)
            SC = 128.0 ** -0.5
            for h in range(4):
                wq = get_w(COL_QB + h * 128, 128)
                wk = get_w(COL_KB + h * 128, 128)
                wv = get_w(COL_VB + h * 256, 256)
                wr = get_w(COL_RB + h * 256, 256)
                for tc in range(T // 512):
                    i2 = tc % 2
                    tsl = slice(tc * 512, (tc + 1) * 512)
                    pz = psn()
                    mmgroup(pz, [(pz[:, i * 128:(i + 1) * 128], [(abT[:, tc * 512 + i * 128:tc * 512 + (i + 1) * 128], w2e[:, h * 128:(h + 1) * 128])]) for i in range(4)], reads=[abT, w2e])
                    P.op("scalar", lambda e, i2=i2, pz=pz: e.activation(out=t1[i2][:].rearrange("p a d -> p (a d)"), in_=pz[:, :], func=AF.Exp, scale=-1.0), reads=[pz], writes=[t1[i2]])
                    P.op("scalar", lambda e, i2=i2: e.activation(out=la[i2][:].rearrange("p a d -> p (a d)"), in_=t1[i2][:].rearrange("p a d -> p (a d)"), func=AF.Ln, bias=1.0, scale=1.0), reads=[t1[i2]], writes=[la[i2]])
                    pbt = psn()
                    mmgroup(pbt, [(pbt[:, i * 128:(i + 1) * 128], [(la[i2][:, i, :], tric[:])]) for i in range(4)], reads=[la[i2], tric])
                    P.op("scalar", lambda e, i2=i2, pbt=pbt: e.activation(out=eb[i2][:], in_=pbt[:, :], func=AF.Exp), reads=[pbt], writes=[eb[i2]])
                    P.op("scalar", lambda e, i2=i2, pbt=pbt: e.activation(out=enb[i2][:], in_=pbt[:, :], func=AF.Exp, scale=-1.0), reads=[pbt], writes=[enb[i2]])
                    P.op("vector", lambda e, i2=i2, tc=tc: e.tensor_copy(out=dec[:, tc * 8:(tc + 1) * 8], in_=eb[i2][:, 63:512:64]), reads=[eb[i2]], writes=[dec], partial=True)
                    prv = psn()
                    mmgroup(prv, [(prv[:, i * 128:(i + 1) * 128], [(trirev[:], la[i2][:, i, :])]) for i in range(4)], reads=[la[i2], trirev])
                    P.op("scalar", lambda e, i2=i2, prv=prv: e.activation(out=erev[i2][:].rearrange("p a d -> p (a d)"), in_=prv[:, :], func=AF.Exp), reads=[prv], writes=[erev[i2]])
                    pq = psn()
                    mmgroup(pq, [(pq[:, :], [(wq[:, k, 0:128], xT[:, k, tsl]) for k in range(8)])], reads=[wq, xT])
                    P.op("vector", lambda e, i2=i2, pq=pq, tsl=tsl: e.scalar_tensor_tensor(out=qeT[:, tsl], in0=pq[:, :], scalar=SC, in1=eb[i2][:], op0=ALU.mult, op1=ALU.mult), reads=[pq, eb[i2]], writes=[qeT], partial=True)
                    pk = psn()
                    mmgroup(pk, [(pk[:, :], [(wk[:, k, 0:128], xT[:, k, tsl]) for k in range(8)])], reads=[wk, xT])
                    P.op("vector", lambda e, i2=i2, pk=pk, tsl=tsl: e.tensor_tensor(out=keT[:, tsl], in0=pk[:, :], in1=enb[i2][:], op=ALU.mult), reads=[pk, enb[i2]], writes=[keT], partial=True)
                    pkt = psn()
                    mmgroup(pkt, [(pkt[:, i * 128:(i + 1) * 128], [(xT[:, k, tc * 512 + i * 128:tc * 512 + (i + 1) * 128], wk[:, k, 0:128]) for k in range(8)]) for i in range(4)], reads=[wk, xT])
                    P.op("vector", lambda e, i2=i2, pkt=pkt, tc=tc: e.tensor_tensor(out=kd[:, tc * 4:(tc + 1) * 4, :], in0=pkt[:, :].rearrange("p (a d) -> p a d", a=4), in1=erev[i2][:], op=ALU.mult), reads=[pkt, erev[i2]], writes=[kd], partial=True)
                    for i in range(2):
                        pv = psn()
                        mmgroup(pv, [(pv[:, a * 256:(a + 1) * 256], [(xT[:, k, tc * 512 + (i * 2 + a) * 128:tc * 512 + (i * 2 + a + 1) * 128], wv[:, k, 0:256]) for k in range(8)]) for a in range(2)], reads=[wv, xT])
                        evac(vv, vv[:, tc * 4 + i * 2:tc * 4 + i * 2 + 2, :], pv, pv[:, :].rearrange("p (a d) -> p a d", a=2), partial=True)
                P.op("vector", lambda e: e.memset(Sst[:], 0.0), writes=[Sst])
                P.op("vector", lambda e: e.memset(Sbf[:], 0.0), writes=[Sbf])
                for c in range(T // 64):
                    n, half = c // 2, c % 2
                    pr = slice(half * 64, half * 64 + 64)
                    csl = slice(c * 64, c * 64 + 64)
                    i2 = c % 2
                    prb = psn()
                    mmgroup(prb, [(prb[0:64, 0:256], [(xT[:, k, csl], wr[:, k, 0:256]) for k in range(8)])], reads=[wr, xT])
                    P.op("scalar", lambda e, i2=i2, prb=prb: e.activation(out=sr[i2][:], in_=prb[0:64, 0:256], func=AF.Silu), reads=[prb], writes=[sr[i2]])
                    pa = psn()
                    mmgroup(pa, [(pa[:, 0:64], [(keT[:, n * 128:(n + 1) * 128], qeT[:, csl])])], reads=[keT, qeT])
                    P.op("vector", lambda e, i2=i2, pa=pa, pr=pr: e.tensor_tensor(out=attm[i2][pr, :], in0=pa[pr, 0:64], in1=maskt[pr, :], op=ALU.mult), reads=[pa, maskt], writes=[attm[i2]])
                    po = psn()
                    mmgroup(po, [(po[0:64, 0:256], [(attm[i2][pr, :], vv[pr, n, :]), (qeT[:, csl], Sbf[:])])], reads=[attm[i2], vv, qeT, Sbf])
                    pS = psn()
                    mmgroup(pS, [(pS[:, 0:256], [(kd[pr, n, :], vv[pr, n, :])])], reads=[kd, vv])
                    P.op("vector", lambda e, c=c, pS=pS: e.scalar_tensor_tensor(out=Sst[:], in0=Sst[:], scalar=dec[:, c:c + 1], in1=pS[:, 0:256], op0=ALU.mult, op1=ALU.add), reads=[Sst, dec, pS], writes=[Sst])
                    P.op("scalar", lambda e: e.copy(out=Sbf[:], in_=Sst[:]), reads=[Sst], writes=[Sbf])
                    P.op("gpsimd", lambda e, i2=i2: e.memset(ss[i2][:], 0.0), writes=[ss[i2]])
                    P.op("scalar", lambda e, i2=i2, po=po: e.activation(out=osq[:], in_=po[0:64, 0:256], func=AF.Square, accum_out=ss[i2][:, 0:1]), reads=[po], writes=[osq, ss[i2]])
                    P.op("vector", lambda e, i2=i2: e.tensor_scalar(out=ss[i2][:, 1:2], in0=ss[i2][:, 0:1], scalar1=1.0 / 256.0, scalar2=1e-5, op0=ALU.mult, op1=ALU.add), reads=[ss[i2]], writes=[ss[i2]])
                    P.op("scalar", lambda e, i2=i2: e.sqrt(out=ss[i2][:, 1:2], in_=ss[i2][:, 1:2]), reads=[ss[i2]], writes=[ss[i2]])
                    P.op("vector", lambda e, i2=i2: e.reciprocal(out=ss[i2][:, 1:2], in_=ss[i2][:, 1:2]), reads=[ss[i2]], writes=[ss[i2]])
                    P.op("vector", lambda e, i2=i2, po=po, h=h: e.scalar_tensor_tensor(out=ot[i2][:], in0=po[0:64, 0:256], scalar=ss[i2][:, 1:2], in1=Gb[:, h * 256:(h + 1) * 256], op0=ALU.mult, op1=ALU.mult), reads=[po, ss[i2], Gb], writes=[ot[i2]])
                    P.op("gpsimd", lambda e, i2=i2: e.tensor_tensor(out=oo[i2][:], in0=ot[i2][:], in1=sr[i2][:], op=ALU.mult), reads=[ot[i2], sr[i2]], writes=[oo[i2]])
                    P.dma("sync", [lambda e, i2=i2, c=c, h=h: e.dma_start(out=OB_d.ap()[c * 64:(c + 1) * 64, h * 256:(h + 1) * 256], in_=oo[i2][:])], oo[i2], reads=[oo[i2]], writes=[OB], is_output=debug, partial=True)
                P.dma("sync", [lambda e, h=h: e.dma_start(out=st_p.ap()[h], in_=Sst[:])], Sst, reads=[Sst], writes=[STP], is_output=True, partial=True)
            P.pop()


        ALPHA = 2.0 ** 0.25

        def load_wres(dst, src_view, K, ncols):
            for c0 in range(0, ncols, 256):
                st_ = wst[wi[0] % 2]
                wi[0] += 1
                P.dma("sync", [lambda e, st_=st_, c0=c0: e.dma_start(out=st_[:, 0:K, :], in_=src_view[:, :, c0:c0 + 256])], st_, writes=[st_])
                ce = ("gpsimd", "vector", "scalar")[(c0 // 256) % 3]
                if ce == "scalar":
                    P.op("scalar", lambda e, st_=st_, c0=c0: e.copy(out=dst[:, :, c0:c0 + 256], in_=st_[:, 0:K, :]), reads=[st_], writes=[dst], partial=True)
                else:
                    P.op(ce, lambda e, st_=st_, c0=c0: e.tensor_copy(out=dst[:, :, c0:c0 + 256], in_=st_[:, 0:K, :]), reads=[st_], writes=[dst], partial=True)

        def layernorm(rows, y, junk, st4, Gt, Bt, out):
            r = slice(0, rows)
            P.op("vector", lambda e: e.tensor_reduce(out=st4[r, 0:1], in_=y[r, :], axis=AX.X, op=ALU.add), reads=[y], writes=[st4])
            P.op("vector", lambda e: e.tensor_single_scalar(out=st4[r, 1:2], in_=st4[r, 0:1], scalar=-1.0 / 1024.0, op=ALU.mult), reads=[st4], writes=[st4])
            P.op("vector", lambda e: e.tensor_scalar(out=y[r, :], in0=y[r, :], scalar1=st4[r, 1:2], scalar2=None, op0=ALU.add), reads=[y, st4], writes=[y])
            P.op("gpsimd", lambda e: e.memset(st4[r, 2:3], 0.0), reads=[], writes=[st4], partial=True)
            P.op("vector", lambda e: e.scalar_tensor_tensor(out=junk[r, :], in0=y[r, :], scalar=1.0, in1=y[r, :], op0=ALU.mult, op1=ALU.mult, accum_out=st4[r, 2:3]), reads=[y, st4], writes=[junk, st4])
            P.op("vector", lambda e: e.tensor_scalar(out=st4[r, 3:4], in0=st4[r, 2:3], scalar1=1.0 / 1024.0, scalar2=1e-5, op0=ALU.mult, op1=ALU.add), reads=[st4], writes=[st4])
            P.op("scalar", lambda e: e.sqrt(out=st4[r, 3:4], in_=st4[r, 3:4]), reads=[st4], writes=[st4])
            P.op("vector", lambda e: e.reciprocal(out=st4[r, 3:4], in_=st4[r, 3:4]), reads=[st4], writes=[st4])
            P.op("vector", lambda e: e.scalar_tensor_tensor(out=out[r, :], in0=y[r, :], scalar=st4[r, 3:4], in1=Gt[r, :], op0=ALU.mult, op1=ALU.mult), reads=[y, st4, Gt], writes=[out])
            P.op("gpsimd", lambda e: e.tensor_tensor(out=out[r, :], in0=out[r, :], in1=Bt[r, :], op=ALU.add), reads=[out, Bt], writes=[out])

        def transpose_to(dstT, src_bf, rows, nk):
            pb = psn()
            pbv = pb[:].bitcast(BF16)
            fns = []
            for k in range(nk):
                fns.append(lambda e, k=k: e.transpose(out=pbv[:, k * 128:k * 128 + rows], in_=src_bf[0:rows, k * 128:(k + 1) * 128], identity=identb[0:rows, 0:rows]))
            P.group("tensor", fns, reads=[src_bf, identb], writes=[pb])
            evac(dstT, dstT[:, 0:nk, 0:rows], pb, pbv[:, 0:nk * 128].rearrange("p (k t) -> p k t", k=nk)[:, :, 0:rows])

        def phase_c():
            P.push()
            Wa = P.sb("Wa", [128, 4, 1024], BF16)
            Wb = P.sb("Wb", [128, 8, 1024], BF16)
            Wo = P.sb("Wo", [128, 8, 1024], BF16)
            Wga = P.sb("Wga", [128, 8, 1024], BF16)
            Wgb = P.sb("Wgb", [128, 8, 1024], BF16)
            load_wres(Wa, w_ba.ap().rearrange("(k p) n -> p k n", p=128), 4, 1024)
            load_wres(Wb, w_bb.ap().rearrange("(k p) n -> p k n", p=128), 8, 1024)
            load_wres(Wo, w_o.ap().rearrange("(k p) n -> p k n", p=128), 8, 1024)
            load_wres(Wga, w_in_v[:, :, COL_GA:COL_GA + 1024], 8, 1024)
            load_wres(Wgb, w_in_v[:, :, COL_GB:COL_GB + 1024], 8, 1024)
            G1 = P.sb("G1", [128, 1024], F32)
            B1 = P.sb("B1", [128, 1024], F32)
            P.dma("sync", [lambda e: e.dma_start(out=G1[:], in_=ln1_g.ap().partition_broadcast(128))], G1, writes=[G1])
            P.dma("sync", [lambda e: e.dma_start(out=B1[:], in_=ln1_b.ap().partition_broadcast(128))], B1, writes=[B1])
            Ut = [P.sb("Ut%d" % i, [128, 8, 65], F32) for i in range(3)]
            OBt = P.sb("OBt", [128, 1024], F32)
            xt = P.sb("xt", [128, 1024], F32)
            rden = P.sb("rden", [128, 8], F32)
            oab = P.sb("oab", [128, 512], BF16)
            obb = P.sb("obb", [128, 1024], BF16)
            oaT = P.sb("oaT", [128, 4, 128], BF16)
            obT = P.sb("obT", [128, 8, 128], BF16)
            sg = P.sb("sg", [128, 1024], F32)
            mixed = P.sb("mixed", [128, 1024], F32)
            mixb = P.sb("mixb", [128, 1024], BF16)
            mixT = P.sb("mixT", [128, 8, 128], BF16)
            st4 = P.sb("st4c", [128, 4], F32)
            X1 = X1ALL
            Ubufs = [P.wrap("Ux%d" % g, U_d[g]) for g in range(3)]
            def tile_c(n):
                samp = (n == T // 128)
                rows = NS if samp else 128
                r = slice(0, rows)
                tcol = slice(n * 128, n * 128 + rows)
                if samp:
                    usrc = [U_s.ap()]
                    obsrc = OB_s.ap()
                    xsrc = x_s.ap()
                else:
                    usrc = [U_d[g].ap()[n * 128:(n + 1) * 128, :] for g in range(3)]
                    obsrc = OB_d.ap()[n * 128:(n + 1) * 128, :]
                    xsrc = x_p.ap()[n * 128:(n + 1) * 128, :]
                for i, us in enumerate(usrc):
                    P.dma("sync", [lambda e, i=i, us=us: e.dma_start(out=Ut[i][r].rearrange("p h d -> p (h d)"), in_=us)], Ut[i], reads=[UB_all], writes=[Ut[i]])
                P.dma("sync", [lambda e, obsrc=obsrc: e.dma_start(out=OBt[r, :], in_=obsrc)], OBt, reads=[UB_all], writes=[OBt])
                P.dma("sync", [lambda e, xsrc=xsrc: e.dma_start(out=xt[r, :], in_=xsrc)], xt, writes=[xt])
                for i in range(1, len(usrc)):
                    P.op("vector", lambda e, i=i: e.tensor_tensor(out=Ut[0][r], in0=Ut[0][r], in1=Ut[i][r], op=ALU.add), reads=[Ut[0], Ut[i]], writes=[Ut[0]])
                P.op("vector", lambda e: e.reciprocal(out=rden[r, :], in_=Ut[0][r, :, 64]), reads=[Ut[0]], writes=[rden])
                P.op("vector", lambda e: e.tensor_tensor(out=oab[r, :].rearrange("p (h d) -> p h d", h=8), in0=Ut[0][r, :, 0:64], in1=rden[r, :].unsqueeze(2).to_broadcast([rows, 8, 64]), op=ALU.mult), reads=[Ut[0], rden], writes=[oab])
                P.op("gpsimd", lambda e: e.tensor_copy(out=obb[r, :], in_=OBt[r, :]), reads=[OBt], writes=[obb])
                transpose_to(oaT, oab, rows, 4)
                transpose_to(obT, obb, rows, 8)
                for nh in range(2):
                    csl = slice(nh * 512, (nh + 1) * 512)
                    pg = psn()
                    mmgroup(pg, [(pg[r, :], [(xT[:, k, tcol], Wga[:, k, csl]) for k in range(8)])], reads=[xT, Wga])
                    P.op("scalar", lambda e, pg=pg, csl=csl: e.activation(out=sg[r, csl], in_=pg[r, :], func=AF.Sigmoid), reads=[pg], writes=[sg], partial=True)
                    pa = psn()
                    mmgroup(pa, [(pa[r, :], [(oaT[:, k, 0:rows], Wa[:, k, csl]) for k in range(4)])], reads=[oaT, Wa])
                    P.op("vector", lambda e, pa=pa, csl=csl: e.tensor_tensor(out=mixed[r, csl], in0=pa[r, :], in1=sg[r, csl], op=ALU.mult), reads=[pa, sg], writes=[mixed], partial=True)
                for nh in range(2):
                    csl = slice(nh * 512, (nh + 1) * 512)
                    pg = psn()
                    mmgroup(pg, [(pg[r, :], [(xT[:, k, tcol], Wgb[:, k, csl]) for k in range(8)])], reads=[xT, Wgb])
                    P.op("scalar", lambda e, pg=pg, csl=csl: e.activation(out=sg[r, csl], in_=pg[r, :], func=AF.Sigmoid), reads=[pg, mixed], writes=[sg], partial=True)
                    pb2 = psn()
                    mmgroup(pb2, [(pb2[r, :], [(obT[:, k, 0:rows], Wb[:, k, csl]) for k in range(8)])], reads=[obT, Wb])
                    P.op("vector", lambda e, pb2=pb2, csl=csl: e.tensor_tensor(out=sg[r, csl], in0=pb2[r, :], in1=sg[r, csl], op=ALU.mult), reads=[pb2, sg], writes=[sg], partial=True)
                P.op("gpsimd", lambda e: e.tensor_tensor(out=mixed[r, :], in0=mixed[r, :], in1=sg[r, :], op=ALU.add), reads=[mixed, sg], writes=[mixed])
                P.op("scalar", lambda e: e.copy(out=mixb[r, :], in_=mixed[r, :]), reads=[mixed], writes=[mixb])
                transpose_to(mixT, mixb, rows, 8)
                for nh in range(2):
                    csl = slice(nh * 512, (nh + 1) * 512)
                    py = psn()
                    mmgroup(py, [(py[r, :], [(mixT[:, k, 0:rows], Wo[:, k, csl]) for k in range(8)])], reads=[mixT, Wo])
                    P.op("vector", lambda e, py=py, csl=csl: e.scalar_tensor_tensor(out=xt[r, csl], in0=xt[r, csl], scalar=ALPHA, in1=py[r, :], op0=ALU.mult, op1=ALU.add), reads=[xt, py], writes=[xt], partial=(nh > 0))
                layernorm(rows, xt, sg, st4, G1, B1, mixed)
                P.dma("gpsimd", [lambda e, n=n, rows=rows: e.dma_start(out=X1_d.ap()[n * 128:n * 128 + rows, :], in_=mixed[0:rows, :])], mixed, reads=[mixed], writes=[X1], is_output=debug, partial=True)

            for n in range(T // 128 + (1 if SAMPLE else 0)):
                tile_c(n)
            P.pop()


        def phase_d():
            P.push()
            wpq = P.sb("wpq", [128, 8, 2048], BF16)
            load_wres(wpq, w_pq.ap().rearrange("(k p) n -> p k n", p=128), 8, 2048)
            KTt = P.sb("KTt", [128, 16, 128], F32)
            P.push()
            kraw = P.sb("kraw", [128, 16, 128], F32)
            P.dma("sync", [lambda e: e.dma_start(out=kraw[:], in_=sub_keys.ap().rearrange("h j k d -> k (h j) d"))], kraw, writes=[kraw])
            for q4 in range(4):
                pb = psn()
                fns = []
                for i in range(4):
                    fns.append(lambda e, i=i, q4=q4, pb=pb: e.transpose(out=pb[:, i * 128:(i + 1) * 128], in_=kraw[:, q4 * 4 + i, :], identity=ident[:]))
                P.group("tensor", fns, reads=[kraw, ident], writes=[pb])
                evac(KTt, KTt[:, q4 * 4:(q4 + 1) * 4, :], pb, pb[:, :].rearrange("p (a k) -> p a k", a=4), partial=True)
            P.pop()
            G2 = P.sb("G2", [128, 1024], F32)
            B2 = P.sb("B2", [128, 1024], F32)
            iot = P.sb("iot", [128, 256], I32)
            iotf = P.sb("iotf", [128, 16], F32)
            P.dma("sync", [lambda e: e.dma_start(out=G2[:], in_=ln2_g.ap().partition_broadcast(128))], G2, writes=[G2])
            P.dma("sync", [lambda e: e.dma_start(out=B2[:], in_=ln2_b.ap().partition_broadcast(128))], B2, writes=[B2])
            P.dma("sync", [lambda e: e.dma_start(out=iot[:], in_=c_iota.ap())], iot, writes=[iot])
            P.op("vector", lambda e: e.tensor_copy(out=iotf[:], in_=iot[:, 0:16]), reads=[iot], writes=[iotf])
            x1t_ = [P.sb("x1t%d" % i, [128, 1024], F32) for i in range(2)]
            x1b_ = [P.sb("x1b%d" % i, [128, 1024], BF16) for i in range(2)]
            exi_ = [P.sb("exi%d" % i, [128, 128], I32) for i in range(2)]
            gt_ = [P.sb("gt%d" % i, [128, 8, 16], F32) for i in range(2)]
            x1T = P.sb("x1T", [128, 8, 128], BF16)
            qs = P.sb("qs", [128, 2048], F32)
            junk2 = P.sb("junk2", [128, 2048], F32)
            st8 = P.sb("st8", [128, 4, 8], F32)
            qnT = P.sb("qnT", [128, 16, 128], F32)
            ssb = P.sb("ssb", [128, 2048], F32)
            wk = P.sb("wk", [128, 256], F32)
            v12 = P.sb("v12", [128, 16, 16], F32)
            e12 = P.sb("e12", [128, 16, 16], I32)
            e12f = P.sb("e12f", [128, 16, 16], F32)
            cand = P.sb("cand", [128, 2048], F32)
            sv = P.sb("sv", [128, 8, 16], F32)
            si = P.sb("si", [128, 3, 128], I32)
            sif = P.sb("sif", [128, 2, 128], F32)
            es = P.sb("es", [128, 3, 128], F32)
            dots = P.sb("dots", [128, 128], F32)
            wgt = P.sb("wgt", [128, 128], F32)
            wgb = P.sb("wgb", [128, 128], BF16)
            wd = [P.sb("wd%d" % i, [128, 8, 128], BF16) for i in range(2)]
            junkb = P.sb("junkb", [128, 1024], BF16)
            yout = P.sb("yout", [128, 1024], F32)
            st4 = P.sb("st4d", [128, 4], F32)
            NG = 16
            gb_ = [P.sb("gb%d" % i, [128, 2048], BF16) for i in range(NG)]
            gi = [0]
            YP = P.wrap("y_p", y_p)
            YS = P.wrap("y_s", y_s)

            def front_d(n):
                samp = (n == T // 128)
                rows = NS if samp else 128
                r = slice(0, rows)
                x1t, x1b, exi, gt = x1t_[n % 2], x1b_[n % 2], exi_[n % 2], gt_[n % 2]
                P.dma("sync", [lambda e: e.dma_start(out=x1t[r, :], in_=X1_d.ap()[n * 128:n * 128 + rows, :])], x1t, reads=[X1ALL], writes=[x1t])
                P.op("scalar", lambda e: e.copy(out=x1b[r, :], in_=x1t[r, :]), reads=[x1t], writes=[x1b])
                yield
                transpose_to(x1T, x1b, rows, 8)
                yield
                for nb in range(4):
                    pq = psn()
                    mmgroup(pq, [(pq[r, :], [(x1T[:, k, 0:rows], wpq[:, k, nb * 512:(nb + 1) * 512]) for k in range(8)])], reads=[x1T, wpq])
                    evac(qs, qs[r, nb * 512:(nb + 1) * 512], pq, pq[r, :], partial=(nb > 0), eng="scalar")
                    yield
                qv = qs[r, :].rearrange("p (h d) -> p h d", h=8)
                jv = junk2[r, :].rearrange("p (h d) -> p h d", h=8)
                P.op("vector", lambda e: e.tensor_reduce(out=st8[r, 0, :], in_=qv, axis=AX.X, op=ALU.add), reads=[qs], writes=[st8])
                P.op("vector", lambda e: e.tensor_single_scalar(out=st8[r, 1, :], in_=st8[r, 0, :], scalar=-1.0 / 256.0, op=ALU.mult), reads=[st8], writes=[st8])
                yield
                P.op("vector", lambda e: e.tensor_tensor(out=qv, in0=qv, in1=st8[r, 1, :].unsqueeze(2).to_broadcast([rows, 8, 256]), op=ALU.add), reads=[qs, st8], writes=[qs])
                yield
                P.op("scalar", lambda e: e.activation(out=junk2[r, :], in_=qs[r, :], func=AF.Square), reads=[qs], writes=[junk2])
                P.op("vector", lambda e: e.tensor_reduce(out=st8[r, 2, :], in_=jv, axis=AX.X, op=ALU.add), reads=[junk2], writes=[st8])
                yield
                P.op("vector", lambda e: e.tensor_scalar(out=st8[r, 3, :], in0=st8[r, 2, :], scalar1=1.0 / 256.0, scalar2=1e-5, op0=ALU.mult, op1=ALU.add), reads=[st8], writes=[st8])
                P.op("scalar", lambda e: e.sqrt(out=st8[r, 3, :], in_=st8[r, 3, :]), reads=[st8], writes=[st8])
                P.op("vector", lambda e: e.reciprocal(out=st8[r, 3, :], in_=st8[r, 3, :]), reads=[st8], writes=[st8])
                yield
                P.op("vector", lambda e: e.tensor_tensor(out=qv, in0=qv, in1=st8[r, 3, :].unsqueeze(2).to_broadcast([rows, 8, 256]), op=ALU.mult), reads=[qs, st8], writes=[qs])
                yield
                for q4 in range(4):
                    pb = psn()
                    fns = []
                    for i in range(4):
                        fns.append(lambda e, i=i, q4=q4, pb=pb: e.transpose(out=pb[:, i * 128:i * 128 + rows], in_=qs[r, (q4 * 4 + i) * 128:(q4 * 4 + i + 1) * 128], identity=ident[0:rows, 0:rows]))
                    P.group("tensor", fns, reads=[qs, ident], writes=[pb])
                    evac(qnT, qnT[:, q4 * 4:(q4 + 1) * 4, 0:rows], pb, pb[:, :].rearrange("p (a t) -> p a t", a=4)[:, :, 0:rows], partial=(q4 > 0), eng="scalar")
                    yield
                ssi = ssb[:].bitcast(I32)
                for q4 in range(4):
                    pb = psn()
                    mmgroup(pb, [(pb[r, i * 128:(i + 1) * 128], [(qnT[:, q4 * 4 + i, 0:rows], KTt[:, q4 * 4 + i, :])]) for i in range(4)], reads=[qnT, KTt])
                    P.op("vector", lambda e, pb=pb, q4=q4: e.tensor_single_scalar(out=ssi[r, q4 * 512:(q4 + 1) * 512], in_=pb[r, :].bitcast(I32), scalar=-128, op=ALU.bitwise_and), reads=[pb], writes=[ssb], partial=(q4 > 0))
                    yield
                P.op("vector", lambda e: e.tensor_tensor(out=ssi[r, :].rearrange("p (a k) -> p a k", a=16), in0=ssi[r, :].rearrange("p (a k) -> p a k", a=16), in1=iot[r, 0:128].unsqueeze(1).to_broadcast([rows, 16, 128]), op=ALU.bitwise_or), reads=[ssb, iot], writes=[ssb])
                yield
                for hj in range(16):
                    sl = slice(hj * 128, (hj + 1) * 128)
                    P.op("vector", lambda e, hj=hj, sl=sl: e.max(out=v12[r, hj, 0:8], in_=ssb[r, sl]), reads=[ssb], writes=[v12], partial=(hj > 0))
                    P.op("vector", lambda e, hj=hj, sl=sl: e.match_replace(out=wk[r, 0:128], in_to_replace=v12[r, hj, 0:8], in_values=ssb[r, sl], imm_value=-1e30), reads=[ssb, v12], writes=[wk])
                    P.op("vector", lambda e, hj=hj: e.max(out=v12[r, hj, 8:16], in_=wk[r, 0:128]), reads=[wk], writes=[v12], partial=True)
                    if hj % 2 == 1:
                        yield
                P.op("vector", lambda e: e.tensor_single_scalar(out=e12[r], in_=v12[r].bitcast(I32), scalar=127, op=ALU.bitwise_and), reads=[v12], writes=[e12])
                P.op("vector", lambda e: e.tensor_copy(out=e12f[r], in_=e12[r]), reads=[e12], writes=[e12f])
                yield
                v12v = v12[r].rearrange("p (h j) k -> p h j k", j=2)
                e12v = e12f[r].rearrange("p (h j) k -> p h j k", j=2)
                cv = cand[r, :].rearrange("p (h a b) -> p h a b", h=8, a=16)
                ci = cand[:].bitcast(I32)
                P.op("vector", lambda e: e.tensor_tensor(out=cv, in0=v12v[:, :, 0, :].unsqueeze(3).to_broadcast([rows, 8, 16, 16]), in1=v12v[:, :, 1, :].unsqueeze(2).to_broadcast([rows, 8, 16, 16]), op=ALU.add), reads=[v12], writes=[cand])
                yield
                P.op("vector", lambda e: e.tensor_single_scalar(out=ci[r, :], in_=ci[r, :], scalar=-256, op=ALU.bitwise_and), reads=[cand], writes=[cand])
                yield
                P.op("vector", lambda e: e.tensor_tensor(out=ci[r, :].rearrange("p (h c) -> p h c", h=8), in0=ci[r, :].rearrange("p (h c) -> p h c", h=8), in1=iot[r, :].unsqueeze(1).to_broadcast([rows, 8, 256]), op=ALU.bitwise_or), reads=[cand, iot], writes=[cand])
                yield
                for h in range(8):
                    sl = slice(h * 256, (h + 1) * 256)
                    P.op("vector", lambda e, h=h, sl=sl: e.max(out=sv[r, h, 0:8], in_=cand[r, sl]), reads=[cand], writes=[sv], partial=(h > 0))
                    P.op("vector", lambda e, h=h, sl=sl: e.match_replace(out=wk[r, :], in_to_replace=sv[r, h, 0:8], in_values=cand[r, sl], imm_value=-1e30), reads=[cand, sv], writes=[wk])
                    P.op("vector", lambda e, h=h: e.max(out=sv[r, h, 8:16], in_=wk[r, :]), reads=[wk], writes=[sv], partial=True)
                    yield
                svf = sv[r].rearrange("p h k -> p (h k)")
                P.op("vector", lambda e: e.tensor_single_scalar(out=si[r, 0, :], in_=svf.bitcast(I32), scalar=255, op=ALU.bitwise_and), reads=[sv], writes=[si])
                P.op("vector", lambda e: e.tensor_single_scalar(out=si[r, 1, :], in_=si[r, 0, :], scalar=4, op=ALU.logical_shift_right), reads=[si], writes=[si])
                P.op("vector", lambda e: e.tensor_single_scalar(out=si[r, 2, :], in_=si[r, 0, :], scalar=15, op=ALU.bitwise_and), reads=[si], writes=[si])
                P.op("vector", lambda e: e.tensor_copy(out=sif[r], in_=si[r, 1:3, :]), reads=[si], writes=[sif])
                yield
                ohv = junk2[r, :].rearrange("p (h k i) -> p h k i", h=8, k=16)
                iob = iotf[r, :].unsqueeze(1).unsqueeze(1).to_broadcast([rows, 8, 16, 16])
                for j in range(2):
                    idxv = sif[r, j, :].rearrange("p (h k) -> p h k", h=8).unsqueeze(3).to_broadcast([rows, 8, 16, 16])
                    P.op("vector", lambda e, idxv=idxv: e.tensor_tensor(out=ohv, in0=idxv, in1=iob, op=ALU.is_equal), reads=[sif, iotf], writes=[junk2])
                    yield
                    P.op("vector", lambda e, j=j: e.tensor_tensor(out=ohv, in0=ohv, in1=e12v[:, :, j, :].unsqueeze(2).to_broadcast([rows, 8, 16, 16]), op=ALU.mult), reads=[junk2, e12f], writes=[junk2])
                    yield
                    P.op("vector", lambda e, j=j: e.tensor_reduce(out=es[r, j, :].rearrange("p (h k) -> p h k", h=8), in_=ohv, axis=AX.X, op=ALU.add), reads=[junk2], writes=[es], partial=(j > 0))
                    yield
                P.op("vector", lambda e: e.scalar_tensor_tensor(out=es[r, 2, :], in0=es[r, 0, :], scalar=128.0, in1=es[r, 1, :], op0=ALU.mult, op1=ALU.add), reads=[es], writes=[es])
                P.op("vector", lambda e: e.tensor_copy(out=exi[r, :], in_=es[r, 2, :]), reads=[es], writes=[exi])
                yield
                P.op("vector", lambda e: e.tensor_reduce(out=st8[r, 0, :], in_=sv[r], axis=AX.X, op=ALU.max), reads=[sv], writes=[st8])
                P.op("vector", lambda e: e.tensor_tensor(out=gt[r], in0=sv[r], in1=st8[r, 0, :].unsqueeze(2).to_broadcast([rows, 8, 16]), op=ALU.subtract), reads=[sv, st8], writes=[gt])
                P.op("scalar", lambda e: e.activation(out=gt[r], in_=gt[r], func=AF.Exp), reads=[gt], writes=[gt])
                yield
                P.op("vector", lambda e: e.tensor_reduce(out=st8[r, 1, :], in_=gt[r], axis=AX.X, op=ALU.add), reads=[gt], writes=[st8])
                P.op("vector", lambda e: e.reciprocal(out=st8[r, 1, :], in_=st8[r, 1, :]), reads=[st8], writes=[st8])
                P.op("vector", lambda e: e.tensor_tensor(out=gt[r], in0=gt[r], in1=st8[r, 1, :].unsqueeze(2).to_broadcast([rows, 8, 16]), op=ALU.mult), reads=[gt, st8], writes=[gt])
                yield

            def back_d(n):
                samp = (n == T // 128)
                rows = NS if samp else 128
                r = slice(0, rows)
                x1t, x1b, exi, gt = x1t_[n % 2], x1b_[n % 2], exi_[n % 2], gt_[n % 2]
                gtf = gt[r].rearrange("p h k -> p (h k)")
                P.op("gpsimd", lambda e: e.memset(dots[r, :], 0.0), writes=[dots])
                pacc = [psn(), psn()]
                ps_reserved.update(p_.name for p_ in pacc)
                for g8 in range(16):
                    bufs = []
                    for j in range(8):
                        hk = g8 * 8 + j
                        g = gb_[gi[0] % NG]
                        gi[0] += 1
                        bufs.append(g)
                        P.dma("gpsimd", [lambda e, g=g, hk=hk: e.indirect_dma_start(out=g[r, :], out_offset=None, in_=UV_d.ap(), in_offset=bass.IndirectOffsetOnAxis(ap=exi[r, hk:hk + 1], axis=0))], g, reads=[exi, TBL], writes=[g])
                        P.op("vector", lambda e, g=g, hk=hk: e.scalar_tensor_tensor(out=junkb[r, :], in0=g[r, 0:1024], scalar=1.0, in1=x1b[r, :], op0=ALU.mult, op1=ALU.mult, accum_out=dots[r, hk:hk + 1]), reads=[g, x1b, dots], writes=[junkb, dots])
                        yield
                    hs = slice(g8 * 8, g8 * 8 + 8)
                    P.op("scalar", lambda e, hs=hs: e.activation(out=wgt[r, hs], in_=dots[r, hs], func=AF.Gelu), reads=[dots], writes=[wgt])
                    P.op("vector", lambda e, hs=hs: e.tensor_tensor(out=wgb[r, hs], in0=wgt[r, hs], in1=gtf[:, hs], op=ALU.mult), reads=[wgt, gt], writes=[wgb])
                    wdc = wd[g8 % 2]
                    P.op("vector", lambda e, wdc=wdc, hs=hs: e.tensor_tensor(out=wdc[r, :, 0:rows], in0=identb[r, 0:rows].unsqueeze(1).to_broadcast([rows, 8, rows]), in1=wgb[r, hs].unsqueeze(2).to_broadcast([rows, 8, rows]), op=ALU.mult), reads=[identb, wgb], writes=[wdc])
                    fns = []
                    for j in range(8):
                        hk = g8 * 8 + j
                        for hf in range(2):
                            fns.append(lambda e, hf=hf, g=bufs[j], wdc=wdc, j=j, hk=hk: e.matmul(pacc[hf][r, :], lhsT=wdc[r, j, 0:rows], rhs=g[r, 1024 + hf * 512:1024 + (hf + 1) * 512], start=(hk == 0), stop=(hk == 127)))
                    P.group("tensor", fns, reads=[wdc] + bufs, writes=pacc)
                for hf in range(2):
                    P.op("vector", lambda e, hf=hf: e.scalar_tensor_tensor(out=x1t[r, hf * 512:(hf + 1) * 512], in0=x1t[r, hf * 512:(hf + 1) * 512], scalar=ALPHA, in1=pacc[hf][r, :], op0=ALU.mult, op1=ALU.add), reads=[x1t, pacc[hf]], writes=[x1t], partial=(hf > 0))
                ps_reserved.clear()
                yield
                layernorm(rows, x1t, yout, st4, G2, B2, yout)
                if samp:
                    P.dma("sync", [lambda e: e.dma_start(out=y_s.ap(), in_=yout[r, :])], yout, reads=[yout], writes=[YS], is_output=True, partial=True)
                else:
                    P.dma("sync", [lambda e: e.dma_start(out=y_p.ap()[n * 128:(n + 1) * 128, :], in_=yout[r, :])], yout, reads=[yout], writes=[YP], is_output=True, partial=True)
                yield

            tl = list(TILES_D if TILES_D is not None else range(NTILES_D if NTILES_D else (T // 128 + (1 if SAMPLE else 0))))
            for _ in front_d(tl[0]):
                pass
            for i, n in enumerate(tl):
                bk = back_d(n)
                fr = front_d(tl[i + 1]) if i + 1 < len(tl) else None
                nstep = 0
                for _ in bk:
                    nstep += 1
                    if fr is not None and nstep <= 128 and nstep % 2 == 0:
                        try:
                            next(fr)
                        except StopIteration:
                            fr = None
                    if fr is not None and nstep == 128:
                        for _ in fr:
                            pass
                        fr = None
            P.pop()

        def phase_s():
            P.push()
            alloc_wbf("S")
            NPC = COL_GA
            psall = P.sb("psall", [16, NPC], F32)
            SC8 = 0.125
            for c0 in range(0, NPC, 256):
                ncol = min(256, NPC - c0)
                w = get_w(c0, ncol)
                pb = psn()
                mmgroup(pb, [(pb[0:16, 0:ncol], [(xT[:, k, T:T + NS], w[:, k, 0:ncol]) for k in range(8)])], reads=[w, xT])
                evac(psall, psall[:, c0:c0 + ncol], pb, pb[0:16, 0:ncol], partial=(c0 > 0))
            esel = P.sb("esel", [128, 16, 16], F32)
            sel16 = P.sb("sel16", [16, 16, 128], F32)
            ohs = P.sb("ohs", [32, 3, 128], F32)
            relb = P.sb("relb", [32, 24], F32)
            rel0 = P.sb("rel0", [16, 24], F32)
            biasP = P.sb("biasP", [128, 24], F32)
            for (dst, src) in ((esel, c_esel), (sel16, c_sel16), (ohs, c_ohs), (relb, rel_bias)):
                P.dma("sync", [lambda e, dst=dst, src=src: e.dma_start(out=dst[:], in_=src.ap())], dst, writes=[dst])
            P.dma("sync", [lambda e: e.dma_start(out=rel0[:], in_=rel_bias.ap()[0:1, :].partition_broadcast(16))], rel0, writes=[rel0])
            for g in range(3):
                pb = psn()
                mmgroup(pb, [(pb[:, 0:8], [(ohs[:, g, :], relb[:, g * 8:(g + 1) * 8])])], reads=[ohs, relb])
                evac(biasP, biasP[:, g * 8:(g + 1) * 8], pb, pb[:, 0:8], partial=(g > 0), eng="vector")
            KVS = [P.wrap("kvs%d" % g, kv_s[g]) for g in range(3)]
            EXT = P.wrap("ext", ext_d)
            for g in range(3):
                wb_ = WBS[g]
                fns = []
                for b in range(4):
                    fns.append(lambda e, g=g, b=b, wb_=wb_: e.dma_start(out=kv_s[g].ap()[b, 0:wb_ - 4, :], in_=cache[g].ap()[b, 4:wb_, :]))
                P.dma("gpsimd", fns, KVS[g], writes=[KVS[g]], is_output=True, partial=True)
                for part, col in ((0, COL_KA), (1, COL_VA)):
                    dst = bass.AP(kv_s[g], (wb_ - 4) * 1024 + part * 512, [[wb_ * 1024, 4], [1024, 4], [1, 512]])
                    P.dma("gpsimd", [lambda e, dst=dst, col=col, g=g: e.dma_start(out=dst, in_=psall[:, col + g * 512:col + (g + 1) * 512])], KVS[g], reads=[psall], writes=[KVS[g]], is_output=True, partial=True)
            P.dma("gpsimd", [lambda e, b=b: e.dma_start(out=ext_d.ap()[b, 0:128, :], in_=cache[0].ap()[b, :, :]) for b in range(4)], EXT, writes=[EXT], partial=True)
            for part, col in ((0, COL_KA), (1, COL_VA)):
                dst = bass.AP(ext_d, 128 * 1024 + part * 512, [[132 * 1024, 4], [1024, 4], [1, 512]])
                P.dma("gpsimd", [lambda e, dst=dst, col=col: e.dma_start(out=dst, in_=psall[:, col:col + 512])], EXT, reads=[psall], writes=[EXT], partial=True)
            kvg = [P.sb("kvg%d" % i, [128, 1024], F32) for i in range(2)]
            qbc = [P.sb("qbc%d" % i, [128, 512], F32) for i in range(2)]
            jk = P.sb("jks", [128, 512], F32)
            sc = [P.sb("scs%d" % i, [128, 8], F32) for i in range(2)]
            pvs = [P.sb("pvs%d" % i, [128, 8, 65], F32) for i in range(2)]
            pacc = [psn(), psn()]
            ps_reserved.update(p_.name for p_ in pacc)
            cnt = 0
            for g in range(3):
                dil = A_DIL[g]
                for bs in range(16):
                    b, s_ = bs // 4, bs % 4
                    i2 = cnt % 2
                    src_t = ext_d if g == 0 else cache[g]
                    rows_t = 132 if g == 0 else WBS[g]
                    src = bass.AP(src_t, (b * rows_t + s_) * 1024, [[dil * 1024, 128], [1, 1024]])
                    P.dma("sync", [lambda e, src=src, i2=i2: e.dma_start(out=kvg[i2][:], in_=src)], kvg[i2], reads=([EXT] if g == 0 else []), writes=[kvg[i2]])
                    pq = psn()
                    mmgroup(pq, [(pq[:, :], [(sel16[:, bs, :], psall[:, COL_QA + g * 512:COL_QA + (g + 1) * 512])])], reads=[sel16, psall])
                    P.op("scalar", lambda e, pq=pq, i2=i2: e.mul(out=qbc[i2][:], in_=pq[:, :], mul=SC8), reads=[pq], writes=[qbc[i2]])
                    P.op("vector", lambda e, i2=i2: e.tensor_tensor(out=jk[:], in0=kvg[i2][:, 0:512], in1=qbc[i2][:], op=ALU.mult), reads=[kvg[i2], qbc[i2]], writes=[jk])
                    P.op("vector", lambda e, i2=i2: e.tensor_reduce(out=sc[i2][:], in_=jk[:].rearrange("p (h d) -> p h d", h=8), axis=AX.X, op=ALU.add), reads=[jk], writes=[sc[i2]])
                    P.op("vector", lambda e, i2=i2, g=g: e.tensor_tensor(out=sc[i2][:], in0=sc[i2][:], in1=biasP[:, g * 8:(g + 1) * 8], op=ALU.add), reads=[sc[i2], biasP], writes=[sc[i2]])
                    P.op("scalar", lambda e, i2=i2: e.activation(out=pvs[i2][:, :, 64], in_=sc[i2][:], func=AF.Exp), reads=[sc[i2]], writes=[pvs[i2]])
                    P.op("vector", lambda e, i2=i2: e.tensor_tensor(out=pvs[i2][:, :, 0:64], in0=kvg[i2][:, 512:1024].rearrange("p (h d) -> p h d", h=8), in1=pvs[i2][:, :, 64:65].to_broadcast([128, 8, 64]), op=ALU.mult), reads=[kvg[i2], pvs[i2]], writes=[pvs[i2]], partial=True)
                    first, last = (cnt == 0), (cnt == 47)
                    pvf = pvs[i2][:].rearrange("p h d -> p (h d)")
                    fns = []
                    for hf in range(2):
                        fns.append(lambda e, hf=hf, pvf=pvf, bs=bs, first=first, last=last: e.matmul(pacc[hf][0:16, 0:260], lhsT=esel[:, bs, :], rhs=pvf[:, hf * 260:(hf + 1) * 260], start=first, stop=last))
                    P.group("tensor", fns, reads=[esel, pvs[i2]], writes=pacc)
                    cnt += 1
            Us = P.sb("Us", [16, 8, 65], F32)
            jks = P.sb("jk16", [16, 512], F32)
            scs = P.sb("sc16", [16, 8], F32)
            pw = P.sb("pw16", [16, 8, 65], F32)
            for hf in range(2):
                evac(Us, Us[:].rearrange("p h d -> p (h d)")[:, hf * 260:(hf + 1) * 260], pacc[hf], pacc[hf][0:16, 0:260], partial=(hf > 0), eng="vector")
            ps_reserved.clear()
            for g in range(3):
                qsl = slice(COL_QA + g * 512, COL_QA + (g + 1) * 512)
                ksl = slice(COL_KA + g * 512, COL_KA + (g + 1) * 512)
                vsl = slice(COL_VA + g * 512, COL_VA + (g + 1) * 512)
                P.op("vector", lambda e, qsl=qsl, ksl=ksl: e.tensor_tensor(out=jks[:], in0=psall[:, qsl], in1=psall[:, ksl], op=ALU.mult), reads=[psall], writes=[jks])
                P.op("vector", lambda e: e.tensor_reduce(out=scs[:], in_=jks[:].rearrange("p (h d) -> p h d", h=8), axis=AX.X, op=ALU.add), reads=[jks], writes=[scs])
                P.op("vector", lambda e, g=g: e.scalar_tensor_tensor(out=scs[:], in0=scs[:], scalar=SC8, in1=rel0[:, g * 8:(g + 1) * 8], op0=ALU.mult, op1=ALU.add), reads=[scs, rel0], writes=[scs])
                P.op("scalar", lambda e: e.activation(out=pw[:, :, 64], in_=scs[:], func=AF.Exp), reads=[scs], writes=[pw])
                P.op("vector", lambda e, vsl=vsl: e.tensor_tensor(out=pw[:, :, 0:64], in0=psall[:, vsl].rearrange("p (h d) -> p h d", h=8), in1=pw[:, :, 64:65].to_broadcast([16, 8, 64]), op=ALU.mult), reads=[psall, pw], writes=[pw], partial=True)
                P.op("vector", lambda e: e.tensor_tensor(out=Us[:], in0=Us[:], in1=pw[:], op=ALU.add), reads=[Us, pw], writes=[Us])
            P.dma("gpsimd", [lambda e: e.dma_start(out=U_s.ap(), in_=Us[:].rearrange("p h d -> p (h d)"))], Us, reads=[Us], writes=[UB_all], partial=True)

            tric16 = P.sb("tric16", [16, 16], F32)
            trirev16 = P.sb("trirev16", [16, 16], F32)
            mask16 = P.sb("mask16", [16, 16], F32)
            colmask = P.sb("colmask", [128, 4, 16], F32)
            rowmask = P.sb("rowmask", [16, 4], F32)
            w2e = P.sb("w2es", [17, 512], F32)
            Gb = P.sb("Gbs", [16, 1024], F32)
            for (dst, src) in ((tric16, c_tric16), (trirev16, c_trirev16), (mask16, c_mask16), (colmask, c_colmask), (rowmask, c_rowmask)):
                P.dma("sync", [lambda e, dst=dst, src=src: e.dma_start(out=dst[:], in_=src.ap())], dst, writes=[dst])
            P.dma("sync", [lambda e: e.dma_start(out=w2e[0:16, :], in_=w_g2.ap()), lambda e: e.dma_start(out=w2e[16:17, :], in_=b_g.ap())], w2e, writes=[w2e])
            P.dma("sync", [lambda e: e.dma_start(out=Gb[:], in_=g_norm.ap().partition_broadcast(16))], Gb, writes=[Gb])
            abTs = P.sb("abTs", [17, 16], F32)
            P.op("vector", lambda e: e.memset(abTs[:], 1.0), writes=[abTs])
            wab = get_w(COL_AB, 16)
            pb = psn()
            mmgroup(pb, [(pb[0:16, 0:16], [(wab[:, k, 0:16], xT[:, k, T:T + NS]) for k in range(8)])], reads=[wab, xT])
            evac(abTs, abTs[0:16, :], pb, pb[0:16, 0:16], eng="vector")
            SC = 128.0 ** -0.5
            t1 = P.sb("t1s", [16, 128], F32)
            la = P.sb("las", [16, 128], F32)
            eb = P.sb("ebs", [128, 16], F32)
            enb = P.sb("enbs", [128, 16], F32)
            erev = P.sb("erevs", [16, 128], F32)
            qeT = P.sb("qeTs", [128, 16], BF16)
            qeTm = P.sb("qeTms", [128, 4, 16], BF16)
            keT = P.sb("keTs", [128, 16], BF16)
            kdf = P.sb("kdfs", [16, 128], F32)
            kdm = P.sb("kdms", [16, 4, 128], BF16)
            vb_ = P.sb("vbs", [16, 256], BF16)
            attm = P.sb("attms", [16, 16], BF16)
            S0 = [P.sb("S0_%d" % i, [128, 256], F32) for i in range(4)]
            S0b = [P.sb("S0b_%d" % i, [128, 256], BF16) for i in range(4)]
            Sn = [P.sb("Sn_%d" % i, [128, 256], F32) for i in range(2)]
            srs = P.sb("srs", [16, 256], F32)
            osq = P.sb("osqs", [16, 256], F32)
            ss = P.sb("sss", [16, 2], F32)
            ot = P.sb("ots", [16, 256], F32)
            oo = P.sb("oos", [16, 256], F32)
            STS = P.wrap("st_s", st_s)
            for h in range(4):
                wq = get_w(COL_QB + h * 128, 128)
                wk = get_w(COL_KB + h * 128, 128)
                for b in range(4):
                    P.dma("sync", [lambda e, b=b, h=h: e.dma_start(out=S0[b][:], in_=state_in.ap()[b, h])], S0[b], writes=[S0[b]])
                    P.op("gpsimd", lambda e, b=b: e.tensor_copy(out=S0b[b][:], in_=S0[b][:]), reads=[S0[b]], writes=[S0b[b]])
                pz = psn()
                mmgroup(pz, [(pz[0:16, 0:128], [(abTs[:, :], w2e[:, h * 128:(h + 1) * 128])])], reads=[abTs, w2e])
                P.op("scalar", lambda e, pz=pz: e.activation(out=t1[:], in_=pz[0:16, 0:128], func=AF.Exp, scale=-1.0), reads=[pz], writes=[t1])
                P.op("scalar", lambda e: e.activation(out=la[:], in_=t1[:], func=AF.Ln, bias=1.0, scale=1.0), reads=[t1], writes=[la])
                pbt = psn()
                mmgroup(pbt, [(pbt[:, 0:16], [(la[:, :], tric16[:])])], reads=[la, tric16])
                P.op("scalar", lambda e, pbt=pbt: e.activation(out=eb[:], in_=pbt[:, 0:16], func=AF.Exp), reads=[pbt], writes=[eb])
                P.op("scalar", lambda e, pbt=pbt: e.activation(out=enb[:], in_=pbt[:, 0:16], func=AF.Exp, scale=-1.0), reads=[pbt], writes=[enb])
                prv = psn()
                mmgroup(prv, [(prv[0:16, 0:128], [(trirev16[:], la[:, :])])], reads=[la, trirev16])
                P.op("scalar", lambda e, prv=prv: e.activation(out=erev[:], in_=prv[0:16, 0:128], func=AF.Exp), reads=[prv], writes=[erev])
                pq = psn()
                mmgroup(pq, [(pq[:, 0:16], [(wq[:, k, 0:128], xT[:, k, T:T + NS]) for k in range(8)])], reads=[wq, xT])
                P.op("vector", lambda e, pq=pq: e.scalar_tensor_tensor(out=qeT[:], in0=pq[:, 0:16], scalar=SC, in1=eb[:], op0=ALU.mult, op1=ALU.mult), reads=[pq, eb], writes=[qeT])
                P.op("vector", lambda e: e.tensor_tensor(out=qeTm[:], in0=colmask[:], in1=qeT[:].unsqueeze(1).to_broadcast([128, 4, 16]), op=ALU.mult), reads=[qeT, colmask], writes=[qeTm])
                pk = psn()
                mmgroup(pk, [(pk[:, 0:16], [(wk[:, k, 0:128], xT[:, k, T:T + NS]) for k in range(8)])], reads=[wk, xT])
                P.op("vector", lambda e, pk=pk: e.tensor_tensor(out=keT[:], in0=pk[:, 0:16], in1=enb[:], op=ALU.mult), reads=[pk, enb], writes=[keT])
                P.op("vector", lambda e, h=h: e.tensor_tensor(out=kdf[:], in0=psall[:, COL_KB + h * 128:COL_KB + (h + 1) * 128], in1=erev[:], op=ALU.mult), reads=[psall, erev], writes=[kdf])
                for b in range(4):
                    P.op("vector", lambda e, b=b: e.tensor_scalar(out=kdm[:, b, :], in0=kdf[:], scalar1=rowmask[:, b:b + 1], scalar2=None, op0=ALU.mult), reads=[kdf, rowmask], writes=[kdm], partial=(b > 0))
                P.op("vector", lambda e, h=h: e.tensor_copy(out=vb_[:], in_=psall[:, COL_VB + h * 256:COL_VB + (h + 1) * 256]), reads=[psall], writes=[vb_])
                pa = psn()
                mmgroup(pa, [(pa[0:16, 0:16], [(keT[:, :], qeT[:, :])])], reads=[keT, qeT])
                P.op("vector", lambda e, pa=pa: e.tensor_tensor(out=attm[:], in0=pa[0:16, 0:16], in1=mask16[:], op=ALU.mult), reads=[pa, mask16], writes=[attm])
                po = psn()
                mmgroup(po, [(po[0:16, 0:256], [(attm[:, :], vb_[:, :])] + [(qeTm[:, b, :], S0b[b][:]) for b in range(4)])], reads=[attm, vb_, qeTm] + S0b)
                for b in range(4):
                    pS = psn()
                    mmgroup(pS, [(pS[:, 0:256], [(kdm[:, b, :], vb_[:, :])])], reads=[kdm, vb_])
                    sn = Sn[b % 2]
                    P.op("vector", lambda e, b=b, pS=pS, sn=sn: e.scalar_tensor_tensor(out=sn[:], in0=S0[b][:], scalar=eb[:, 4 * b + 3:4 * b + 4], in1=pS[:, 0:256], op0=ALU.mult, op1=ALU.add), reads=[S0[b], eb, pS], writes=[sn])
                    P.dma("gpsimd", [lambda e, b=b, h=h, sn=sn: e.dma_start(out=st_s.ap()[b, h], in_=sn[:])], sn, reads=[sn], writes=[STS], is_output=True, partial=True)
                P.op("scalar", lambda e, h=h: e.activation(out=srs[:], in_=psall[:, COL_RB + h * 256:COL_RB + (h + 1) * 256], func=AF.Silu), reads=[psall], writes=[srs])
                P.op("gpsimd", lambda e: e.memset(ss[:], 0.0), writes=[ss])
                P.op("scalar", lambda e, po=po: e.activation(out=osq[:], in_=po[0:16, 0:256], func=AF.Square, accum_out=ss[:, 0:1]), reads=[po, ss], writes=[osq, ss])
                P.op("vector", lambda e: e.tensor_scalar(out=ss[:, 1:2], in0=ss[:, 0:1], scalar1=1.0 / 256.0, scalar2=1e-5, op0=ALU.mult, op1=ALU.add), reads=[ss], writes=[ss])
                P.op("scalar", lambda e: e.sqrt(out=ss[:, 1:2], in_=ss[:, 1:2]), reads=[ss], writes=[ss])
                P.op("vector", lambda e: e.reciprocal(out=ss[:, 1:2], in_=ss[:, 1:2]), reads=[ss], writes=[ss])
                P.op("vector", lambda e, po=po, h=h: e.scalar_tensor_tensor(out=ot[:], in0=po[0:16, 0:256], scalar=ss[:, 1:2], in1=Gb[:, h * 256:(h + 1) * 256], op0=ALU.mult, op1=ALU.mult), reads=[po, ss, Gb], writes=[ot])
                P.op("vector", lambda e: e.tensor_tensor(out=oo[:], in0=ot[:], in1=srs[:], op=ALU.mult), reads=[ot, srs], writes=[oo])
                P.dma("gpsimd", [lambda e, h=h: e.dma_start(out=OB_s.ap()[:, h * 256:(h + 1) * 256], in_=oo[:])], oo, reads=[oo], writes=[UB_all], partial=True)
            P.pop()

        P.push()
        alloc_wbf("A")
        tst = [P.sb("tst%d" % i, [128, 2048], F32) for i in range(2)]
        tbf = [P.sb("tbf%d" % i, [128, 2048], BF16) for i in range(2)]

        def prepass_gen():
            ti = 0
            for (src_t, c0_) in ((peer_u, 0), (peer_v, 1024)):
                sv_ = src_t.ap().rearrange("(i p j) d -> i p (j d)", p=128, j=2)
                dv_ = UV_d.ap()[:, c0_:c0_ + 1024].rearrange("(i p j) d -> i p j d", p=128, j=2)
                for i in range(64):
                    a, b = tst[ti % 2], tbf[ti % 2]
                    P.dma("sync", [lambda e, a=a, i=i, sv_=sv_: e.dma_start(out=a[:], in_=sv_[i])], a, writes=[a])
                    if ti % 2 == 0:
                        P.op("vector", lambda e, a=a, b=b: e.tensor_copy(out=b[:], in_=a[:]), reads=[a], writes=[b])
                    else:
                        P.op("scalar", lambda e, a=a, b=b: e.copy(out=b[:], in_=a[:]), reads=[a], writes=[b])
                    P.dma("gpsimd", [lambda e, b=b, i=i, dv_=dv_: e.dma_start(out=dv_[i], in_=b[:].rearrange("p (j d) -> p j d", j=2))], b, reads=[b], writes=[TBL], partial=True)
                    ti += 1
                    yield

        prepass = prepass_gen()

        def prepass_step():
            try:
                next(prepass)
            except StopIteration:
                pass
        QT = P.sb("QT", [128, 2, T], BF16)
        KT = P.sb("KT", [128, 2, T], BF16)
        Vb = P.sb("Vb", [128, 32, 4, 72], BF16)
        Hk = P.sb("Hk", [128, 8, 2, 128], F32)
        BT = P.sb("BT", [128, 8, 2, 128], F32)
        Sf = [P.sb("Sf%d" % i, [128, 2, 2, 128], F32) for i in range(2)]
        PT = [P.sb("PT%d" % i, [128, 2, 2, 128], BF16) for i in range(4)]
        Ost = [P.sb("Ost%d" % i, [128, 260], F32) for i in range(2)]
        KVst = [P.sb("KVst%d" % i, [128, 2, 256], F32) for i in range(2)]
        P.op("vector", lambda e: e.memset(Vb[:], 1.0), writes=[Vb])
        Ub = [UB_all for g in range(3)]
        KVo = [P.wrap("kvo%d" % g, kv_p[g]) for g in range(3)]
        cnt = [0]

        for g in range(3):
            dil = A_DIL[g]
            span = 128 * dil
            nspan = T // span
            hfn = []
            for h8 in range(8):
                hsrc = bass.AP(vec_d, (g * 8 + h8) * 384, [[1, 128], [128, 2], [1, 128]])
                hfn.append(lambda e, hsrc=hsrc, h8=h8: e.dma_start(out=Hk[:, h8, :, :], in_=hsrc))
            P.dma("sync", hfn, Hk, reads=[VEC], writes=[Hk])
            if STOP == 2:
                break
            Hkf = Hk[:].rearrange("p h a q -> p (h a q)")
            BTf = BT[:].rearrange("p h a q -> p (h a q)")
            for cc in range(4):
                pb = psn()
                mmgroup(pb, [(pb[:, :], [(flip[:], Hkf[:, cc * 512:(cc + 1) * 512])])], reads=[flip, Hk])
                evac(BT, BTf[:, cc * 512:(cc + 1) * 512], pb, pb[:, :], partial=(cc > 0))

            def tok_slice(blk):
                s, r = blk // dil, blk % dil
                base = s * span + r
                return slice(base, base + 127 * dil + 1, dil) if dil > 1 else slice(base, base + 128)

            for hh in range(2):
                wq = get_w(COL_QA + g * 512 + hh * 256)
                wk = get_w(COL_KA + g * 512 + hh * 256)
                wv = get_w(COL_VA + g * 512 + hh * 256)
                for (dst, w, sc) in ((QT, wq, 0.125), (KT, wk, None)):
                    for c in range(2):
                        for tc in range(T // 512):
                            pb = psn()
                            mmgroup(pb, [(pb[:, :], [(w[:, k, c * 128:(c + 1) * 128], xT[:, k, tc * 512:(tc + 1) * 512]) for k in range(8)])], reads=[w, xT])
                            evac(dst, dst[:, c, tc * 512:(tc + 1) * 512], pb, pb[:, :], scale=sc, partial=True)
                if STOP == 3:
                    break
                for blk in range(32):
                    ts_ = tok_slice(blk)
                    pb = psn()
                    mmgroup(pb, [(pb[:, 0:256], [(xT[:, k, ts_], wv[:, k, :]) for k in range(8)])], reads=[wv, xT])
                    evac(Vb, Vb[:, blk, :, 0:64], pb, pb[:, 0:256].rearrange("p (h d) -> p h d", h=4), partial=True)
                    s, r = blk // dil, blk % dil
                    if s == nspan - 1:
                        kst = KVst[cnt[0] % 2]
                        cnt[0] += 1
                        pk = psn()
                        mmgroup(pk, [(pk[:, 0:256], [(xT[:, k, ts_], wk[:, k, :]) for k in range(8)])], reads=[wk, xT])
                        evac(kst, kst[:, 0, :], pk, pk[:, 0:256])
                        evac(kst, kst[:, 1, :], pb, pb[:, 0:256], partial=True)
                        dst = bass.AP(kv_p[g], r * 1024 + hh * 256, [[dil * 1024, 128], [512, 2], [1, 256]])
                        P.dma("gpsimd", [lambda e, dst=dst, kst=kst: e.dma_start(out=dst, in_=kst[:])], kst, reads=[kst], writes=[KVo[g]], is_output=True, partial=True)
                if STOP == 4:
                    break
                for blk in range(32 if STOP < 5 else 2):
                    prepass_step()
                    s, r = blk // dil, blk % dil
                    tq = tok_slice(blk)
                    tp = tok_slice(blk - dil) if s > 0 else None
                    po = psn()
                    pts = []
                    na = 2 if s > 0 else 1
                    for j in range(2):
                        pb = psn()
                        pbv = pb[:].rearrange("p (c a q) -> p c a q", c=2, a=2)
                        pr = slice(j * 64, (j + 1) * 64)
                        specs = []
                        for c in range(2):
                            specs.append((pbv[:, c, 0, :], [(KT[pr, c, tq], QT[pr, c, tq])]))
                            if s > 0:
                                specs.append((pbv[:, c, 1, :], [(KT[pr, c, tp], QT[pr, c, tq])]))
                        mmgroup(pb, specs, reads=[KT, QT])
                        sf = Sf[cnt[0] % 2]
                        pt = PT[cnt[0] % 4]
                        cnt[0] += 1
                        hsl = slice(hh * 4 + j, hh * 4 + j + 3, 2)
                        P.op("vector", lambda e, sf=sf, pbv=pbv, na=na, hsl=hsl: e.tensor_tensor(out=sf[:, :, 0:na, :], in0=pbv[:, :, 0:na, :], in1=BT[:, hsl, 0:na, :], op=ALU.add), reads=[pb, BT], writes=[sf])
                        P.op("scalar", lambda e, sf=sf, pt=pt, na=na: e.activation(out=pt[:, :, 0:na, :], in_=sf[:, :, 0:na, :], func=AF.Exp), reads=[sf], writes=[pt])
                        pts.append(pt)
                    pov = po[:, 0:288].rearrange("p (h d) -> p h d", h=4)
                    specs = []
                    for hl in range(4):
                        c, j = hl // 2, hl % 2
                        pt = pts[j]
                        items = [(pt[:, c, 0, :], Vb[:, blk, hl, :])]
                        if s > 0:
                            items.append((pt[:, c, 1, :], Vb[:, blk - dil, hl, :]))
                        specs.append((pov[:, hl, :], items))
                    mmgroup(po, specs, reads=[pts[0], pts[1], Vb])
                    ost = Ost[cnt[0] % 2]
                    evac(ost, ost[:, :].rearrange("p (h d) -> p h d", h=4), po, pov[:, :, 0:65])
                    udst = bass.AP(U_d[g], (s * span + r) * 520 + hh * 260, [[dil * 520, 128], [1, 260]])
                    P.dma("gpsimd", [lambda e, udst=udst, ost=ost: e.dma_start(out=udst, in_=ost[:])], ost, reads=[ost], writes=[Ub[g]], is_output=debug, partial=True)

            if STOP >= 2:
                break
        for _ in prepass:
            pass
        P.pop()
        if STOP == 0:
            phase_b()
            if SAMPLE:
                phase_s()
            phase_c()
        P.pop()
        if STOP == 0:
            phase_d()
        P.finish()
        print("instructions:", P.ninstr, "sems:", P.nsem)
    return nc


def core_inputs(inp, c, consts=None):
    if consts is None:
        consts = make_consts()
    g = lambda k: np.asarray(inp[k])
    m = dict(x_p=g("x_prompt")[c], x_s=g("x_sample")[4 * c:4 * c + 4].reshape(NS, D),
             rel_bias=g("rel_bias"), w_in=g("w_in")[0], w_gla_gate2=g("w_gla_gate2")[0],
             b_gla_gate=g("b_gla_gate"), g_gla_norm=g("g_gla_norm"),
             w_branch_a=g("w_branch_a")[0], w_branch_b=g("w_branch_b")[0], w_out=g("w_out")[0],
             ln1_g=g("ln1_g"), ln1_b=g("ln1_b"), ln2_g=g("ln2_g"), ln2_b=g("ln2_b"),
             w_peer_query=g("w_peer_query")[0], peer_sub_keys=g("peer_sub_keys")[0],
             peer_u=g("peer_u")[0], peer_v=g("peer_v")[0],
             cache1=g("cache_kv_a1")[0, 4 * c:4 * c + 4].reshape(4, 128, 1024),
             cache2=g("cache_kv_a2")[0, 4 * c:4 * c + 4].reshape(4, 512, 1024),
             cache3=g("cache_kv_a3")[0, 4 * c:4 * c + 4].reshape(4, 2048, 1024),
             state=g("state_gla")[0, 4 * c:4 * c + 4])
    m.update(consts)
    return m


_CACHE = {}


def kernel(**inputs):
    if "nc" not in _CACHE:
        _CACHE["nc"] = build_program()
        _CACHE["consts"] = make_consts()
    nc = _CACHE["nc"]
    consts = _CACHE["consts"]
    inp = {k: np.asarray(v) for k, v in inputs.items()}
    maps = [core_inputs(inp, c, consts) for c in range(NCORES)]
    res = run_bass_kernel_spmd(nc, maps, core_ids=list(range(NCORES)))
    R = res.results
    f = np.float32
    y_p = np.stack([R[c]["y_p"] for c in range(NCORES)]).astype(f)
    y_s = np.concatenate([R[c]["y_s"].reshape(4, 4, D) for c in range(NCORES)]).astype(f)
    outs = [y_p, y_s]
    for g in range(3):
        wb = 128 * A_DIL[g]
        outs.append(np.stack([R[c]["kv%d_p" % (g + 1)].reshape(wb, 2, 8, 64) for c in range(NCORES)])[None].astype(f))
    outs.append(np.stack([R[c]["st_p"] for c in range(NCORES)])[None].astype(f))
    for g in range(3):
        wb = 128 * A_DIL[g]
        outs.append(np.concatenate([R[c]["kv%d_s" % (g + 1)].reshape(4, wb, 2, 8, 64) for c in range(NCORES)])[None].astype(f))
    outs.append(np.concatenate([R[c]["st_s"] for c in range(NCORES)])[None].astype(f))
    return tuple(outs)
```
